# Optimizing a Trainium2 kernel written in Bass

```python
import math
import jax, jax.numpy as jnp
from jax import lax
import numpy as np

D_MODEL = 2048
BATCH = 4
SEQ = 2048
DEPTH = 4

N_MIXERS = 3
HEAD_DIM = 128
N_HEADS = D_MODEL // HEAD_DIM
ATTN_WIDTH = N_HEADS * HEAD_DIM
BLOCK_Q = 128
D_FF = 4 * D_MODEL
MLA_Q_RANK = 3 * D_MODEL // 8
MLA_KV_RANK = D_MODEL // 4
MLA_NOPE = 128
MLA_ROPE = 64
MLA_V = 128
MLA_QK = MLA_NOPE + MLA_ROPE
ROPE_THETA = 10000.0
EPS = 1e-6
N_SB = (DEPTH + 2) // 3
N_FOX = (DEPTH + 1) // 3
N_MLA = DEPTH // 3

kernel_name = "interleaved_sb_fox_mla_trunk"


def rmsnorm(x, g):
    xf = x.astype(jnp.float32)
    y = xf * lax.rsqrt(jnp.mean(xf * xf, axis=-1, keepdims=True) + EPS)
    return (y * g.astype(jnp.float32)).astype(x.dtype)


def sweep_query_blocks(block_fn, seq_len):
    outs = [block_fn(qs, qs + BLOCK_Q) for qs in range(0, seq_len, BLOCK_Q)]
    return jnp.concatenate(outs, axis=2)


def split_heads_qkv(qkv, b, s):
    t = qkv.reshape(b, s, 3, N_HEADS, HEAD_DIM).transpose(2, 0, 3, 1, 4)
    return t[0], t[1], t[2]


def merge_heads(o):
    b, h, s, d = o.shape
    return o.transpose(0, 2, 1, 3).reshape(b, s, h * d)


def stick_breaking_block(q_blk, k_pre, v_pre, q_start):
    n_q, n_k = q_blk.shape[2], k_pre.shape[2]
    z = jnp.einsum('bhtd,bhsd->bhts', q_blk, k_pre).astype(jnp.float32) * (1.0 / math.sqrt(HEAD_DIM))
    strict = jnp.arange(n_k)[None, :] < (q_start + jnp.arange(n_q))[:, None]
    log_keep = jnp.where(strict, jax.nn.log_sigmoid(-z), 0.0)
    later = lax.cumsum(log_keep, axis=3, reverse=True) - log_keep
    a = jnp.where(strict, jnp.exp(jax.nn.log_sigmoid(z) + later), 0.0)
    return jnp.einsum('bhts,bhsd->bhtd', a.astype(v_pre.dtype), v_pre)


def causal_softmax_block(q_blk, k_pre, v_pre, q_start, scale, bias=None):
    n_q, n_k = q_blk.shape[2], k_pre.shape[2]
    logits = jnp.einsum('bhtd,bhsd->bhts', q_blk, k_pre).astype(jnp.float32) * scale
    if bias is not None:
        logits = logits + bias
    causal = jnp.arange(n_k)[None, :] <= (q_start + jnp.arange(n_q))[:, None]
    p = jax.nn.softmax(jnp.where(causal, logits, -jnp.inf), axis=-1)
    return jnp.einsum('bhts,bhsd->bhtd', p.astype(v_pre.dtype), v_pre)


def stick_breaking_mixer(h, w_in, w_out):
    b, s, _ = h.shape
    q, k, v = split_heads_qkv(h @ w_in, b, s)
    o = sweep_query_blocks(
        lambda qs, qe: stick_breaking_block(q[:, :, qs:qe], k[:, :, :qe], v[:, :, :qe], qs), s)
    return merge_heads(o) @ w_out


def forgetting_mixer(h, w_in, b_f, q_gain, k_gain, w_out):
    b, s, _ = h.shape
    proj = h @ w_in
    q, k, v = split_heads_qkv(proj[..., :3 * ATTN_WIDTH], b, s)
    log_f = jax.nn.log_sigmoid((proj[..., 3 * ATTN_WIDTH:] + b_f).astype(jnp.float32))
    cf = jnp.cumsum(log_f, axis=1).transpose(0, 2, 1)
    q = rmsnorm(q, q_gain)
    k = rmsnorm(k, k_gain)
    scale = 1.0 / math.sqrt(HEAD_DIM)

    def blk(qs, qe):
        bias = cf[:, :, qs:qe, None] - cf[:, :, None, :qe]
        return causal_softmax_block(q[:, :, qs:qe], k[:, :, :qe], v[:, :, :qe], qs, scale, bias)

    return merge_heads(sweep_query_blocks(blk, s)) @ w_out


def rope(x, positions):
    half = x.shape[-1] // 2
    inv_freq = ROPE_THETA ** (-jnp.arange(0, half, dtype=jnp.float32) * 2.0 / x.shape[-1])
    ang = positions.astype(jnp.float32)[..., None] * inv_freq
    if x.ndim == 4:
        ang = ang[:, :, None, :]
    cos, sin = jnp.cos(ang), jnp.sin(ang)
    xf = x.astype(jnp.float32)
    x1, x2 = xf[..., :half], xf[..., half:]
    return jnp.concatenate([x1 * cos - x2 * sin, x1 * sin + x2 * cos], axis=-1).astype(x.dtype)


def mla_mixer(h, positions, w_in, q_norm, kv_norm, w_uq, w_ukv, q_gain, k_gain, w_out):
    b, s, _ = h.shape
    down = h @ w_in
    c_q = rmsnorm(down[..., :MLA_Q_RANK], q_norm)
    c_kv = rmsnorm(down[..., MLA_Q_RANK:MLA_Q_RANK + MLA_KV_RANK], kv_norm)
    k_rope = rope(down[..., MLA_Q_RANK + MLA_KV_RANK:], positions)
    q = (c_q @ w_uq).reshape(b, s, N_HEADS, MLA_QK)
    q = jnp.concatenate([q[..., :MLA_NOPE], rope(q[..., MLA_NOPE:], positions)], axis=-1)
    kv = (c_kv @ w_ukv).reshape(b, s, N_HEADS, MLA_NOPE + MLA_V)
    k = jnp.concatenate(
        [kv[..., :MLA_NOPE], jnp.broadcast_to(k_rope[:, :, None, :], (b, s, N_HEADS, MLA_ROPE))], axis=-1)
    v = kv[..., MLA_NOPE:].transpose(0, 2, 1, 3)
    q = rmsnorm(q, q_gain).transpose(0, 2, 1, 3)
    k = rmsnorm(k, k_gain).transpose(0, 2, 1, 3)
    scale = 1.0 / math.sqrt(MLA_QK)
    o = sweep_query_blocks(
        lambda qs, qe: causal_softmax_block(q[:, :, qs:qe], k[:, :, :qe], v[:, :, :qe], qs, scale), s)
    return merge_heads(o) @ w_out


def sq_relu_mlp(h, w1, w2):
    a = jax.nn.relu(h @ w1)
    return (a * a) @ w2


def setup_inputs(seed: int = 0) -> dict:
    key = jax.random.key(seed)
    ks = jax.random.split(key, 24)
    f32 = jnp.float32

    def nrm(k, shape, fan_in):
        return jax.random.normal(k, shape, f32) * (fan_in ** -0.5)

    def gain(k, shape):
        return 1.0 + 0.02 * jax.random.normal(k, shape, f32)

    return {
        "x": jax.random.normal(ks[0], (BATCH, SEQ, D_MODEL), f32),
        "positions": jnp.broadcast_to(jnp.arange(SEQ, dtype=jnp.int32)[None, :], (BATCH, SEQ)),
        "mix_norm": gain(ks[1], (DEPTH, D_MODEL)),
        "mlp_norm": gain(ks[2], (DEPTH, D_MODEL)),
        "sb_w_in": nrm(ks[3], (N_SB, D_MODEL, 3 * ATTN_WIDTH), D_MODEL),
        "sb_w_out": nrm(ks[4], (N_SB, ATTN_WIDTH, D_MODEL), ATTN_WIDTH),
        "fox_w_in": nrm(ks[5], (N_FOX, D_MODEL, 3 * ATTN_WIDTH + N_HEADS), D_MODEL),
        "fox_b_f": jax.random.uniform(ks[6], (N_FOX, N_HEADS), f32, 1.0, 5.0),
        "fox_q_gain": gain(ks[7], (N_FOX, HEAD_DIM)),
        "fox_k_gain": gain(ks[8], (N_FOX, HEAD_DIM)),
        "fox_w_out": nrm(ks[9], (N_FOX, ATTN_WIDTH, D_MODEL), ATTN_WIDTH),
        "mla_w_in": nrm(ks[10], (N_MLA, D_MODEL, MLA_Q_RANK + MLA_KV_RANK + MLA_ROPE), D_MODEL),
        "mla_q_norm": gain(ks[11], (N_MLA, MLA_Q_RANK)),
        "mla_kv_norm": gain(ks[12], (N_MLA, MLA_KV_RANK)),
        "mla_w_uq": nrm(ks[13], (N_MLA, MLA_Q_RANK, N_HEADS * MLA_QK), MLA_Q_RANK),
        "mla_w_ukv": nrm(ks[14], (N_MLA, MLA_KV_RANK, N_HEADS * (MLA_NOPE + MLA_V)), MLA_KV_RANK),
        "mla_q_gain": gain(ks[15], (N_MLA, MLA_QK)),
        "mla_k_gain": gain(ks[16], (N_MLA, MLA_QK)),
        "mla_w_out": nrm(ks[17], (N_MLA, N_HEADS * MLA_V, D_MODEL), N_HEADS * MLA_V),
        "mlp_w1": nrm(ks[18], (DEPTH, D_MODEL, D_FF), D_MODEL),
        "mlp_w2": nrm(ks[19], (DEPTH, D_FF, D_MODEL), D_FF),
    }


def reference(x, positions, mix_norm, mlp_norm, sb_w_in, sb_w_out, fox_w_in, fox_b_f, fox_q_gain,
              fox_k_gain, fox_w_out, mla_w_in, mla_q_norm, mla_kv_norm, mla_w_uq, mla_w_ukv,
              mla_q_gain, mla_k_gain, mla_w_out, mlp_w1, mlp_w2):
    for i in range(DEPTH):
        kind, j = i % N_MIXERS, i // N_MIXERS
        h = rmsnorm(x, mix_norm[i])
        if kind == 0:
            y = stick_breaking_mixer(h, sb_w_in[j], sb_w_out[j])
        elif kind == 1:
            y = forgetting_mixer(h, fox_w_in[j], fox_b_f[j], fox_q_gain[j], fox_k_gain[j], fox_w_out[j])
        else:
            y = mla_mixer(h, positions, mla_w_in[j], mla_q_norm[j], mla_kv_norm[j], mla_w_uq[j],
                          mla_w_ukv[j], mla_q_gain[j], mla_k_gain[j], mla_w_out[j])
        x = x + y
        x = x + sq_relu_mlp(rmsnorm(x, mlp_norm[i]), mlp_w1[i], mlp_w2[i])
    return x
```

```python
import math
import numpy as np
import ml_dtypes
import concourse.bass as bass
import concourse.mybir as mybir
from concourse.bass_utils import run_bass_kernel_spmd

F32 = mybir.dt.float32
BF16 = mybir.dt.bfloat16
I32 = mybir.dt.int32
U8 = mybir.dt.uint8
AF = mybir.ActivationFunctionType
ALU = mybir.AluOpType

D = 2048
NCH = 16
TOK = 1024
TB = 512
DFF = 8192
NH = 16
EPS = 1e-6
NEG = -30000.0
GROUPS = [[0, 1], [2, 3], [4, 5], [6, 7]]
AG_GLOBAL = [0, 3, 1, 2]
GCOL = {0: 0, 3: 512, 1: 1024, 2: 1536}
MASKW = 898
DEBUG = False
SKIP_MLP = False
MLA_STOP = None
DBG_HEAD = 0


class Sem:
    def __init__(self, h, name):
        self.h = h
        self.n = 0
        self.name = name


class Tl:
    __slots__ = ("w", "r", "name", "pend")

    def __init__(self, name=""):
        self.w = None
        self.r = {}
        self.name = name
        self.pend = 0


class Eng:
    def __init__(self, name, sem):
        self.name = name
        self.sem = sem
        self.ops = []
        self.waited = {}


class Prog:
    def __init__(self, nc, stack):
        self.nc = nc
        self.stack = stack
        self.nsem = 0
        self.pe = Eng("pe", self.new_sem("pe"))
        self.act = Eng("act", self.new_sem("act"))
        self.dve = Eng("dve", self.new_sem("dve"))
        self.pool = Eng("pool", self.new_sem("pool"))
        self.sp = Eng("sp", self.new_sem("sp"))
        self.pend_r = []
        self.pend_w = []

    def new_sem(self, name):
        self.nsem += 1
        h = self.stack.enter_context(self.nc.semaphore(f"s{self.nsem}_{name}"))
        return Sem(h, name)

    def emit(self, eng, fn, reads=(), writes=(), dsem=None, signal=True, inc=None):
        waits = {}
        is_pe = eng is self.pe

        def need(ev):
            if ev is None:
                return
            s, v = ev
            if is_pe and s is self.pe.sem:
                return
            if waits.get(s, 0) < v:
                waits[s] = v

        for t in reads:
            if not is_pe:
                assert t.w != "PEND", f"read of PE-pending tile {t.name}"
            if t.w != "PEND":
                need(t.w)
        for t in writes:
            if not is_pe:
                assert t.pend == 0, f"write to tile {t.name} with unsignaled PE access"
                assert t.w != "PEND"
            if t.w != "PEND":
                need(t.w)
            for s, v in t.r.items():
                need((s, v))
        wl = [(s, v) for s, v in waits.items() if eng.waited.get(s, 0) < v]
        for s, v in wl:
            eng.waited[s] = v
        if dsem is not None:
            k = 16 if inc is None else inc
            dsem.n += k
            ev = (dsem, dsem.n)
            incr = (dsem, k)
        elif signal:
            eng.sem.n += 1
            ev = (eng.sem, eng.sem.n)
            incr = (eng.sem, 1)
        else:
            ev = None
            incr = None
        eng.ops.append((wl, fn, incr))
        if is_pe and ev is None:
            for t in reads:
                t.pend += 1
                self.pend_r.append(t)
            for t in writes:
                t.r = {}
                if t.w != "PEND":
                    t.w = "PEND"
                    t.pend += 1
                    self.pend_w.append(t)
            return None
        if is_pe:
            for t in self.pend_r:
                t.pend -= 1
                if t.r.get(ev[0], 0) < ev[1]:
                    t.r[ev[0]] = ev[1]
            self.pend_r = []
            for t in self.pend_w:
                t.pend -= 1
                t.w = ev
            self.pend_w = []
        for t in reads:
            if t.r.get(ev[0], 0) < ev[1]:
                t.r[ev[0]] = ev[1]
        for t in writes:
            t.w = ev
            t.r = {}
        return ev

    def flush(self, block, final_waits):
        def run(e, eng, extra=()):
            for wl, fn, incr in eng.ops:
                for s, v in wl:
                    e.wait_ge(s.h, v)
                ins = fn(e)
                if incr is not None:
                    ins.then_inc(incr[0].h, incr[1])
            for s, v in extra:
                e.wait_ge(s.h, v)

        @block.tensor
        def _(e):
            run(e, self.pe)

        @block.scalar
        def _(e):
            run(e, self.act)

        @block.vector
        def _(e):
            run(e, self.dve)

        @block.gpsimd
        def _(e):
            run(e, self.pool)

        @block.sync
        def _(e):
            run(e, self.sp, final_waits)


class Builder:
    def __init__(self, layers, debug_mid=False):
        self.layers = layers
        self.nc = bass.Bass("TRN2", target_bir_lowering=False)
        self.build()

    def dram_in(self, name, shape, dt):
        return self.nc.dram_tensor(name, list(shape), dt, kind="ExternalInput").ap()

    def build(self):
        import contextlib
        nc = self.nc
        with contextlib.ExitStack() as stack:
            self.stack = stack
            P = self.P = Prog(nc, stack)
            self.x_in = self.dram_in("x_in", [TOK, D], F32)
            self.out = nc.dram_tensor("out", [TOK, D], F32, kind="ExternalOutput").ap()
            self.dbg = nc.dram_tensor("dbg", [128, 4096], F32, kind="ExternalOutput").ap() if DEBUG else None
            self.dbg_off = 0
            self.dbg_map = {}
            self.negm_d = self.dram_in("negm", [128, 4 * MASKW], BF16)
            self.cbf_d = self.dram_in("cbf", [128, 4 * 128], BF16)
            self.gains_d = self.dram_in("gains", [128, 128], F32)
            self.cols_d = self.dram_in("cols", [128, 64], F32)
            self.c32_d = self.dram_in("c32", [128, 3 * 128], F32)
            self.bfb_d = self.dram_in("bfb", [128, 128], F32)
            self.selw_d = self.dram_in("selw", [128, 8 * 128], F32)
            self.w = {}
            for li in self.layers:
                kind = li % 3
                if kind == 0:
                    self.w[(li, "in")] = self.dram_in(f"w{li}_in", [D, 3 * D], F32)
                elif kind == 1:
                    self.w[(li, "in")] = self.dram_in(f"w{li}_in", [D, 3 * D + NH], F32)
                else:
                    self.w[(li, "in")] = self.dram_in(f"w{li}_in", [D, 1344], F32)
                    self.w[(li, "uq")] = self.dram_in(f"w{li}_uq", [768, 3072], F32)
                    self.w[(li, "ukv")] = self.dram_in(f"w{li}_ukv", [512, 4096], F32)
                self.w[(li, "out")] = self.dram_in(f"w{li}_out", [D, D], F32)
                if not SKIP_MLP:
                    self.w[(li, "w1")] = self.dram_in(f"w{li}_w1", [D, DFF], F32)
                    self.w[(li, "w2")] = self.dram_in(f"w{li}_w2", [DFF, D], F32)
            self.kag_src = [nc.dram_tensor(f"kag_src{g}", [512, TOK], BF16) for g in range(4)]
            self.kag_dst = [nc.dram_tensor(f"kag_dst{g}", [1024, TOK], BF16) for g in range(4)]
            self.vag_src = [nc.dram_tensor(f"vag_src{g}", [512, TOK], BF16) for g in range(4)]
            self.vag_dst = [nc.dram_tensor(f"vag_dst{g}", [1024, TOK], BF16) for g in range(4)]
            self.q_scr = nc.dram_tensor("q_scr", [NH * 128, TOK], BF16)
            self.q_scr_mla = nc.dram_tensor("q_scr_mla", [NH * 192, TOK], BF16)
            self.kag_src_mla = [nc.dram_tensor(f"kag_src_mla{g}", [768, TOK], BF16) for g in range(4)]
            self.kag_dst_mla = [nc.dram_tensor(f"kag_dst_mla{g}", [1536, TOK], BF16) for g in range(4)]
            if any(l % 3 == 2 for l in self.layers):
                self.pos_d = self.dram_in("pos", [1, TOK], I32)
            self.t_kag_src = [Tl(f"kag_src{g}") for g in range(4)]
            self.t_kag_dst = [Tl(f"kag_dst{g}") for g in range(4)]
            self.t_vag_src = [Tl(f"vag_src{g}") for g in range(4)]
            self.t_vag_dst = [Tl(f"vag_dst{g}") for g in range(4)]
            self.t_q_scr = [Tl(f"q_scr{h}") for h in range(NH)]
            self.cc_sems = [P.new_sem(f"cc{i}") for i in range(9)]
            self.lf_src = nc.dram_tensor("lf_src", [TOK, NH], F32)
            self.lf_dst = nc.dram_tensor("lf_dst", [2 * TOK, NH], F32)
            self.cqT_scr = nc.dram_tensor("cqT_scr", [NH, TOK], F32)
            self.t_lf_src, self.t_lf_dst, self.t_cqT = Tl("lf_src"), Tl("lf_dst"), Tl("cqT")
            sb = lambda name, shape, dt: stack.enter_context(nc.sbuf_tensor(name, shape, dt))
            self.xT = sb("xT", [128, NCH, TOK], F32)
            self.hT = sb("hT", [128, NCH, TOK], BF16)
            self.WB = [sb(f"WB{i}", [128, 8192], BF16) for i in range(3)]
            self.regD = sb("regD", [128, 45056], U8)
            self.negm = sb("negm_sb", [128, 4, MASKW], BF16)
            self.cbf = sb("cbf_sb", [128, 4, 128], BF16)
            self.c32 = sb("c32_sb", [128, 3, 128], F32)
            self.foxc = sb("foxc_sb", [128, 256], F32)
            self.t_foxc = Tl("foxc")
            self.gains = sb("gains_sb", [128, 128], F32)
            self.cols = sb("cols_sb", [128, 64], F32)
            self.sq = sb("sq_sb", [128, 4, TB], BF16)
            self.rstd = sb("rstd_sb", [128, 2, TB], F32)
            self.psum = stack.enter_context(nc.psum_tensor("ps", [128, 8, TB], F32))
            self.t_x = [[Tl(f"x{c}_{tb}") for tb in range(2)] for c in range(NCH)]
            self.t_h = [[Tl(f"h{c}_{tb}") for tb in range(2)] for c in range(NCH)]
            self.t_WB = [Tl(f"WB{i}") for i in range(3)]
            self.wb_sem = [P.new_sem(f"wb{i}") for i in range(3)]
            self.wb_next = 0
            self.t_bank = [Tl(f"bank{i}") for i in range(8)]
            self.t_sq = [Tl(f"sq{i}") for i in range(4)]
            self.t_rstd = [Tl(f"rstd{i}") for i in range(2)]
            self.t_const = Tl("const")
            self.t_negm = Tl("negm")
            self.t_regD = []
            self.regD_retired = {}
            self.mm_rot = 0
            self.mm_mod = 4
            self.deferred = None
            self.ev_rot = 0
            self.sq_rot = 0
            self.rs_rot = 0
            self.misc_sems = {}

            self.id32 = self.c32[:, 0, :]
            self.ones32 = self.c32[:, 1, :]
            self.tri32 = self.c32[:, 2, :]
            self.ident = self.cbf[:, 0, :]
            self.ones = self.cbf[:, 1, :]
            self.negutri = self.cbf[:, 2, :]
            self.negones = self.cbf[:, 3, :]

            self.emit_setup()
            for li in self.layers:
                kind, j = li % 3, li // 3
                if kind == 0:
                    self.emit_sb_layer(li, j)
                elif kind == 1:
                    self.emit_fox_layer(li, j)
                else:
                    self.emit_mla_layer(li, j)
                if not SKIP_MLP:
                    self.emit_mlp(li)
            fin = self.emit_output()
            with nc.Block() as block:
                P.flush(block, fin)

    def debug_dump(self, name, ap, t, parts=128):
        if not DEBUG:
            return
        n = ap.shape[-1]
        self.dbg_map[name] = (self.dbg_off, n, parts)
        self.dma(self.P.sp, self.dbg[0:parts, self.dbg_off:self.dbg_off + n], ap, reads=[t], writes=[], sem=self.dsem("dbg"))
        self.dbg_off += n

    def dsem(self, name):
        if name not in self.misc_sems:
            self.misc_sems[name] = self.P.new_sem(name)
        return self.misc_sems[name]

    def regD_phase(self):
        ret = dict(self.regD_retired)
        for t in self.t_regD:
            assert t.pend == 0 and t.w != "PEND", t.name
            if t.w is not None:
                s, v = t.w
                if ret.get(s, 0) < v:
                    ret[s] = v
            for s, v in t.r.items():
                if ret.get(s, 0) < v:
                    ret[s] = v
        self.regD_retired = ret
        self.t_regD = []
        self.regD_off = 0

    def regD_alloc(self, nbytes, dt, name, parts=128):
        off = self.regD_off
        assert off % 4 == 0
        self.regD_off += nbytes
        assert self.regD_off <= 45056, (name, self.regD_off)
        ap = self.regD[0:parts, off:off + nbytes].bitcast(dt)
        t = Tl(name)
        t.r = dict(self.regD_retired)
        self.t_regD.append(t)
        return ap, t

    def regD_tile(self, name):
        t = Tl(name)
        t.r = dict(self.regD_retired)
        self.t_regD.append(t)
        return t

    def bank(self, i):
        return self.psum[:, i, :]

    def next_mm_bank(self, lo=0, n=None):
        n = n or self.mm_mod
        b = lo + self.mm_rot % n
        self.mm_rot += 1
        return b

    def evac_engine(self):
        self.ev_rot += 1
        return self.P.act if self.ev_rot % 2 == 0 else self.P.dve

    def mm(self, out_ap, lhsT, rhs, start, stop, reads, writes, signal):
        def fn(e, out_ap=out_ap, lhsT=lhsT, rhs=rhs, start=start, stop=stop):
            return e.matmul(out_ap, lhsT=lhsT, rhs=rhs, start=start, stop=stop)
        return self.P.emit(self.P.pe, fn, reads=reads, writes=writes, signal=signal)

    def act(self, out, in_, func, reads, writes, bias=None, scale=None):
        def fn(e, out=out, in_=in_, func=func, bias=bias, scale=scale):
            kw = {}
            if bias is not None:
                kw["bias"] = bias
            if scale is not None:
                kw["scale"] = scale
            return e.activation(out=out, in_=in_, func=func, **kw)
        return self.P.emit(self.P.act, fn, reads=reads, writes=writes)

    def copy(self, eng, out, in_, reads, writes):
        if eng is self.P.act:
            return self.act(out, in_, AF.Copy, reads, writes)
        def fn(e, out=out, in_=in_):
            return e.tensor_copy(out=out, in_=in_)
        return self.P.emit(eng, fn, reads=reads, writes=writes)

    def tt(self, out, in0, in1, op, reads, writes, eng=None):
        def fn(e, out=out, in0=in0, in1=in1, op=op):
            return e.tensor_tensor(out=out, in0=in0, in1=in1, op=op)
        return self.P.emit(eng or self.P.dve, fn, reads=reads, writes=writes)

    def ts(self, out, in0, s1, s2, op0, op1, reads, writes):
        def fn(e, out=out, in0=in0, s1=s1, s2=s2, op0=op0, op1=op1):
            if op1 is None:
                return e.tensor_scalar(out=out, in0=in0, scalar1=s1, scalar2=None, op0=op0)
            return e.tensor_scalar(out=out, in0=in0, scalar1=s1, scalar2=s2, op0=op0, op1=op1)
        return self.P.emit(self.P.dve, fn, reads=reads, writes=writes)

    def stt(self, out, in0, scalar, in1, op0, op1, reads, writes):
        def fn(e, out=out, in0=in0, scalar=scalar, in1=in1, op0=op0, op1=op1):
            return e.scalar_tensor_tensor(out=out, in0=in0, scalar=scalar, in1=in1, op0=op0, op1=op1)
        return self.P.emit(self.P.dve, fn, reads=reads, writes=writes)

    def dma(self, q, out, in_, reads, writes, sem):
        def fn(e, out=out, in_=in_):
            return e.dma_start(out=out, in_=in_)
        return self.P.emit(q, fn, reads=reads, writes=writes, dsem=sem)

    def load_w(self, views):
        i = self.wb_next % 3
        self.wb_next += 1
        for k, (dst_fn, src) in enumerate(views):
            self.dma(self.P.pool, dst_fn(self.WB[i]), src, reads=[], writes=[self.t_WB[i]], sem=self.wb_sem[i])
        return i

    def allgather(self, src, dst, t_src, t_dst, slot):
        sem = self.cc_sems[slot]
        def fn(e, src=src, dst=dst):
            return e.collective_compute("AllGather", ALU.bypass, replica_groups=GROUPS,
                                        ins=[src.ap().opt()], outs=[dst.ap().opt()])
        return self.P.emit(self.P.pool, fn, reads=[t_src], writes=[t_dst], dsem=sem, inc=1)

    def emit_setup(self):
        P = self.P
        cs = self.dsem("const")
        for dst, src in ((self.negm[:].rearrange("p a b -> p (a b)"), self.negm_d), (self.cbf[:].rearrange("p a b -> p (a b)"), self.cbf_d),
                         (self.c32[:].rearrange("p a b -> p (a b)"), self.c32_d), (self.gains[:], self.gains_d), (self.cols[:], self.cols_d)):
            self.dma(P.sp, dst, src, reads=[], writes=[self.t_const], sem=cs)
        self.regD_phase()
        xs = []
        for j in range(4):
            ap, t = self.regD_alloc(8192, F32, f"xstage{j}")
            xs.append((ap, t, P.new_sem(f"xs{j}")))
        for tb in range(2):
            for j in range(4):
                ap, t, sem = xs[j]
                r0 = tb * TB + j * 128
                self.dma(P.sp, ap, self.x_in[r0:r0 + 128, :], reads=[], writes=[t], sem=sem)
            for c in range(NCH):
                b = self.next_mm_bank()
                for j in range(4):
                    ap, t, sem = xs[j]
                    def fn(e, o=self.psum[:, b, j * 128:(j + 1) * 128], i=ap[:, c * 128:(c + 1) * 128]):
                        return e.transpose(o, i, self.id32)
                    P.emit(P.pe, fn, reads=[t, self.t_const], writes=[self.t_bank[b]], signal=(j == 3))
                self.copy(self.evac_engine(), self.xT[:, c, tb * TB:(tb + 1) * TB], self.bank(b),
                          reads=[self.t_bank[b]], writes=[self.t_x[c][tb]])

    def emit_norm(self, gi):
        P = self.P
        for tb in range(2):
            b = self.next_mm_bank()
            for c in range(NCH):
                s = self.sq_rot % 4
                self.sq_rot += 1
                self.act(self.sq[:, s, :], self.xT[:, c, tb * TB:(tb + 1) * TB], AF.Square,
                         reads=[self.t_x[c][tb]], writes=[self.t_sq[s]])
                self.mm(self.bank(b), self.ones, self.sq[:, s, :], c == 0, c == NCH - 1,
                        reads=[self.t_sq[s], self.t_const], writes=[self.t_bank[b]], signal=True)
            self.act(self.rstd[:, tb, :], self.bank(b), AF.Ln, reads=[self.t_bank[b]], writes=[self.t_rstd[tb]],
                     bias=float(EPS), scale=1.0 / D)
            self.act(self.rstd[:, tb, :], self.rstd[:, tb, :], AF.Exp, reads=[self.t_rstd[tb]], writes=[self.t_rstd[tb]],
                     scale=-0.5)
            for c in range(NCH):
                self.stt(self.hT[:, c, tb * TB:(tb + 1) * TB], self.xT[:, c, tb * TB:(tb + 1) * TB],
                         self.gains[:, gi * 16 + c:gi * 16 + c + 1], self.rstd[:, tb, :], ALU.mult, ALU.mult,
                         reads=[self.t_x[c][tb], self.t_rstd[tb], self.t_const], writes=[self.t_h[c][tb]])

    def proj_fm(self, wi, wview, col0, evac, nkc=NCH, src=None, src_t=None, ncols=128, extra_reads=()):
        src = src if src is not None else self.hT
        src_t = src_t if src_t is not None else self.t_h
        for tb in range(2):
            b = self.next_mm_bank()
            for kc in range(nkc):
                self.mm(self.psum[0:ncols, b, :], wview[:, kc, col0:col0 + ncols], src[:, kc, tb * TB:(tb + 1) * TB],
                        kc == 0, kc == nkc - 1, reads=[self.t_WB[wi], src_t[kc][tb]] + list(extra_reads),
                        writes=[self.t_bank[b]], signal=(kc == nkc - 1))
            if self.deferred is not None:
                d, self.deferred = self.deferred, None
                d()
            r = evac(b, tb)
            self.deferred = r if callable(r) else None

    def flush_deferred(self):
        if self.deferred is not None:
            d, self.deferred = self.deferred, None
            d()

    def proj_tm(self, wi, wview_cols, evac, nkc=NCH, src=None, src_t=None, out3=False):
        src = src if src is not None else self.hT
        src_t = src_t if src_t is not None else self.t_h
        for tt in range(8):
            b = self.next_mm_bank()
            for kc in range(nkc):
                o = self.bank(b).rearrange("p (h x) -> p h x", x=128) if out3 else self.bank(b)
                self.mm(o, src[:, kc, tt * 128:(tt + 1) * 128], wview_cols(kc),
                        kc == 0, kc == nkc - 1, reads=[self.t_WB[wi], src_t[kc][tt // 4]],
                        writes=[self.t_bank[b]], signal=(kc == nkc - 1))
            evac(b, tt)

    def w_tile16(self, src2d):
        return [(lambda wb: wb[:, :].rearrange("p (c n) -> p c n", n=512), src2d.rearrange("(c p) n -> p c n", p=128))]

    def emit_qkv_generic(self, li, w_in, kcol, vcol, qcol, q_evac_scale, qk_norm=None):
        P = self.P
        self.regD_phase()
        stages = []
        for i in range(3):
            ap, t = self.regD_alloc(8192, BF16, f"stage{i}")
            stages.append((ap, t, self.dsem(f"stage{i}")))
        srot = [0]

        def next_stage():
            s = stages[srot[0] % 3]
            srot[0] += 1
            return s

        for g in range(4):
            wi = self.load_w(self.w_tile16(w_in[:, kcol + g * 512:kcol + (g + 1) * 512]))
            wv = self.WB[wi][:, :].rearrange("p (c n) -> p c n", n=512)
            st_ap, st_t, st_sem = next_stage()
            stv = st_ap.rearrange("p (h t) -> p h t", t=TOK)
            for m in range(4):
                def evac(b, tb, m=m, stv=stv, st_t=st_t):
                    if qk_norm is not None:
                        return self.evac_headnorm(b, stv[:, m, tb * TB:(tb + 1) * TB], st_t, qk_norm[1], 128)
                    self.copy(self.evac_engine(), stv[:, m, tb * TB:(tb + 1) * TB], self.bank(b),
                              reads=[self.t_bank[b]], writes=[st_t])
                self.proj_fm(wi, wv, m * 128, evac)
            self.flush_deferred()
            self.dma(P.sp, self.kag_src[g].ap().rearrange("(h d) t -> d h t", d=128), stv,
                     reads=[st_t], writes=[self.t_kag_src[g]], sem=st_sem)
            self.allgather(self.kag_src[g], self.kag_dst[g], self.t_kag_src[g], self.t_kag_dst[g], 2 * g)
            wi = self.load_w(self.w_tile16(w_in[:, vcol + g * 512:vcol + (g + 1) * 512]))
            wv = self.WB[wi][:, :].rearrange("p (c n) -> p c n", n=512)
            st_ap, st_t, st_sem = next_stage()
            stv = st_ap.rearrange("p (n f) -> p n f", f=512)

            def evac_v(b, tt, stv=stv, st_t=st_t):
                self.copy(self.evac_engine(), stv[:, tt, :], self.bank(b), reads=[self.t_bank[b]], writes=[st_t])
            self.proj_tm(wi, lambda kc, wv=wv: wv[:, kc, :], evac_v)
            vdst = self.vag_src[g].ap().rearrange("r (two f) -> (r two) f", two=2).rearrange("(n p) f -> p n f", p=128)
            self.dma(P.sp, vdst, stv, reads=[st_t], writes=[self.t_vag_src[g]], sem=st_sem)
            self.allgather(self.vag_src[g], self.vag_dst[g], self.t_vag_src[g], self.t_vag_dst[g], 2 * g + 1)
        for g in range(4):
            wi = self.load_w(self.w_tile16(w_in[:, qcol + g * 512:qcol + (g + 1) * 512]))
            wv = self.WB[wi][:, :].rearrange("p (c n) -> p c n", n=512)
            st_ap, st_t, st_sem = next_stage()
            stv = st_ap.rearrange("p (h t) -> p h t", t=TOK)
            for m in range(4):
                def evac(b, tb, m=m, stv=stv, st_t=st_t):
                    o = stv[:, m, tb * TB:(tb + 1) * TB]
                    if qk_norm is not None:
                        return self.evac_headnorm(b, o, st_t, qk_norm[0], 128)
                    self.act(o, self.bank(b), AF.Copy, reads=[self.t_bank[b]], writes=[st_t], scale=q_evac_scale)
                self.proj_fm(wi, wv, m * 128, evac)
            self.flush_deferred()
            qdst = self.q_scr.ap()[g * 512:(g + 1) * 512, :].rearrange("(h d) t -> d h t", d=128)
            self.dma(P.sp, qdst, stv, reads=[st_t], writes=[self.t_q_scr[4 * g + m] for m in range(4)], sem=st_sem)

    def attn_load(self, h, bf, mla=False):
        P = self.P
        g, m = h // 4, h % 4
        q_ap, q_t, sq_ = bf["q"]
        k_ap, k_t, sk_ = bf["k"]
        v_ap, v_t, sv_ = bf["v"]
        HR = 192 if mla else 128
        qsrc = (self.q_scr_mla if mla else self.q_scr).ap()
        self.dma(P.sp, q_ap, qsrc[h * HR:h * HR + 128, :], reads=[self.t_q_scr[h]], writes=[q_t], sem=sq_)
        kd = (self.kag_dst_mla if mla else self.kag_dst)[g]
        ksrc = kd.ap().rearrange("(r x) t -> x r t", r=2)
        self.dma(P.sp, k_ap.rearrange("p (r t) -> p r t", r=2), ksrc[m * HR:m * HR + 128, :, :],
                 reads=[self.t_kag_dst[g]], writes=[k_t], sem=sk_)
        if mla:
            qr_ap, qr_t, sqr_ = bf["qr"]
            kr_ap, kr_t, skr_ = bf["kr"]
            self.dma(P.sp, qr_ap, qsrc[h * HR + 128:(h + 1) * HR, :], reads=[self.t_q_scr[h]], writes=[qr_t], sem=sqr_)
            self.dma(P.sp, kr_ap.rearrange("p (r t) -> p r t", r=2), ksrc[m * HR + 128:(m + 1) * HR, :, :],
                     reads=[self.t_kag_dst[g]], writes=[kr_t], sem=skr_)
        vsrc = self.vag_dst[g].ap().rearrange("(r x) (two f) -> r (x two) f", r=2, two=2)
        vsrc = vsrc.rearrange("r (n p) f -> p r n f", p=128)[:, :, :, m * 128:(m + 1) * 128]
        for r in range(2):
            self.dma(P.sp, v_ap.rearrange("p (r n f) -> p r n f", r=2, f=128)[:, r, :, :], vsrc[:, r, :, :],
                     reads=[self.t_vag_dst[g]], writes=[v_t], sem=sv_)

    def attn_bufs(self, mla=False):
        bufs = []
        for i in range(2):
            d = {}
            d["q"] = self.regD_alloc(2048, BF16, f"q{i}") + (self.dsem(f"aq{i}"),)
            d["k"] = self.regD_alloc(4096, BF16, f"k{i}") + (self.dsem(f"ak{i}"),)
            d["v"] = self.regD_alloc(4096, BF16, f"v{i}") + (self.dsem(f"av{i}"),)
            if mla:
                d["qr"] = self.regD_alloc(2048, BF16, f"qr{i}", parts=64) + (self.dsem(f"aqr{i}"),)
                d["kr"] = self.regD_alloc(4096, BF16, f"kr{i}", parts=64) + (self.dsem(f"akr{i}"),)
            bufs.append(d)
        return bufs

    def key_tiles(self, lq, reverse):
        blocks = [(0, 0), (1, 1)] if lq == 0 else [(0, None), (1, None), (2, 2), (3, 3)]
        res = [(G, i, u) for (G, u) in blocks for i in range(4)]
        return res[::-1] if reverse else res

    def emit_sb_attention(self):
        P = self.P
        self.regD_phase()
        bufs = self.attn_bufs()
        NS = 3
        e_t = [self.regD_alloc(2048, F32, f"e{i}") for i in range(NS)]
        sp_t = [self.regD_alloc(1024, BF16, f"sp{i}") for i in range(NS)]
        p_t = [self.regD_alloc(1024, BF16, f"p{i}") for i in range(NS)]
        sps32 = [self.regD_alloc(2048, F32, f"sps32_{i}") for i in range(2)]
        spsbf = [self.regD_alloc(1024, BF16, f"spsbf_{i}") for i in range(2)]
        self.attn_load(0, bufs[0])
        items = []
        for h in range(NH):
            seqs = [self.key_tiles(1, True), self.key_tiles(0, True)]
            for n in range(16):
                for si, lq in ((0, 1), (1, 0)):
                    if n < len(seqs[si]):
                        G, i, u = seqs[si][n]
                        items.append((h, lq, n, G, i, u, n == len(seqs[si]) - 1))
        rot = [0]
        state = {}

        def s1(it):
            h, lq, n, G, i, u, last = it
            bf = bufs[h % 2]
            q_ap, q_t, k_ap, k_t = bf["q"][0], bf["q"][1], bf["k"][0], bf["k"][1]
            if lq == 1 and n == 3 and h + 1 < NH:
                self.attn_load(h + 1, bufs[(h + 1) % 2])
            slot = rot[0] % NS
            rot[0] += 1
            ba = self.next_mm_bank(0, 4)
            kc0 = GCOL[G] + i * 128
            masked = u is not None
            self.mm(self.bank(ba), k_ap[:, kc0:kc0 + 128], q_ap[:, lq * TB:(lq + 1) * TB], True, not masked,
                    reads=[k_t, q_t], writes=[self.t_bank[ba]], signal=not masked)
            if masked:
                c0 = 384 - 128 * i
                self.mm(self.bank(ba), self.ident, self.negm[:, u, c0:c0 + TB], False, True,
                        reads=[self.t_const], writes=[self.t_bank[ba]], signal=True)
            ea, et = e_t[slot]
            sa, st = sp_t[slot]
            self.act(ea, self.bank(ba), AF.Exp, reads=[self.t_bank[ba]], writes=[et])
            self.act(sa, ea, AF.Ln, reads=[et], writes=[st], bias=1.0)
            state[(h, lq, n)] = (slot, ba)

        def s2(it):
            h, lq, n, G, i, u, last = it
            slot, ba = state[(h, lq, n)]
            sa, st = sp_t[slot]
            pa, pt = p_t[slot]
            s32a, s32t = sps32[lq]
            sbfa, sbft = spsbf[lq]
            self.mm(self.bank(ba), self.negutri, sa, False, n == 0, reads=[st, self.t_const],
                    writes=[self.t_bank[ba]], signal=(n == 0))
            if n > 0:
                self.mm(self.bank(ba), self.negones, sbfa, False, True, reads=[sbft, self.t_const],
                        writes=[self.t_bank[ba]], signal=True)
            self.act(pa, self.bank(ba), AF.Exp, reads=[self.t_bank[ba]], writes=[pt])
            if not last:
                if n == 0:
                    self.copy(P.dve, s32a, sa, reads=[st], writes=[s32t])
                else:
                    self.tt(s32a, s32a, sa, ALU.add, reads=[s32t, st], writes=[s32t])
                self.copy(P.dve, sbfa, s32a, reads=[s32t], writes=[sbft])

        def s3(it):
            h, lq, n, G, i, u, last = it
            slot, ba = state.pop((h, lq, n))
            v_ap, v_t = bufs[h % 2]["v"][0], bufs[h % 2]["v"][1]
            pa, pt = p_t[slot]
            bo = 4 + (h % 2) * 2 + lq
            vt = AG_GLOBAL.index(G) * 4 + i
            self.mm(self.bank(bo), v_ap[:, vt * 128:(vt + 1) * 128], pa, n == 0, last,
                    reads=[v_t, pt], writes=[self.t_bank[bo]], signal=True)
            if last:
                self.copy(self.evac_engine(), self.hT[:, h, lq * TB:(lq + 1) * TB], self.bank(bo),
                          reads=[self.t_bank[bo]], writes=[self.t_h[h][lq]])

        n_it = len(items)
        for k in range(n_it + 2):
            if k < n_it:
                s1(items[k])
            if 0 <= k - 1 < n_it:
                s2(items[k - 1])
            if 0 <= k - 2 < n_it:
                s3(items[k - 2])


    def evac_headnorm(self, b, out_ap, out_t, gcol, nfeat):
        s = self.sq_rot % 4
        self.sq_rot += 1
        self.act(self.sq[:, s, :], self.bank(b), AF.Square, reads=[self.t_bank[b]], writes=[self.t_sq[s]])

        def cont():
            b2 = self.next_mm_bank()
            self.mm(self.bank(b2), self.ones, self.sq[:, s, :], True, True, reads=[self.t_sq[s], self.t_const],
                    writes=[self.t_bank[b2]], signal=True)
            rs = self.rs_rot % 2
            self.rs_rot += 1
            self.act(self.rstd[:, rs, :], self.bank(b2), AF.Ln, reads=[self.t_bank[b2]], writes=[self.t_rstd[rs]],
                     bias=float(EPS), scale=1.0 / nfeat)
            self.act(self.rstd[:, rs, :], self.rstd[:, rs, :], AF.Exp, reads=[self.t_rstd[rs]], writes=[self.t_rstd[rs]],
                     scale=-0.5)
            self.stt(out_ap, self.bank(b), self.cols[:, gcol:gcol + 1], self.rstd[:, rs, :], ALU.mult, ALU.mult,
                     reads=[self.t_bank[b], self.t_rstd[rs], self.t_const], writes=[out_t])
        return cont

    def emit_fox_gate(self, li):
        P = self.P
        w_in = self.w[(li, "in")]
        self.regD_phase()
        L_ap, L_t = self.regD_alloc(512, F32, "Lown")
        La_ap, La_t = self.regD_alloc(1024, F32, "Lall")
        cq_ap, cq_t = self.regD_alloc(512, F32, "Cq")
        cqT_ap, cqT_t = self.regD_alloc(4096, F32, "CqT", parts=16)
        bfb_ap, bfb_t = self.regD_alloc(512, F32, "bfb")
        selw_ap, selw_t = self.regD_alloc(4096, F32, "selw")
        gs = self.dsem("gate")
        self.dma(P.sp, bfb_ap, self.bfb_d, reads=[], writes=[bfb_t], sem=gs)
        self.dma(P.sp, selw_ap, self.selw_d, reads=[], writes=[selw_t], sem=gs)
        selw = selw_ap.rearrange("p (a b) -> p a b", b=128)
        wi = self.load_w([(lambda wb: wb[:, 0:256].rearrange("p (c n) -> p c n", n=NH),
                           w_in[:, 3 * D:3 * D + NH].rearrange("(c p) n -> p c n", p=128))])
        wv = self.WB[wi][:, 0:256].rearrange("p (c n) -> p c n", n=NH)
        b = self.next_mm_bank()
        for tt in range(8):
            for c in range(NCH):
                self.mm(self.psum[:, b, tt * NH:(tt + 1) * NH], self.hT[:, c, tt * 128:(tt + 1) * 128], wv[:, c, :],
                        c == 0, c == NCH - 1, reads=[self.t_WB[wi], self.t_h[c][tt // 4]], writes=[self.t_bank[b]],
                        signal=(c == NCH - 1))
        self.tt(L_ap, self.psum[:, b, 0:128], bfb_ap, ALU.add, reads=[self.t_bank[b], bfb_t], writes=[L_t])
        self.act(L_ap, L_ap, AF.Exp, reads=[L_t], writes=[L_t], scale=-1.0)
        self.act(L_ap, L_ap, AF.Ln, reads=[L_t], writes=[L_t], bias=1.0)
        self.dma(P.sp, self.lf_src.ap().rearrange("(n p) h -> p n h", p=128), L_ap.rearrange("p (n h) -> p n h", h=NH),
                 reads=[L_t], writes=[self.t_lf_src], sem=gs)
        self.allgather(self.lf_src, self.lf_dst, self.t_lf_src, self.t_lf_dst, 8)
        self.dma(P.sp, La_ap.rearrange("p (n h) -> p n h", h=NH), self.lf_dst.ap().rearrange("(n p) h -> p n h", p=128),
                 reads=[self.t_lf_dst], writes=[La_t], sem=gs)
        Lall = La_ap.rearrange("p (n h) -> p n h", h=NH)
        Lown = L_ap.rearrange("p (n h) -> p n h", h=NH)
        bk = self.next_mm_bank()
        gt = lambda T: AG_GLOBAL[T // 4] * 4 + T % 4
        for T in range(16):
            terms = [(self.ones32, Tp) for Tp in range(16) if gt(Tp) < gt(T)] + [(self.tri32, T)]
            for k, (lt, Tp) in enumerate(terms):
                self.mm(self.psum[:, bk, T * NH:(T + 1) * NH], lt, Lall[:, Tp, :], k == 0, k == len(terms) - 1,
                        reads=[La_t, self.t_const], writes=[self.t_bank[bk]], signal=(k == len(terms) - 1))
        self.copy(P.dve, self.foxc[:, :], self.psum[:, bk, 0:256], reads=[self.t_bank[bk]], writes=[self.t_foxc])
        self.debug_dump("Lown", L_ap, L_t)
        self.debug_dump("Lall", La_ap, La_t)
        self.debug_dump("Ck", self.foxc[:, :], self.t_foxc)
        bq = self.next_mm_bank()
        for t in range(8):
            l = t // 4
            terms = [(self.ones32, Lown[:, tp, :], L_t) for tp in range(l * 4, t)] + [(self.tri32, Lown[:, t, :], L_t)]
            terms += [(selw[:, jb * 2 + l, :], Lall[:, jb * 4 + i, :], La_t) for jb in range(4) for i in range(4)]
            for k, (lt, rh, rt) in enumerate(terms):
                self.mm(self.psum[:, bq, t * NH:(t + 1) * NH], lt, rh, k == 0, k == len(terms) - 1,
                        reads=[rt, selw_t, self.t_const], writes=[self.t_bank[bq]], signal=(k == len(terms) - 1))
        self.copy(P.dve, cq_ap, self.psum[:, bq, 0:128], reads=[self.t_bank[bq]], writes=[cq_t])
        self.debug_dump("Cq", cq_ap, cq_t)
        for half in range(2):
            bt = self.next_mm_bank()
            for k in range(4):
                t = half * 4 + k
                def fn(e, o=self.psum[0:NH, bt, k * 128:(k + 1) * 128], i=cq_ap[:, t * NH:(t + 1) * NH]):
                    return e.transpose(o, i, self.id32)
                P.emit(P.pe, fn, reads=[cq_t, self.t_const], writes=[self.t_bank[bt]], signal=(k == 3))
            self.act(cqT_ap[:, half * TB:(half + 1) * TB], self.psum[0:NH, bt, :], AF.Copy,
                     reads=[self.t_bank[bt]], writes=[cqT_t], scale=-1.0)
        self.dma(P.sp, self.cqT_scr.ap(), cqT_ap, reads=[cqT_t], writes=[self.t_cqT], sem=gs)

    def emit_softmax_attention(self, fox, scale, mla=False):
        P = self.P
        self.regD_phase()
        bufs = self.attn_bufs(mla)
        NS = 3
        p_t = [self.regD_alloc(1024, BF16, f"p{i}") for i in range(NS)]
        rinv = [self.regD_alloc(2048, F32, f"rinv{i}") for i in range(2)]
        if fox:
            tmp_t = [self.regD_alloc(2048, F32, f"tmp{i}") for i in range(NS)]
            cfb = [self.regD_alloc(4096, F32, f"cfb{i}") + (self.dsem(f"cfb{i}"),) for i in range(2)]

        def load(h):
            self.attn_load(h, bufs[h % 2], mla)
            if fox:
                ap, t, sem = cfb[h % 2]
                self.dma(P.sp, ap, self.cqT_scr.ap()[h:h + 1, :].partition_broadcast(128), reads=[self.t_cqT],
                         writes=[t], sem=sem)
        load(0)
        items = []
        for h in range(NH):
            seqs = [self.key_tiles(1, False), self.key_tiles(0, False)]
            for n in range(16):
                for si, lq in ((0, 1), (1, 0)):
                    if n < len(seqs[si]):
                        G, i, u = seqs[si][n]
                        items.append((h, lq, n, G, i, u, n == len(seqs[si]) - 1))
        rot = [0]
        state = {}

        def s1(it):
            h, lq, n, G, i, u, last = it
            bf = bufs[h % 2]
            if lq == 1 and n == 3 and h + 1 < NH:
                load(h + 1)
            slot = rot[0] % NS
            rot[0] += 1
            ba = self.next_mm_bank(0, 4)
            kc0 = GCOL[G] + i * 128
            masked = u is not None
            qs = slice(lq * TB, (lq + 1) * TB)
            self.mm(self.bank(ba), bf["k"][0][:, kc0:kc0 + 128], bf["q"][0][:, qs], True, not (masked or mla),
                    reads=[bf["k"][1], bf["q"][1]], writes=[self.t_bank[ba]], signal=not (masked or mla))
            if mla:
                self.mm(self.bank(ba), bf["kr"][0][:, kc0:kc0 + 128], bf["qr"][0][:, qs], False, not masked,
                        reads=[bf["kr"][1], bf["qr"][1]], writes=[self.t_bank[ba]], signal=not masked)
            if masked:
                c0 = 385 - 128 * i
                self.mm(self.bank(ba), self.ident, self.negm[:, u, c0:c0 + TB], False, True,
                        reads=[self.t_const], writes=[self.t_bank[ba]], signal=True)
            pa, pt = p_t[slot]
            if fox:
                ta, tt_ = tmp_t[slot]
                ca, ct, _ = cfb[h % 2]
                self.stt(ta, self.bank(ba), float(scale), ca[:, qs], ALU.mult, ALU.add,
                         reads=[self.t_bank[ba], ct], writes=[tt_])
                T_ag = AG_GLOBAL.index(G) * 4 + i
                self.act(pa, ta, AF.Exp, reads=[tt_, self.t_foxc], writes=[pt],
                         bias=self.foxc[:, T_ag * NH + h:T_ag * NH + h + 1])
            else:
                self.act(pa, self.bank(ba), AF.Exp, reads=[self.t_bank[ba]], writes=[pt], scale=float(scale))
            state[(h, lq, n)] = slot

        def s2(it):
            h, lq, n, G, i, u, last = it
            slot = state.pop((h, lq, n))
            bf = bufs[h % 2]
            pa, pt = p_t[slot]
            bo = 4 + lq * 2
            bs = bo + 1
            vt = AG_GLOBAL.index(G) * 4 + i
            self.mm(self.bank(bo), bf["v"][0][:, vt * 128:(vt + 1) * 128], pa, n == 0, last,
                    reads=[bf["v"][1], pt], writes=[self.t_bank[bo]], signal=False)
            self.mm(self.bank(bs), self.ones, pa, n == 0, last,
                    reads=[pt, self.t_const], writes=[self.t_bank[bs]], signal=True)
            if last:
                ra, rt = rinv[lq]
                self.act(ra, self.bank(bs), AF.Ln, reads=[self.t_bank[bs]], writes=[rt])
                self.act(ra, ra, AF.Exp, reads=[rt], writes=[rt], scale=-1.0)
                self.tt(self.hT[:, h, lq * TB:(lq + 1) * TB], self.bank(bo), ra, ALU.mult,
                        reads=[self.t_bank[bo], rt], writes=[self.t_h[h][lq]])
                if DEBUG and h == DBG_HEAD and lq == 1:
                    d1, d1t = self.regD_alloc(2048, F32, "dbg1")
                    self.copy(P.dve, d1, self.hT[:, h, lq * TB:(lq + 1) * TB], reads=[self.t_h[h][lq]], writes=[d1t])
                    self.debug_dump("oT", d1, d1t)
                    self.copy(P.dve, d1, ra, reads=[rt, d1t], writes=[d1t])
                    self.debug_dump("rinv", d1, d1t)
                    self.copy(P.dve, d1, bf["v"][0][:, 4 * 128:8 * 128], reads=[bf["v"][1], d1t], writes=[d1t])
                    self.debug_dump("v47", d1, d1t)

        n_it = len(items)
        for k in range(n_it + 1):
            if k < n_it:
                s1(items[k])
            if 0 <= k - 1 < n_it:
                s2(items[k - 1])

    def emit_fox_layer(self, li, j):
        w_in = self.w[(li, "in")]
        self.emit_norm(li)
        self.emit_fox_gate(li)
        self.emit_qkv_generic(li, w_in, kcol=D, vcol=2 * D, qcol=0, q_evac_scale=1.0, qk_norm=(0, 1))
        self.emit_softmax_attention(True, 1.0 / math.sqrt(128.0))
        self.emit_outproj(self.w[(li, "out")])

    def emit_outproj(self, w_out):
        for t in range(4):
            wi = self.load_w(self.w_tile16(w_out[:, t * 512:(t + 1) * 512]))
            wv = self.WB[wi][:, :].rearrange("p (c n) -> p c n", n=512)
            for m in range(4):
                c = 4 * t + m
                def evac(b, tb, c=c):
                    xs = self.xT[:, c, tb * TB:(tb + 1) * TB]
                    self.tt(xs, xs, self.bank(b), ALU.add, reads=[self.t_x[c][tb], self.t_bank[b]], writes=[self.t_x[c][tb]])
                self.proj_fm(wi, wv, m * 128, evac)

    def emit_sb_layer(self, li, j):
        w_in = self.w[(li, "in")]
        self.emit_norm(li)
        self.emit_qkv_generic(li, w_in, kcol=D, vcol=2 * D, qcol=0, q_evac_scale=1.0 / math.sqrt(128.0))
        self.emit_sb_attention()
        self.emit_outproj(self.w[(li, "out")])


    def regD_subphase(self, keep, off):
        ret = dict(self.regD_retired)
        rest = []
        for t in self.t_regD:
            if any(t is k for k in keep):
                rest.append(t)
                continue
            assert t.pend == 0 and t.w != "PEND", t.name
            if t.w is not None:
                s, v = t.w
                if ret.get(s, 0) < v:
                    ret[s] = v
            for s, v in t.r.items():
                if ret.get(s, 0) < v:
                    ret[s] = v
        self.regD_retired = ret
        self.t_regD = rest
        self.regD_off = off

    def emit_sincos(self, ang, ang_t, out_ap, out_t, shift, tmp, tmp_t, nn, nn_t):
        MAGIC = 12582912.0
        TWO_PI_HI = 6.28125
        TWO_PI_LO = 2.0 * math.pi - 6.28125
        self.ts(tmp, ang, 1.0 / (2.0 * math.pi), shift / (2.0 * math.pi), ALU.mult, ALU.add, reads=[ang_t], writes=[tmp_t])
        self.ts(nn, tmp, MAGIC, None, ALU.add, None, reads=[tmp_t], writes=[nn_t])
        self.ts(nn, nn, -MAGIC, None, ALU.add, None, reads=[nn_t], writes=[nn_t])
        self.stt(tmp, nn, -TWO_PI_HI, ang, ALU.mult, ALU.add, reads=[nn_t, ang_t], writes=[tmp_t])
        self.stt(tmp, nn, -TWO_PI_LO, tmp, ALU.mult, ALU.add, reads=[nn_t, tmp_t], writes=[tmp_t])
        self.ts(tmp, tmp, float(shift), math.pi, ALU.add, ALU.min, reads=[tmp_t], writes=[tmp_t])
        self.ts(tmp, tmp, -math.pi, None, ALU.max, None, reads=[tmp_t], writes=[tmp_t])
        self.act(out_ap, tmp, AF.Sin, reads=[tmp_t], writes=[out_t])

    def emit_mla_layer(self, li, j):
        P = self.P
        w_in, w_uq, w_ukv = self.w[(li, "in")], self.w[(li, "uq")], self.w[(li, "ukv")]
        self.emit_norm(li)
        self.regD_phase()
        cos_ap, cos_t = self.regD_alloc(4096, F32, "cos", parts=64)
        sin_ap, sin_t = self.regD_alloc(4096, F32, "sinS", parts=64)
        krR_ap, krR_t = self.regD_alloc(4096, F32, "krR", parts=64)
        krsq_ap, krsq_t = self.regD_alloc(2048, BF16, "krsq", parts=64)
        keep = [cos_t, sin_t, krR_t, krsq_t]
        keep_off = self.regD_off
        posi_ap, posi_t = self.regD_alloc(4096, I32, "posi", parts=64)
        ang_ap, ang_t = self.regD_alloc(4096, F32, "ang", parts=64)
        tmp_ap, tmp_t = self.regD_alloc(4096, F32, "sctmp", parts=64)
        nn_ap, nn_t = self.regD_alloc(4096, F32, "scn", parts=64)
        ms = self.dsem("mla")
        self.dma(P.sp, posi_ap, self.pos_d.partition_broadcast(64), reads=[], writes=[posi_t], sem=ms)
        self.copy(P.dve, ang_ap, posi_ap, reads=[posi_t], writes=[ang_t])
        self.ts(ang_ap, ang_ap, self.cols[0:64, 18:19], None, ALU.mult, None, reads=[ang_t, self.t_const], writes=[ang_t])
        self.emit_sincos(ang_ap, ang_t, sin_ap, sin_t, 0.0, tmp_ap, tmp_t, nn_ap, nn_t)
        self.ts(sin_ap, sin_ap, self.cols[0:64, 19:20], None, ALU.mult, None, reads=[sin_t, self.t_const], writes=[sin_t])
        self.emit_sincos(ang_ap, ang_t, cos_ap, cos_t, math.pi / 2.0, tmp_ap, tmp_t, nn_ap, nn_t)
        if MLA_STOP == 1:
            return
        self.regD_subphase(keep, keep_off)
        st32_ap, st32_t0 = self.regD_alloc(20480, F32, "st32")
        st32 = st32_ap.rearrange("p (c t) -> p c t", t=TB)
        st32_t = [st32_t0] + [self.regD_tile(f"st32_{c}") for c in range(1, 10)]
        kr32_ap, kr32_t = self.regD_alloc(2048, F32, "kr32", parts=64)
        krs32_ap, krs32_t = self.regD_alloc(2048, F32, "krs32", parts=64)
        BQ, BKV = 6, 7
        for tb in range(2):
            tbs = slice(tb * TB, (tb + 1) * TB)
            for T in range(3):
                if T < 2:
                    wi = self.load_w(self.w_tile16(w_in[:, T * 512:(T + 1) * 512]))
                    wv = self.WB[wi][:, :].rearrange("p (c n) -> p c n", n=512)
                    chunks = [(T * 4 + m, m * 128, 128) for m in range(4)]
                else:
                    v3 = lambda wb: wb[:, 0:16 * 384].rearrange("p (c n) -> p c n", n=384)
                    src = lambda a, b: w_in[:, a:b].rearrange("(c p) n -> p c n", p=128)
                    wi = self.load_w([(lambda wb: v3(wb)[:, :, 0:320], src(1024, 1344)),
                                      (lambda wb: v3(wb)[:, :, 320:352], src(1312, 1344)),
                                      (lambda wb: v3(wb)[:, :, 352:384], src(1280, 1312))])
                    wv = v3(self.WB[wi])
                    chunks = [(8, 0, 128), (9, 128, 128), ("kr", 256, 64), ("krs", 320, 64)]
                for ch, col0, ncols in chunks:
                    self.mla_down_chunk(wi, wv, ch, col0, ncols, tb, st32, st32_t, kr32_ap, kr32_t, krs32_ap, krs32_t, BQ, BKV)
            self.flush_deferred()
            for rs, bnk, n in ((0, BQ, 768), (1, BKV, 512)):
                self.act(self.rstd[:, rs, :], self.bank(bnk), AF.Ln, reads=[self.t_bank[bnk]], writes=[self.t_rstd[rs]],
                         bias=float(EPS), scale=1.0 / n)
                self.act(self.rstd[:, rs, :], self.rstd[:, rs, :], AF.Exp, reads=[self.t_rstd[rs]], writes=[self.t_rstd[rs]],
                         scale=-0.5)
            for ch in range(10):
                rs = 0 if ch < 6 else 1
                self.stt(self.hT[:, ch, tbs], st32[:, ch, :], self.cols[:, 8 + ch:9 + ch], self.rstd[:, rs, :], ALU.mult, ALU.mult,
                         reads=[st32_t[ch], self.t_rstd[rs], self.t_const], writes=[self.t_h[ch][tb]])
            self.tt(kr32_ap, kr32_ap, cos_ap[:, tbs], ALU.mult, reads=[kr32_t, cos_t], writes=[kr32_t])
            self.tt(krs32_ap, krs32_ap, sin_ap[:, tbs], ALU.mult, reads=[krs32_t, sin_t], writes=[krs32_t])
            self.tt(krR_ap[:, tbs], kr32_ap, krs32_ap, ALU.add, reads=[kr32_t, krs32_t], writes=[krR_t])
            self.act(krsq_ap[:, tbs], krR_ap[:, tbs], AF.Square, reads=[krR_t], writes=[krsq_t])
        if MLA_STOP == 2:
            return
        self.regD_subphase(keep, keep_off)
        kst_ap, kst_t = self.regD_alloc(8192, BF16, "kstage")
        ksr_ap, ksr_t = self.regD_alloc(8192, BF16, "kstage_r", parts=64)
        vst_ap, vst_t = self.regD_alloc(8192, BF16, "vstage")
        tA_ap, tA_t = self.regD_alloc(2048, F32, "ropeA", parts=64)
        tB_ap, tB_t = self.regD_alloc(2048, F32, "ropeB", parts=64)
        kst = kst_ap.rearrange("p (h t) -> p h t", t=TOK)
        ksr = ksr_ap.rearrange("p (h t) -> p h t", t=TOK)
        vst = vst_ap.rearrange("p (n f) -> p n f", f=512)
        kvsrc = self.hT[:, 6:10, :]
        kvsrc_t = self.t_h[6:10]
        ones64 = self.cbf[0:64, 1, :]
        ks_sem, vs_sem = self.dsem("mla_ks"), self.dsem("mla_vs")

        def headnorm_tail(b_n, rope_ap, rope_t, sq_r_ap, sq_r_t, gn_col, gr_col, out_n, out_r, out_t_n, out_t_r, s):
            b2 = self.next_mm_bank()
            self.mm(self.bank(b2), self.ones, self.sq[:, s, :], True, False, reads=[self.t_sq[s], self.t_const],
                    writes=[self.t_bank[b2]], signal=False)
            self.mm(self.bank(b2), ones64, sq_r_ap, False, True, reads=[sq_r_t, self.t_const],
                    writes=[self.t_bank[b2]], signal=True)
            rs = self.rs_rot % 2
            self.rs_rot += 1
            self.act(self.rstd[:, rs, :], self.bank(b2), AF.Ln, reads=[self.t_bank[b2]], writes=[self.t_rstd[rs]],
                     bias=float(EPS), scale=1.0 / 192.0)
            self.act(self.rstd[:, rs, :], self.rstd[:, rs, :], AF.Exp, reads=[self.t_rstd[rs]], writes=[self.t_rstd[rs]],
                     scale=-0.5)
            self.stt(out_n, self.bank(b_n), self.cols[:, gn_col:gn_col + 1], self.rstd[:, rs, :], ALU.mult, ALU.mult,
                     reads=[self.t_bank[b_n], self.t_rstd[rs], self.t_const], writes=[out_t_n])
            self.stt(out_r, rope_ap, self.cols[0:64, gr_col:gr_col + 1], self.rstd[0:64, rs, :], ALU.mult, ALU.mult,
                     reads=[rope_t, self.t_rstd[rs], self.t_const], writes=[out_t_r])

        for g in range(4):
            src = w_ukv[:, g * 1024:(g + 1) * 1024].rearrange("(c p) n -> p c n", p=128)
            wi = self.load_w([(lambda wb: wb[:, 0:4096].rearrange("p (c n) -> p c n", n=1024), src)])
            wv = self.WB[wi][:, 0:4096].rearrange("p (c n) -> p c n", n=1024)
            for m in range(4):
                def evac(b, tb, m=m):
                    tbs = slice(tb * TB, (tb + 1) * TB)
                    s = self.sq_rot % 4
                    self.sq_rot += 1
                    self.act(self.sq[:, s, :], self.bank(b), AF.Square, reads=[self.t_bank[b]], writes=[self.t_sq[s]])
                    return lambda: headnorm_tail(b, krR_ap[:, tbs], krR_t, krsq_ap[:, tbs], krsq_t, 4, 5,
                                                 kst[:, m, tbs], ksr[:, m, tbs], kst_t, ksr_t, s)
                self.proj_fm(wi, wv, m * 256, evac, nkc=4, src=kvsrc, src_t=kvsrc_t)
            self.flush_deferred()
            kd = self.kag_src_mla[g].ap().rearrange("(h x) t -> x h t", x=192)
            self.dma(P.sp, kd[0:128, :, :], kst, reads=[kst_t], writes=[self.t_kag_src[g]], sem=ks_sem)
            self.dma(P.sp, kd[128:192, :, :], ksr, reads=[ksr_t], writes=[self.t_kag_src[g]], sem=ks_sem)
            self.allgather(self.kag_src_mla[g], self.kag_dst_mla[g], self.t_kag_src[g], self.t_kag_dst[g], 2 * g)

            def evac_v(b, tt):
                self.copy(self.evac_engine(), vst[:, tt, :], self.bank(b), reads=[self.t_bank[b]], writes=[vst_t])
            self.proj_tm(wi, lambda kc, wv=wv: wv[:, kc, :].rearrange("p (h x) -> p h x", x=256)[:, :, 128:256], evac_v,
                         nkc=4, src=kvsrc, src_t=kvsrc_t, out3=True)
            vdst = self.vag_src[g].ap().rearrange("r (two f) -> (r two) f", two=2).rearrange("(n p) f -> p n f", p=128)
            self.dma(P.sp, vdst, vst, reads=[vst_t], writes=[self.t_vag_src[g]], sem=vs_sem)
            self.allgather(self.vag_src[g], self.vag_dst[g], self.t_vag_src[g], self.t_vag_dst[g], 2 * g + 1)

        self.mm_mod = 8
        for g in range(4):
            v4 = lambda wb: wb[:, 0:6144].rearrange("p (c n) -> p c n", n=1024)
            views = []
            for m in range(4):
                h = 4 * g + m
                srcf = lambda a, b: w_uq[:, a:b].rearrange("(c p) n -> p c n", p=128)
                views.append((lambda wb, m=m: v4(wb)[:, :, m * 256:m * 256 + 192], srcf(h * 192, h * 192 + 192)))
                views.append((lambda wb, m=m: v4(wb)[:, :, m * 256 + 192:m * 256 + 224], srcf(h * 192 + 160, h * 192 + 192)))
                views.append((lambda wb, m=m: v4(wb)[:, :, m * 256 + 224:m * 256 + 256], srcf(h * 192 + 128, h * 192 + 160)))
            wi = self.load_w(views)
            wv = v4(self.WB[wi])
            pending = None
            for m in range(4):
                for tb in range(2):
                    tbs = slice(tb * TB, (tb + 1) * TB)
                    banks = []
                    for col0, ncols in ((m * 256, 128), (m * 256 + 128, 64), (m * 256 + 192, 64)):
                        b = self.next_mm_bank()
                        banks.append(b)
                        for kc in range(6):
                            self.mm(self.psum[0:ncols, b, :], wv[:, kc, col0:col0 + ncols], self.hT[:, kc, tbs],
                                    kc == 0, kc == 5, reads=[self.t_WB[wi], self.t_h[kc][tb]],
                                    writes=[self.t_bank[b]], signal=(kc == 5))
                    if pending is not None:
                        pending()
                    bn, br, bs = banks
                    self.tt(tA_ap, self.psum[0:64, br, :], cos_ap[:, tbs], ALU.mult, reads=[self.t_bank[br], cos_t], writes=[tA_t])
                    self.tt(tB_ap, self.psum[0:64, bs, :], sin_ap[:, tbs], ALU.mult, reads=[self.t_bank[bs], sin_t], writes=[tB_t])
                    self.tt(tA_ap, tA_ap, tB_ap, ALU.add, reads=[tA_t, tB_t], writes=[tA_t])
                    s = self.sq_rot % 4
                    self.sq_rot += 1
                    self.act(self.sq[:, s, :], self.bank(bn), AF.Square, reads=[self.t_bank[bn]], writes=[self.t_sq[s]])
                    s2 = self.sq_rot % 4
                    self.sq_rot += 1
                    self.act(self.sq[0:64, s2, :], tA_ap, AF.Square, reads=[tA_t], writes=[self.t_sq[s2]])
                    pending = (lambda bn=bn, s=s, s2=s2, m=m, tbs=tbs:
                               headnorm_tail(bn, tA_ap, tA_t, self.sq[0:64, s2, :], self.t_sq[s2], 2, 3,
                                             kst[:, m, tbs], ksr[:, m, tbs], kst_t, ksr_t, s))
            pending()
            qd = self.q_scr_mla.ap()[g * 768:(g + 1) * 768, :].rearrange("(h x) t -> x h t", x=192)
            self.dma(P.sp, qd[0:128, :, :], kst, reads=[kst_t], writes=[self.t_q_scr[4 * g + m] for m in range(4)], sem=ks_sem)
            self.dma(P.sp, qd[128:192, :, :], ksr, reads=[ksr_t], writes=[self.t_q_scr[4 * g + m] for m in range(4)], sem=ks_sem)
        self.mm_mod = 4
        if MLA_STOP == 3:
            return
        self.emit_softmax_attention(False, 1.0 / math.sqrt(192.0), mla=True)
        if MLA_STOP == 4:
            return
        self.emit_outproj(self.w[(li, "out")])

    def mla_down_chunk(self, wi, wv, ch, col0, ncols, tb, st32, st32_t, kr32_ap, kr32_t, krs32_ap, krs32_t, BQ, BKV):
        P = self.P
        b = self.next_mm_bank()
        tbs = slice(tb * TB, (tb + 1) * TB)
        for kc in range(NCH):
            self.mm(self.psum[0:ncols, b, :], wv[:, kc, col0:col0 + ncols], self.hT[:, kc, tbs],
                    kc == 0, kc == NCH - 1, reads=[self.t_WB[wi], self.t_h[kc][tb]],
                    writes=[self.t_bank[b]], signal=(kc == NCH - 1))
        if self.deferred is not None:
            d, self.deferred = self.deferred, None
            d()
        if ch == "kr":
            self.copy(P.dve, kr32_ap, self.psum[0:64, b, :], reads=[self.t_bank[b]], writes=[kr32_t])
            return
        if ch == "krs":
            self.copy(P.dve, krs32_ap, self.psum[0:64, b, :], reads=[self.t_bank[b]], writes=[krs32_t])
            return
        s = self.sq_rot % 4
        self.sq_rot += 1
        self.act(self.sq[:, s, :], self.bank(b), AF.Square, reads=[self.t_bank[b]], writes=[self.t_sq[s]])
        self.copy(P.dve, st32[:, ch, :], self.bank(b), reads=[self.t_bank[b], self.t_sq[s]], writes=[st32_t[ch]])
        acc = BQ if ch < 6 else BKV
        first = ch in (0, 6)
        lastc = ch in (5, 9)

        def cont():
            self.mm(self.bank(acc), self.ones, self.sq[:, s, :], first, lastc, reads=[self.t_sq[s], self.t_const],
                    writes=[self.t_bank[acc]], signal=True)
        self.deferred = cont

    def emit_mlp(self, li):
        P = self.P
        self.emit_norm(4 + li)
        self.regD_phase()
        aT = [self.regD_alloc(8192, BF16, f"aT{i}") for i in range(2)]
        sqt = [self.regD_alloc(2048, F32, f"sqt{i}") for i in range(2)]
        w1 = self.w[(li, "w1")]
        w2 = self.w[(li, "w2")]
        NG = DFF // 512
        rot = [0]

        def ffn1(g):
            wi = self.load_w(self.w_tile16(w1[:, g * 512:(g + 1) * 512]))
            wv = self.WB[wi][:, :].rearrange("p (c n) -> p c n", n=512)
            a_ap, a_t = aT[g % 2]
            av = a_ap.rearrange("p (m t) -> p m t", t=TOK)
            for m in range(4):
                def evac(b, tb, m=m):
                    sa, st = sqt[rot[0] % 2]
                    rot[0] += 1
                    self.act(sa, self.bank(b), AF.Square, reads=[self.t_bank[b]], writes=[st])
                    self.stt(av[:, m, tb * TB:(tb + 1) * TB], self.bank(b), 0.0, sa, ALU.is_gt, ALU.mult,
                             reads=[self.t_bank[b], st], writes=[a_t])
                self.proj_fm(wi, wv, m * 128, evac)

        def ffn2(g):
            src = w2[g * 512:(g + 1) * 512, :].rearrange("(m p) n -> p m n", p=128)
            wi = self.load_w([(lambda wb: wb[:, :].rearrange("p (m n) -> p m n", n=D), src)])
            wv = self.WB[wi][:, :].rearrange("p (m n) -> p m n", n=D)
            a_ap, a_t = aT[g % 2]
            av = a_ap.rearrange("p (m t) -> p m t", t=TOK)
            for c in range(NCH):
                for tb in range(2):
                    b = 4 + self.next_mm_bank(0, 4)
                    for m in range(4):
                        self.mm(self.bank(b), wv[:, m, c * 128:(c + 1) * 128], av[:, m, tb * TB:(tb + 1) * TB],
                                m == 0, m == 3, reads=[self.t_WB[wi], a_t], writes=[self.t_bank[b]], signal=(m == 3))
                    xs = self.xT[:, c, tb * TB:(tb + 1) * TB]
                    self.tt(xs, xs, self.bank(b), ALU.add, reads=[self.t_x[c][tb], self.t_bank[b]], writes=[self.t_x[c][tb]])

        ffn1(0)
        for g in range(NG):
            if g + 1 < NG:
                ffn1(g + 1)
            ffn2(g)

    def emit_output(self):
        P = self.P
        self.regD_phase()
        osem = self.dsem("out")
        st = [self.regD_alloc(8192, F32, f"ostage{i}") for i in range(2)]
        for tt in range(8):
            o_ap, o_t = st[tt % 2]
            for c4 in range(4):
                b = self.next_mm_bank()
                for k in range(4):
                    c = c4 * 4 + k
                    def fn(e, o=self.psum[:, b, k * 128:(k + 1) * 128], i=self.xT[:, c, tt * 128:(tt + 1) * 128]):
                        return e.transpose(o, i, self.id32)
                    P.emit(P.pe, fn, reads=[self.t_x[c][tt // 4], self.t_const], writes=[self.t_bank[b]], signal=(k == 3))
                self.copy(self.evac_engine(), o_ap[:, c4 * 512:(c4 + 1) * 512], self.bank(b),
                          reads=[self.t_bank[b]], writes=[o_t])
            self.dma(P.sp, self.out[tt * 128:(tt + 1) * 128, :], o_ap, reads=[o_t], writes=[], sem=osem)
        fin = [(osem, osem.n)]
        if DEBUG and "dbg" in self.misc_sems:
            fin.append((self.misc_sems["dbg"], self.misc_sems["dbg"].n))
        return fin


def _bf16(a):
    return np.asarray(a, dtype=np.float32).astype(ml_dtypes.bfloat16)


def _negmask(rank):
    types = [["diag", "zero", "full", "diag"], ["full", "diag", "diag", "zero"]][rank]
    out = np.zeros((128, 4, MASKW), np.float32)
    k = np.arange(128)[:, None]
    for u, ty in enumerate(types):
        if ty == "zero":
            out[:, u, :] = NEG
        elif ty == "diag":
            out[:, u, 0] = NEG
            s = np.arange(MASKW - 1)[None, :]
            m = np.where(s < 384, NEG, np.where(s < 512, np.where(k > (s - 384), NEG, 0.0), 0.0))
            out[:, u, 1:] = m
    return _bf16(out.reshape(128, 4 * MASKW))


def _consts_bf():
    j = np.arange(128)[:, None]
    s = np.arange(128)[None, :]
    ident = (j == s).astype(np.float32)
    ones = np.ones((128, 128), np.float32)
    negutri = np.where(j >= s, -1.0, 0.0).astype(np.float32)
    negones = -ones
    return _bf16(np.concatenate([ident, ones, negutri, negones], axis=1))


def _col_layout(v):
    v = np.asarray(v, np.float32)
    return np.ascontiguousarray(v.reshape(-1, 128).T)


_BUILD_CACHE = {}


def _get_nc(layers):
    key = tuple(layers)
    if key not in _BUILD_CACHE:
        _BUILD_CACHE[key] = Builder(list(layers))
    return _BUILD_CACHE[key].nc


def _own_rows(rank):
    blocks = [0, 3] if rank == 0 else [1, 2]
    return np.concatenate([np.arange(b * TB, (b + 1) * TB) for b in blocks])


def _selw(rank):
    mine = [0, 3] if rank == 0 else [1, 2]
    out = np.zeros((128, 8, 128), np.float32)
    for jb in range(4):
        for l in range(2):
            if AG_GLOBAL[jb] < mine[l]:
                out[:, jb * 2 + l, :] = 1.0
    return out.reshape(128, 8 * 128)


def run_layers(layers, x, inputs):
    nc = _get_nc(layers)
    gains = np.zeros((128, 128), np.float32)
    for n in range(4):
        gains[:, n * 16:(n + 1) * 16] = _col_layout(inputs["mix_norm"][n])
        gains[:, (4 + n) * 16:(5 + n) * 16] = _col_layout(inputs["mlp_norm"][n])
    cols = np.zeros((128, 64), np.float32)
    cols[:, 0] = inputs["fox_q_gain"][0]
    cols[:, 1] = inputs["fox_k_gain"][0]
    cols[:, 2] = inputs["mla_q_gain"][0][:128]
    cols[:64, 3] = inputs["mla_q_gain"][0][128:]
    cols[:, 4] = inputs["mla_k_gain"][0][:128]
    cols[:64, 5] = inputs["mla_k_gain"][0][128:]
    cols[:, 8:14] = _col_layout(inputs["mla_q_norm"][0])
    cols[:, 14:18] = _col_layout(inputs["mla_kv_norm"][0])
    half = 32
    invf = (10000.0 ** (-np.arange(0, half, dtype=np.float32) * 2.0 / 64.0)).astype(np.float32)
    cols[:64, 18] = np.concatenate([invf, invf])
    cols[:64, 19] = np.concatenate([-np.ones(half, np.float32), np.ones(half, np.float32)])
    cols[:, 20] = -math.pi
    bfb = np.ascontiguousarray(np.broadcast_to(np.tile(np.asarray(inputs["fox_b_f"][0], np.float32), 8)[None, :], (128, 128)))
    j = np.arange(128)[:, None]
    f = np.arange(128)[None, :]
    c32 = np.concatenate([np.eye(128, dtype=np.float32), np.ones((128, 128), np.float32),
                          (j <= f).astype(np.float32)], axis=1)
    cbf = _consts_bf()
    shared = {}
    for li in layers:
        kind, jj = li % 3, li // 3
        if kind == 0:
            shared[f"w{li}_in"] = np.ascontiguousarray(inputs["sb_w_in"][jj])
            shared[f"w{li}_out"] = np.ascontiguousarray(inputs["sb_w_out"][jj])
        elif kind == 1:
            shared[f"w{li}_in"] = np.ascontiguousarray(inputs["fox_w_in"][jj])
            shared[f"w{li}_out"] = np.ascontiguousarray(inputs["fox_w_out"][jj])
        else:
            shared[f"w{li}_in"] = np.ascontiguousarray(inputs["mla_w_in"][jj])
            shared[f"w{li}_uq"] = np.ascontiguousarray(inputs["mla_w_uq"][jj])
            shared[f"w{li}_ukv"] = np.ascontiguousarray(inputs["mla_w_ukv"][jj])
            shared[f"w{li}_out"] = np.ascontiguousarray(inputs["mla_w_out"][jj])
        if not SKIP_MLP:
            shared[f"w{li}_w1"] = np.ascontiguousarray(inputs["mlp_w1"][li])
            shared[f"w{li}_w2"] = np.ascontiguousarray(inputs["mlp_w2"][li])
    pos = np.asarray(inputs["positions"])
    in_maps = []
    for c in range(8):
        b, r = c // 2, c % 2
        m = {
            "x_in": np.ascontiguousarray(x[b][_own_rows(r)]),
            "negm": _negmask(r),
            "cbf": cbf,
            "c32": c32,
            "bfb": bfb,
            "selw": _selw(r),
            "gains": gains,
            "cols": cols,
        }
        if any(l % 3 == 2 for l in layers):
            m["pos"] = np.ascontiguousarray(pos[b][_own_rows(r)][None, :].astype(np.int32))
        m.update(shared)
        in_maps.append(m)
    res = run_bass_kernel_spmd(nc, in_maps, core_ids=list(range(8)))
    out = np.empty((4, 2048, D), np.float32)
    for c in range(8):
        b, r = c // 2, c % 2
        out[b][_own_rows(r)] = np.asarray(res.results[c]["out"])
    if DEBUG:
        global LAST_DBG
        LAST_DBG = [np.asarray(res.results[c]["dbg"]) for c in range(8)]
    return out


def kernel(**inputs):
    inputs = {k: np.asarray(v) for k, v in inputs.items()}
    x = np.asarray(inputs["x"], np.float32)
    return run_layers([0, 1, 2, 3], x, inputs)
```

```python
import math
import numpy as np
import ml_dtypes
import concourse.bass as bass
import concourse.mybir as mybir
from concourse.bass_utils import run_bass_kernel_spmd

F32 = mybir.dt.float32
BF16 = mybir.dt.bfloat16
I32 = mybir.dt.int32
U8 = mybir.dt.uint8
AF = mybir.ActivationFunctionType
ALU = mybir.AluOpType

D = 2048
NCH = 16
TOK = 1024
TB = 512
DFF = 8192
NH = 16
EPS = 1e-6
NEG = -30000.0
GROUPS = [[0, 1], [2, 3], [4, 5], [6, 7]]
AG_GLOBAL = [0, 3, 1, 2]
GCOL = {0: 0, 3: 512, 1: 1024, 2: 1536}
MASKW = 898
DEBUG = False
PROFILE_SCOPES = False
SKIP_MLP = False
MLA_STOP = None
DBG_HEAD = 0


class Sem:
    def __init__(self, h, name):
        self.h = h
        self.n = 0
        self.name = name


class Tl:
    __slots__ = ("w", "r", "name", "pend")

    def __init__(self, name=""):
        self.w = None
        self.r = {}
        self.name = name
        self.pend = 0


class Eng:
    def __init__(self, name, sem):
        self.name = name
        self.sem = sem
        self.ops = []
        self.waited = {}


class Prog:
    def __init__(self, nc, stack):
        self.nc = nc
        self.stack = stack
        self.nsem = 0
        self.pe = Eng("pe", self.new_sem("pe"))
        self.act = Eng("act", self.new_sem("act"))
        self.dve = Eng("dve", self.new_sem("dve"))
        self.pool = Eng("pool", self.new_sem("pool"))
        self.sp = Eng("sp", self.new_sem("sp"))
        self.pend_r = []
        self.pend_w = []
        self.scope = None

    def new_sem(self, name):
        self.nsem += 1
        h = self.stack.enter_context(self.nc.semaphore(f"s{self.nsem}_{name}"))
        return Sem(h, name)

    def emit(self, eng, fn, reads=(), writes=(), dsem=None, signal=True, inc=None):
        waits = {}
        is_pe = eng is self.pe

        def need(ev):
            if ev is None:
                return
            s, v = ev
            if is_pe and s is self.pe.sem:
                return
            if waits.get(s, 0) < v:
                waits[s] = v

        for t in reads:
            if not is_pe:
                assert t.w != "PEND", f"read of PE-pending tile {t.name}"
            if t.w != "PEND":
                need(t.w)
        for t in writes:
            if not is_pe:
                assert t.pend == 0, f"write to tile {t.name} with unsignaled PE access"
                assert t.w != "PEND"
            if t.w != "PEND":
                need(t.w)
            for s, v in t.r.items():
                need((s, v))
        wl = [(s, v) for s, v in waits.items() if eng.waited.get(s, 0) < v]
        for s, v in wl:
            eng.waited[s] = v
        if dsem is not None:
            k = 16 if inc is None else inc
            dsem.n += k
            ev = (dsem, dsem.n)
            incr = (dsem, k)
        elif signal:
            eng.sem.n += 1
            ev = (eng.sem, eng.sem.n)
            incr = (eng.sem, 1)
        else:
            ev = None
            incr = None
        eng.ops.append((wl, fn, incr, self.scope))
        if is_pe and ev is None:
            for t in reads:
                t.pend += 1
                self.pend_r.append(t)
            for t in writes:
                t.r = {}
                if t.w != "PEND":
                    t.w = "PEND"
                    t.pend += 1
                    self.pend_w.append(t)
            return None
        if is_pe:
            for t in self.pend_r:
                t.pend -= 1
                if t.r.get(ev[0], 0) < ev[1]:
                    t.r[ev[0]] = ev[1]
            self.pend_r = []
            for t in self.pend_w:
                t.pend -= 1
                t.w = ev
            self.pend_w = []
        for t in reads:
            if t.r.get(ev[0], 0) < ev[1]:
                t.r[ev[0]] = ev[1]
        for t in writes:
            t.w = ev
            t.r = {}
        return ev

    def flush(self, block, final_waits):
        def run(e, eng, extra=()):
            cur = None
            sid = None
            for wl, fn, incr, scope in eng.ops:
                if PROFILE_SCOPES and scope != cur:
                    if cur is not None:
                        self.nc.leave_named_scope(cur, sid, False)
                    cur = scope
                    if cur is not None:
                        sid, _ = self.nc.enter_named_scope(cur, False)
                for s, v in wl:
                    e.wait_ge(s.h, v)
                ins = fn(e)
                if incr is not None:
                    ins.then_inc(incr[0].h, incr[1])
            if PROFILE_SCOPES and cur is not None:
                self.nc.leave_named_scope(cur, sid, False)
            for s, v in extra:
                e.wait_ge(s.h, v)

        @block.tensor
        def _(e):
            run(e, self.pe)

        @block.scalar
        def _(e):
            run(e, self.act)

        @block.vector
        def _(e):
            run(e, self.dve)

        @block.gpsimd
        def _(e):
            run(e, self.pool)

        @block.sync
        def _(e):
            run(e, self.sp, final_waits)


class Builder:
    def __init__(self, layers, debug_mid=False):
        self.layers = layers
        self.nc = bass.Bass("TRN2", target_bir_lowering=False)
        self.build()

    def dram_in(self, name, shape, dt):
        return self.nc.dram_tensor(name, list(shape), dt, kind="ExternalInput").ap()

    def build(self):
        import contextlib
        nc = self.nc
        with contextlib.ExitStack() as stack:
            self.stack = stack
            P = self.P = Prog(nc, stack)
            self.x_in = self.dram_in("x_in", [TOK, D], F32)
            self.out = nc.dram_tensor("out", [TOK, D], F32, kind="ExternalOutput").ap()
            self.dbg = nc.dram_tensor("dbg", [128, 4096], F32, kind="ExternalOutput").ap() if DEBUG else None
            self.dbg_off = 0
            self.dbg_map = {}
            self.negm_d = self.dram_in("negm", [128, 4 * MASKW], BF16)
            self.cbf_d = self.dram_in("cbf", [128, 4 * 128], BF16)
            self.gains_d = self.dram_in("gains", [128, 128], F32)
            self.cols_d = self.dram_in("cols", [128, 64], F32)
            self.c32_d = self.dram_in("c32", [128, 3 * 128], F32)
            self.bfb_d = self.dram_in("bfb", [128, 128], F32)
            self.selw_d = self.dram_in("selw", [128, 8 * 128], F32)
            self.w = {}
            for li in self.layers:
                kind = li % 3
                if kind == 0:
                    self.w[(li, "in")] = self.dram_in(f"w{li}_in", [12, 128, 8192], F32)
                elif kind == 1:
                    self.w[(li, "in")] = self.dram_in(f"w{li}_in", [12, 128, 8192], F32)
                    self.w[(li, "gate")] = self.dram_in(f"w{li}_gate", [128, 256], F32)
                else:
                    self.w[(li, "in")] = self.dram_in(f"w{li}_in", [3, 128, 8192], F32)
                    self.w[(li, "uq")] = self.dram_in(f"w{li}_uq", [4, 128, 6144], F32)
                    self.w[(li, "ukv")] = self.dram_in(f"w{li}_ukv", [512, 4096], F32)
                self.w[(li, "out")] = self.dram_in(f"w{li}_out", [4, 128, 8192], F32)
                if not SKIP_MLP:
                    self.w[(li, "w1")] = self.dram_in(f"w{li}_w1", [16, 128, 8192], F32)
                    self.w[(li, "w2")] = self.dram_in(f"w{li}_w2", [DFF, D], F32)
            self.kag_src = [nc.dram_tensor(f"kag_src{g}", [512, TOK], BF16) for g in range(4)]
            self.kag_dst = [nc.dram_tensor(f"kag_dst{g}", [1024, TOK], BF16) for g in range(4)]
            self.vag_src = [nc.dram_tensor(f"vag_src{g}", [512, TOK], BF16) for g in range(4)]
            self.vag_dst = [nc.dram_tensor(f"vag_dst{g}", [1024, TOK], BF16) for g in range(4)]
            self.q_scr = nc.dram_tensor("q_scr", [NH * 128, TOK], BF16)
            self.q_scr_mla = nc.dram_tensor("q_scr_mla", [NH * 192, TOK], BF16)
            self.kag_src_mla = [nc.dram_tensor(f"kag_src_mla{g}", [768, TOK], BF16) for g in range(4)]
            self.kag_dst_mla = [nc.dram_tensor(f"kag_dst_mla{g}", [1536, TOK], BF16) for g in range(4)]
            if any(l % 3 == 2 for l in self.layers):
                self.pos_d = self.dram_in("pos", [1, TOK], I32)
            self.t_kag_src = [Tl(f"kag_src{g}") for g in range(4)]
            self.t_kag_dst = [Tl(f"kag_dst{g}") for g in range(4)]
            self.t_vag_src = [Tl(f"vag_src{g}") for g in range(4)]
            self.t_vag_dst = [Tl(f"vag_dst{g}") for g in range(4)]
            self.t_q_scr = [Tl(f"q_scr{h}") for h in range(NH)]
            self.cc_sems = [P.new_sem(f"cc{i}") for i in range(9)]
            self.lf_src = nc.dram_tensor("lf_src", [TOK, NH], F32)
            self.lf_dst = nc.dram_tensor("lf_dst", [2 * TOK, NH], F32)
            self.cqT_scr = nc.dram_tensor("cqT_scr", [NH, TOK], F32)
            self.t_lf_src, self.t_lf_dst, self.t_cqT = Tl("lf_src"), Tl("lf_dst"), Tl("cqT")
            sb = lambda name, shape, dt: stack.enter_context(nc.sbuf_tensor(name, shape, dt))
            self.xT = sb("xT", [128, NCH, TOK], F32)
            self.hT = sb("hT", [128, NCH, TOK], BF16)
            self.WB = [sb(f"WB{i}", [128, 8192], BF16) for i in range(3)]
            self.regD = sb("regD", [128, 45056], U8)
            self.negm = sb("negm_sb", [128, 4, MASKW], BF16)
            self.cbf = sb("cbf_sb", [128, 4, 128], BF16)
            self.c32 = sb("c32_sb", [128, 3, 128], F32)
            self.foxc = sb("foxc_sb", [128, 256], F32)
            self.t_foxc = Tl("foxc")
            self.gains = sb("gains_sb", [128, 128], F32)
            self.cols = sb("cols_sb", [128, 64], F32)
            self.sq = sb("sq_sb", [128, 4, TB], BF16)
            self.rstd = sb("rstd_sb", [128, 2, TB], F32)
            self.psum = stack.enter_context(nc.psum_tensor("ps", [128, 8, TB], F32))
            self.t_x = [[Tl(f"x{c}_{tb}") for tb in range(2)] for c in range(NCH)]
            self.t_h = [[Tl(f"h{c}_{tb}") for tb in range(2)] for c in range(NCH)]
            self.t_WB = [Tl(f"WB{i}") for i in range(3)]
            self.wb_sem = [P.new_sem(f"wb{i}") for i in range(3)]
            self.wb_next = 0
            self.t_bank = [Tl(f"bank{i}") for i in range(8)]
            self.t_sq = [Tl(f"sq{i}") for i in range(4)]
            self.t_rstd = [Tl(f"rstd{i}") for i in range(2)]
            self.t_const = Tl("const")
            self.t_negm = Tl("negm")
            self.t_regD = []
            self.regD_retired = {}
            self.mm_rot = 0
            self.ag_pending = []
            self.mm_mod = 4
            self.deferred = None
            self.ev_rot = 0
            self.sq_rot = 0
            self.rs_rot = 0
            self.misc_sems = {}

            self.id32 = self.c32[:, 0, :]
            self.ones32 = self.c32[:, 1, :]
            self.tri32 = self.c32[:, 2, :]
            self.ident = self.cbf[:, 0, :]
            self.ones = self.cbf[:, 1, :]
            self.negutri = self.cbf[:, 2, :]
            self.negones = self.cbf[:, 3, :]

            P.scope = "setup"
            self.emit_setup()
            for li in self.layers:
                kind, j = li % 3, li // 3
                if kind == 0:
                    self.emit_sb_layer(li, j)
                elif kind == 1:
                    self.emit_fox_layer(li, j)
                else:
                    self.emit_mla_layer(li, j)
                if not SKIP_MLP:
                    self.emit_mlp(li)
            fin = self.emit_output()
            with nc.Block() as block:
                P.flush(block, fin)

    def debug_dump(self, name, ap, t, parts=128):
        if not DEBUG:
            return
        n = ap.shape[-1]
        self.dbg_map[name] = (self.dbg_off, n, parts)
        self.dma(self.P.sp, self.dbg[0:parts, self.dbg_off:self.dbg_off + n], ap, reads=[t], writes=[], sem=self.dsem("dbg"))
        self.dbg_off += n

    def dsem(self, name):
        if name not in self.misc_sems:
            self.misc_sems[name] = self.P.new_sem(name)
        return self.misc_sems[name]

    def regD_phase(self):
        ret = dict(self.regD_retired)
        for t in self.t_regD:
            assert t.pend == 0 and t.w != "PEND", t.name
            if t.w is not None:
                s, v = t.w
                if ret.get(s, 0) < v:
                    ret[s] = v
            for s, v in t.r.items():
                if ret.get(s, 0) < v:
                    ret[s] = v
        self.regD_retired = ret
        self.t_regD = []
        self.regD_off = 0

    def regD_alloc(self, nbytes, dt, name, parts=128):
        off = self.regD_off
        assert off % 4 == 0
        self.regD_off += nbytes
        assert self.regD_off <= 45056, (name, self.regD_off)
        ap = self.regD[0:parts, off:off + nbytes].bitcast(dt)
        t = Tl(name)
        t.r = dict(self.regD_retired)
        self.t_regD.append(t)
        return ap, t

    def regD_tile(self, name):
        t = Tl(name)
        t.r = dict(self.regD_retired)
        self.t_regD.append(t)
        return t

    def bank(self, i):
        return self.psum[:, i, :]

    def next_mm_bank(self, lo=0, n=None):
        n = n or self.mm_mod
        b = lo + self.mm_rot % n
        self.mm_rot += 1
        return b

    def evac_engine(self):
        self.ev_rot += 1
        return self.P.act if self.ev_rot % 2 == 0 else self.P.dve

    def mm(self, out_ap, lhsT, rhs, start, stop, reads, writes, signal):
        def fn(e, out_ap=out_ap, lhsT=lhsT, rhs=rhs, start=start, stop=stop):
            return e.matmul(out_ap, lhsT=lhsT, rhs=rhs, start=start, stop=stop)
        return self.P.emit(self.P.pe, fn, reads=reads, writes=writes, signal=signal)

    def act(self, out, in_, func, reads, writes, bias=None, scale=None):
        def fn(e, out=out, in_=in_, func=func, bias=bias, scale=scale):
            kw = {}
            if bias is not None:
                kw["bias"] = bias
            if scale is not None:
                kw["scale"] = scale
            return e.activation(out=out, in_=in_, func=func, **kw)
        return self.P.emit(self.P.act, fn, reads=reads, writes=writes)

    def copy(self, eng, out, in_, reads, writes):
        if eng is self.P.act:
            return self.act(out, in_, AF.Copy, reads, writes)
        def fn(e, out=out, in_=in_):
            return e.tensor_copy(out=out, in_=in_)
        return self.P.emit(eng, fn, reads=reads, writes=writes)

    def tt(self, out, in0, in1, op, reads, writes, eng=None):
        def fn(e, out=out, in0=in0, in1=in1, op=op):
            return e.tensor_tensor(out=out, in0=in0, in1=in1, op=op)
        return self.P.emit(eng or self.P.dve, fn, reads=reads, writes=writes)

    def ts(self, out, in0, s1, s2, op0, op1, reads, writes):
        def fn(e, out=out, in0=in0, s1=s1, s2=s2, op0=op0, op1=op1):
            if op1 is None:
                return e.tensor_scalar(out=out, in0=in0, scalar1=s1, scalar2=None, op0=op0)
            return e.tensor_scalar(out=out, in0=in0, scalar1=s1, scalar2=s2, op0=op0, op1=op1)
        return self.P.emit(self.P.dve, fn, reads=reads, writes=writes)

    def stt(self, out, in0, scalar, in1, op0, op1, reads, writes):
        def fn(e, out=out, in0=in0, scalar=scalar, in1=in1, op0=op0, op1=op1):
            return e.scalar_tensor_tensor(out=out, in0=in0, scalar=scalar, in1=in1, op0=op0, op1=op1)
        return self.P.emit(self.P.dve, fn, reads=reads, writes=writes)

    def dma(self, q, out, in_, reads, writes, sem):
        def fn(e, out=out, in_=in_):
            return e.dma_start(out=out, in_=in_)
        return self.P.emit(q, fn, reads=reads, writes=writes, dsem=sem)

    def load_w(self, views):
        i = self.wb_next % 3
        self.wb_next += 1
        for k, (dst_fn, src) in enumerate(views):
            self.dma(self.P.pool, dst_fn(self.WB[i]), src, reads=[], writes=[self.t_WB[i]], sem=self.wb_sem[i])
        for item in self.ag_pending:
            item[0] -= 1
        self.flush_ag()
        return i

    def allgather(self, src, dst, t_src, t_dst, slot, defer=True):
        sem = self.cc_sems[slot]

        def emit_now(src=src, dst=dst, t_src=t_src, t_dst=t_dst, sem=sem):
            def fn(e):
                return e.collective_compute("AllGather", ALU.bypass, replica_groups=GROUPS,
                                            ins=[src.ap().opt()], outs=[dst.ap().opt()])
            return self.P.emit(self.P.pool, fn, reads=[t_src], writes=[t_dst], dsem=sem, inc=1)
        if not defer:
            return emit_now()
        self.ag_pending.append([2, emit_now])

    def flush_ag(self, force=False):
        keep = []
        for item in self.ag_pending:
            if force or item[0] <= 0:
                item[1]()
            else:
                keep.append(item)
        self.ag_pending = keep

    def emit_setup(self):
        P = self.P
        cs = self.dsem("const")
        for dst, src in ((self.negm[:].rearrange("p a b -> p (a b)"), self.negm_d), (self.cbf[:].rearrange("p a b -> p (a b)"), self.cbf_d),
                         (self.c32[:].rearrange("p a b -> p (a b)"), self.c32_d), (self.gains[:], self.gains_d), (self.cols[:], self.cols_d)):
            self.dma(P.sp, dst, src, reads=[], writes=[self.t_const], sem=cs)
        self.regD_phase()
        xs = []
        for j in range(4):
            ap, t = self.regD_alloc(8192, F32, f"xstage{j}")
            xs.append((ap, t, P.new_sem(f"xs{j}")))
        for tb in range(2):
            for j in range(4):
                ap, t, sem = xs[j]
                r0 = tb * TB + j * 128
                self.dma(P.sp, ap, self.x_in[r0:r0 + 128, :], reads=[], writes=[t], sem=sem)
            for c in range(NCH):
                b = self.next_mm_bank()
                for j in range(4):
                    ap, t, sem = xs[j]
                    def fn(e, o=self.psum[:, b, j * 128:(j + 1) * 128], i=ap[:, c * 128:(c + 1) * 128]):
                        return e.transpose(o, i, self.id32)
                    P.emit(P.pe, fn, reads=[t, self.t_const], writes=[self.t_bank[b]], signal=(j == 3))
                self.copy(self.evac_engine(), self.xT[:, c, tb * TB:(tb + 1) * TB], self.bank(b),
                          reads=[self.t_bank[b]], writes=[self.t_x[c][tb]])

    def emit_norm(self, gi):
        P = self.P
        P.scope = f"norm{gi}"
        for tb in range(2):
            b = self.next_mm_bank()
            for c in range(NCH):
                s = self.sq_rot % 4
                self.sq_rot += 1
                self.act(self.sq[:, s, :], self.xT[:, c, tb * TB:(tb + 1) * TB], AF.Square,
                         reads=[self.t_x[c][tb]], writes=[self.t_sq[s]])
                self.mm(self.bank(b), self.ones, self.sq[:, s, :], c == 0, c == NCH - 1,
                        reads=[self.t_sq[s], self.t_const], writes=[self.t_bank[b]], signal=True)
            self.act(self.rstd[:, tb, :], self.bank(b), AF.Ln, reads=[self.t_bank[b]], writes=[self.t_rstd[tb]],
                     bias=float(EPS), scale=1.0 / D)
            self.act(self.rstd[:, tb, :], self.rstd[:, tb, :], AF.Exp, reads=[self.t_rstd[tb]], writes=[self.t_rstd[tb]],
                     scale=-0.5)
            for c in range(NCH):
                self.stt(self.hT[:, c, tb * TB:(tb + 1) * TB], self.xT[:, c, tb * TB:(tb + 1) * TB],
                         self.gains[:, gi * 16 + c:gi * 16 + c + 1], self.rstd[:, tb, :], ALU.mult, ALU.mult,
                         reads=[self.t_x[c][tb], self.t_rstd[tb], self.t_const], writes=[self.t_h[c][tb]])

    def proj_fm(self, wi, wview, col0, evac, nkc=NCH, src=None, src_t=None, ncols=128, extra_reads=()):
        src = src if src is not None else self.hT
        src_t = src_t if src_t is not None else self.t_h
        for tb in range(2):
            b = self.next_mm_bank()
            for kc in range(nkc):
                self.mm(self.psum[0:ncols, b, :], wview[:, kc, col0:col0 + ncols], src[:, kc, tb * TB:(tb + 1) * TB],
                        kc == 0, kc == nkc - 1, reads=[self.t_WB[wi], src_t[kc][tb]] + list(extra_reads),
                        writes=[self.t_bank[b]], signal=(kc == nkc - 1))
            if self.deferred is not None:
                d, self.deferred = self.deferred, None
                d()
            r = evac(b, tb)
            self.deferred = r if callable(r) else None

    def flush_deferred(self):
        if self.deferred is not None:
            d, self.deferred = self.deferred, None
            d()

    def proj_tm(self, wi, wview_cols, evac, nkc=NCH, src=None, src_t=None, out3=False):
        src = src if src is not None else self.hT
        src_t = src_t if src_t is not None else self.t_h
        for tt in range(8):
            b = self.next_mm_bank()
            for kc in range(nkc):
                o = self.bank(b).rearrange("p (h x) -> p h x", x=128) if out3 else self.bank(b)
                self.mm(o, src[:, kc, tt * 128:(tt + 1) * 128], wview_cols(kc),
                        kc == 0, kc == nkc - 1, reads=[self.t_WB[wi], src_t[kc][tt // 4]],
                        writes=[self.t_bank[b]], signal=(kc == nkc - 1))
            evac(b, tt)

    def w_tile16(self, src_tile, L=8192):
        return [(lambda wb: wb[:, 0:L].rearrange("p (a b) -> p a b", b=2048),
                 src_tile[:, 0:L].rearrange("p (a b) -> p a b", b=2048))]

    def emit_qkv_generic(self, li, w_in, kcol, vcol, qcol, q_evac_scale, qk_norm=None):
        P = self.P
        P.scope = f"qkv{li}"
        self.regD_phase()
        stages = []
        for i in range(3):
            ap, t = self.regD_alloc(8192, BF16, f"stage{i}")
            stages.append((ap, t, self.dsem(f"stage{i}")))
        srot = [0]

        def next_stage():
            s = stages[srot[0] % 3]
            srot[0] += 1
            return s

        for g in range(4):
            wi = self.load_w(self.w_tile16(w_in[kcol // 512 + g]))
            wv = self.WB[wi][:, :].rearrange("p (c n) -> p c n", n=512)
            st_ap, st_t, st_sem = next_stage()
            stv = st_ap.rearrange("p (h t) -> p h t", t=TOK)
            for m in range(4):
                def evac(b, tb, m=m, stv=stv, st_t=st_t):
                    if qk_norm is not None:
                        return self.evac_headnorm(b, stv[:, m, tb * TB:(tb + 1) * TB], st_t, qk_norm[1], 128)
                    self.copy(self.evac_engine(), stv[:, m, tb * TB:(tb + 1) * TB], self.bank(b),
                              reads=[self.t_bank[b]], writes=[st_t])
                self.proj_fm(wi, wv, m * 128, evac)
            self.flush_deferred()
            self.dma(P.sp, self.kag_src[g].ap().rearrange("(h d) t -> d h t", d=128), stv,
                     reads=[st_t], writes=[self.t_kag_src[g]], sem=st_sem)
            self.allgather(self.kag_src[g], self.kag_dst[g], self.t_kag_src[g], self.t_kag_dst[g], 2 * g)
            wi = self.load_w(self.w_tile16(w_in[vcol // 512 + g]))
            wv = self.WB[wi][:, :].rearrange("p (c n) -> p c n", n=512)
            st_ap, st_t, st_sem = next_stage()
            stv = st_ap.rearrange("p (n f) -> p n f", f=512)

            def evac_v(b, tt, stv=stv, st_t=st_t):
                self.copy(self.evac_engine(), stv[:, tt, :], self.bank(b), reads=[self.t_bank[b]], writes=[st_t])
            self.proj_tm(wi, lambda kc, wv=wv: wv[:, kc, :], evac_v)
            vdst = self.vag_src[g].ap().rearrange("r (two f) -> (r two) f", two=2).rearrange("(n p) f -> p n f", p=128)
            self.dma(P.sp, vdst, stv, reads=[st_t], writes=[self.t_vag_src[g]], sem=st_sem)
            self.allgather(self.vag_src[g], self.vag_dst[g], self.t_vag_src[g], self.t_vag_dst[g], 2 * g + 1)
        for g in range(4):
            wi = self.load_w(self.w_tile16(w_in[qcol // 512 + g]))
            wv = self.WB[wi][:, :].rearrange("p (c n) -> p c n", n=512)
            st_ap, st_t, st_sem = next_stage()
            stv = st_ap.rearrange("p (h t) -> p h t", t=TOK)
            for m in range(4):
                def evac(b, tb, m=m, stv=stv, st_t=st_t):
                    o = stv[:, m, tb * TB:(tb + 1) * TB]
                    if qk_norm is not None:
                        return self.evac_headnorm(b, o, st_t, qk_norm[0], 128)
                    self.act(o, self.bank(b), AF.Copy, reads=[self.t_bank[b]], writes=[st_t], scale=q_evac_scale)
                self.proj_fm(wi, wv, m * 128, evac)
            self.flush_deferred()
            qdst = self.q_scr.ap()[g * 512:(g + 1) * 512, :].rearrange("(h d) t -> d h t", d=128)
            self.dma(P.sp, qdst, stv, reads=[st_t], writes=[self.t_q_scr[4 * g + m] for m in range(4)], sem=st_sem)

    def attn_load(self, h, bf, mla=False):
        P = self.P
        g, m = h // 4, h % 4
        q_ap, q_t, sq_ = bf["q"]
        k_ap, k_t, sk_ = bf["k"]
        v_ap, v_t, sv_ = bf["v"]
        HR = 192 if mla else 128
        qsrc = (self.q_scr_mla if mla else self.q_scr).ap()
        self.dma(P.sp, q_ap, qsrc[h * HR:h * HR + 128, :], reads=[self.t_q_scr[h]], writes=[q_t], sem=sq_)
        kd = (self.kag_dst_mla if mla else self.kag_dst)[g]
        ksrc = kd.ap().rearrange("(r x) t -> x r t", r=2)
        self.dma(P.sp, k_ap.rearrange("p (r t) -> p r t", r=2), ksrc[m * HR:m * HR + 128, :, :],
                 reads=[self.t_kag_dst[g]], writes=[k_t], sem=sk_)
        if mla:
            qr_ap, qr_t, sqr_ = bf["qr"]
            kr_ap, kr_t, skr_ = bf["kr"]
            self.dma(P.sp, qr_ap, qsrc[h * HR + 128:(h + 1) * HR, :], reads=[self.t_q_scr[h]], writes=[qr_t], sem=sqr_)
            self.dma(P.sp, kr_ap.rearrange("p (r t) -> p r t", r=2), ksrc[m * HR + 128:(m + 1) * HR, :, :],
                     reads=[self.t_kag_dst[g]], writes=[kr_t], sem=skr_)
        vsrc = self.vag_dst[g].ap().rearrange("(r x) (two f) -> r (x two) f", r=2, two=2)
        vsrc = vsrc.rearrange("r (n p) f -> p r n f", p=128)[:, :, :, m * 128:(m + 1) * 128]
        for r in range(2):
            self.dma(P.sp, v_ap.rearrange("p (r n f) -> p r n f", r=2, f=128)[:, r, :, :], vsrc[:, r, :, :],
                     reads=[self.t_vag_dst[g]], writes=[v_t], sem=sv_)

    def attn_bufs(self, mla=False):
        bufs = []
        for i in range(2):
            d = {}
            d["q"] = self.regD_alloc(2048, BF16, f"q{i}") + (self.dsem(f"aq{i}"),)
            d["k"] = self.regD_alloc(4096, BF16, f"k{i}") + (self.dsem(f"ak{i}"),)
            d["v"] = self.regD_alloc(4096, BF16, f"v{i}") + (self.dsem(f"av{i}"),)
            if mla:
                d["qr"] = self.regD_alloc(2048, BF16, f"qr{i}", parts=64) + (self.dsem(f"aqr{i}"),)
                d["kr"] = self.regD_alloc(4096, BF16, f"kr{i}", parts=64) + (self.dsem(f"akr{i}"),)
            bufs.append(d)
        return bufs

    def key_tiles(self, lq, reverse):
        blocks = [(0, 0), (1, 1)] if lq == 0 else [(0, None), (1, None), (2, 2), (3, 3)]
        res = [(G, i, u) for (G, u) in blocks for i in range(4)]
        return res[::-1] if reverse else res

    def emit_sb_attention(self):
        P = self.P
        self.flush_ag(force=True)
        P.scope = f"sbattn{self.wb_next}"
        self.regD_phase()
        bufs = self.attn_bufs()
        NS = 3
        e_t = [self.regD_alloc(2048, F32, f"e{i}") for i in range(NS)]
        sp_t = [self.regD_alloc(1024, BF16, f"sp{i}") for i in range(NS)]
        p_t = [self.regD_alloc(1024, BF16, f"p{i}") for i in range(NS)]
        sps32 = [self.regD_alloc(2048, F32, f"sps32_{i}") for i in range(2)]
        spsbf = [self.regD_alloc(1024, BF16, f"spsbf_{i}") for i in range(2)]
        self.attn_load(0, bufs[0])
        items = []
        for h in range(NH):
            seqs = [self.key_tiles(1, True), self.key_tiles(0, True)]
            for n in range(16):
                for si, lq in ((0, 1), (1, 0)):
                    if n < len(seqs[si]):
                        G, i, u = seqs[si][n]
                        items.append((h, lq, n, G, i, u, n == len(seqs[si]) - 1))
        rot = [0]
        state = {}

        def s1(it):
            h, lq, n, G, i, u, last = it
            bf = bufs[h % 2]
            q_ap, q_t, k_ap, k_t = bf["q"][0], bf["q"][1], bf["k"][0], bf["k"][1]
            if lq == 1 and n == 3 and h + 1 < NH:
                self.attn_load(h + 1, bufs[(h + 1) % 2])
            slot = rot[0] % NS
            rot[0] += 1
            ba = self.next_mm_bank(0, 4)
            kc0 = GCOL[G] + i * 128
            masked = u is not None
            self.mm(self.bank(ba), k_ap[:, kc0:kc0 + 128], q_ap[:, lq * TB:(lq + 1) * TB], True, not masked,
                    reads=[k_t, q_t], writes=[self.t_bank[ba]], signal=not masked)
            if masked:
                c0 = 384 - 128 * i
                self.mm(self.bank(ba), self.ident, self.negm[:, u, c0:c0 + TB], False, True,
                        reads=[self.t_const], writes=[self.t_bank[ba]], signal=True)
            ea, et = e_t[slot]
            sa, st = sp_t[slot]
            self.act(ea, self.bank(ba), AF.Exp, reads=[self.t_bank[ba]], writes=[et])
            self.act(sa, ea, AF.Ln, reads=[et], writes=[st], bias=1.0)
            state[(h, lq, n)] = (slot, ba)

        def s2(it):
            h, lq, n, G, i, u, last = it
            slot, ba = state[(h, lq, n)]
            sa, st = sp_t[slot]
            pa, pt = p_t[slot]
            s32a, s32t = sps32[lq]
            sbfa, sbft = spsbf[lq]
            self.mm(self.bank(ba), self.negutri, sa, False, n == 0, reads=[st, self.t_const],
                    writes=[self.t_bank[ba]], signal=(n == 0))
            if n > 0:
                self.mm(self.bank(ba), self.negones, sbfa, False, True, reads=[sbft, self.t_const],
                        writes=[self.t_bank[ba]], signal=True)
            self.act(pa, self.bank(ba), AF.Exp, reads=[self.t_bank[ba]], writes=[pt])
            if not last:
                if n == 0:
                    self.copy(P.dve, s32a, sa, reads=[st], writes=[s32t])
                else:
                    self.tt(s32a, s32a, sa, ALU.add, reads=[s32t, st], writes=[s32t])
                self.copy(P.dve, sbfa, s32a, reads=[s32t], writes=[sbft])

        def s3(it):
            h, lq, n, G, i, u, last = it
            slot, ba = state.pop((h, lq, n))
            v_ap, v_t = bufs[h % 2]["v"][0], bufs[h % 2]["v"][1]
            pa, pt = p_t[slot]
            bo = 4 + (h % 2) * 2 + lq
            vt = AG_GLOBAL.index(G) * 4 + i
            self.mm(self.bank(bo), v_ap[:, vt * 128:(vt + 1) * 128], pa, n == 0, last,
                    reads=[v_t, pt], writes=[self.t_bank[bo]], signal=True)
            if last:
                self.copy(self.evac_engine(), self.hT[:, h, lq * TB:(lq + 1) * TB], self.bank(bo),
                          reads=[self.t_bank[bo]], writes=[self.t_h[h][lq]])

        n_it = len(items)
        for k in range(n_it + 2):
            if k < n_it:
                s1(items[k])
            if 0 <= k - 1 < n_it:
                s2(items[k - 1])
            if 0 <= k - 2 < n_it:
                s3(items[k - 2])


    def evac_headnorm(self, b, out_ap, out_t, gcol, nfeat):
        s = self.sq_rot % 4
        self.sq_rot += 1
        self.act(self.sq[:, s, :], self.bank(b), AF.Square, reads=[self.t_bank[b]], writes=[self.t_sq[s]])

        def cont():
            b2 = self.next_mm_bank()
            self.mm(self.bank(b2), self.ones, self.sq[:, s, :], True, True, reads=[self.t_sq[s], self.t_const],
                    writes=[self.t_bank[b2]], signal=True)
            rs = self.rs_rot % 2
            self.rs_rot += 1
            self.act(self.rstd[:, rs, :], self.bank(b2), AF.Ln, reads=[self.t_bank[b2]], writes=[self.t_rstd[rs]],
                     bias=float(EPS), scale=1.0 / nfeat)
            self.act(self.rstd[:, rs, :], self.rstd[:, rs, :], AF.Exp, reads=[self.t_rstd[rs]], writes=[self.t_rstd[rs]],
                     scale=-0.5)
            self.stt(out_ap, self.bank(b), self.cols[:, gcol:gcol + 1], self.rstd[:, rs, :], ALU.mult, ALU.mult,
                     reads=[self.t_bank[b], self.t_rstd[rs], self.t_const], writes=[out_t])
        return cont

    def emit_fox_gate(self, li):
        P = self.P
        P.scope = "foxgate"
        w_in = self.w[(li, "in")]
        self.regD_phase()
        L_ap, L_t = self.regD_alloc(512, F32, "Lown")
        La_ap, La_t = self.regD_alloc(1024, F32, "Lall")
        cq_ap, cq_t = self.regD_alloc(512, F32, "Cq")
        cqT_ap, cqT_t = self.regD_alloc(4096, F32, "CqT", parts=16)
        bfb_ap, bfb_t = self.regD_alloc(512, F32, "bfb")
        selw_ap, selw_t = self.regD_alloc(4096, F32, "selw")
        gs = self.dsem("gate")
        self.dma(P.sp, bfb_ap, self.bfb_d, reads=[], writes=[bfb_t], sem=gs)
        self.dma(P.sp, selw_ap, self.selw_d, reads=[], writes=[selw_t], sem=gs)
        selw = selw_ap.rearrange("p (a b) -> p a b", b=128)
        wi = self.load_w([(lambda wb: wb[:, 0:256], self.w[(li, "gate")])])
        wv = self.WB[wi][:, 0:256].rearrange("p (c n) -> p c n", n=NH)
        b = self.next_mm_bank()
        for tt in range(8):
            for c in range(NCH):
                self.mm(self.psum[:, b, tt * NH:(tt + 1) * NH], self.hT[:, c, tt * 128:(tt + 1) * 128], wv[:, c, :],
                        c == 0, c == NCH - 1, reads=[self.t_WB[wi], self.t_h[c][tt // 4]], writes=[self.t_bank[b]],
                        signal=(c == NCH - 1))
        self.tt(L_ap, self.psum[:, b, 0:128], bfb_ap, ALU.add, reads=[self.t_bank[b], bfb_t], writes=[L_t])
        self.act(L_ap, L_ap, AF.Exp, reads=[L_t], writes=[L_t], scale=-1.0)
        self.act(L_ap, L_ap, AF.Ln, reads=[L_t], writes=[L_t], bias=1.0)
        self.dma(P.sp, self.lf_src.ap().rearrange("(n p) h -> p n h", p=128), L_ap.rearrange("p (n h) -> p n h", h=NH),
                 reads=[L_t], writes=[self.t_lf_src], sem=gs)
        self.allgather(self.lf_src, self.lf_dst, self.t_lf_src, self.t_lf_dst, 8, defer=False)
        self.dma(P.sp, La_ap.rearrange("p (n h) -> p n h", h=NH), self.lf_dst.ap().rearrange("(n p) h -> p n h", p=128),
                 reads=[self.t_lf_dst], writes=[La_t], sem=gs)
        Lall = La_ap.rearrange("p (n h) -> p n h", h=NH)
        Lown = L_ap.rearrange("p (n h) -> p n h", h=NH)
        bk = self.next_mm_bank()
        gt = lambda T: AG_GLOBAL[T // 4] * 4 + T % 4
        for T in range(16):
            terms = [(self.ones32, Tp) for Tp in range(16) if gt(Tp) < gt(T)] + [(self.tri32, T)]
            for k, (lt, Tp) in enumerate(terms):
                self.mm(self.psum[:, bk, T * NH:(T + 1) * NH], lt, Lall[:, Tp, :], k == 0, k == len(terms) - 1,
                        reads=[La_t, self.t_const], writes=[self.t_bank[bk]], signal=(k == len(terms) - 1))
        self.copy(P.dve, self.foxc[:, :], self.psum[:, bk, 0:256], reads=[self.t_bank[bk]], writes=[self.t_foxc])
        self.debug_dump("Lown", L_ap, L_t)
        self.debug_dump("Lall", La_ap, La_t)
        self.debug_dump("Ck", self.foxc[:, :], self.t_foxc)
        bq = self.next_mm_bank()
        for t in range(8):
            l = t // 4
            terms = [(self.ones32, Lown[:, tp, :], L_t) for tp in range(l * 4, t)] + [(self.tri32, Lown[:, t, :], L_t)]
            terms += [(selw[:, jb * 2 + l, :], Lall[:, jb * 4 + i, :], La_t) for jb in range(4) for i in range(4)]
            for k, (lt, rh, rt) in enumerate(terms):
                self.mm(self.psum[:, bq, t * NH:(t + 1) * NH], lt, rh, k == 0, k == len(terms) - 1,
                        reads=[rt, selw_t, self.t_const], writes=[self.t_bank[bq]], signal=(k == len(terms) - 1))
        self.copy(P.dve, cq_ap, self.psum[:, bq, 0:128], reads=[self.t_bank[bq]], writes=[cq_t])
        self.debug_dump("Cq", cq_ap, cq_t)
        for half in range(2):
            bt = self.next_mm_bank()
            for k in range(4):
                t = half * 4 + k
                def fn(e, o=self.psum[0:NH, bt, k * 128:(k + 1) * 128], i=cq_ap[:, t * NH:(t + 1) * NH]):
                    return e.transpose(o, i, self.id32)
                P.emit(P.pe, fn, reads=[cq_t, self.t_const], writes=[self.t_bank[bt]], signal=(k == 3))
            self.act(cqT_ap[:, half * TB:(half + 1) * TB], self.psum[0:NH, bt, :], AF.Copy,
                     reads=[self.t_bank[bt]], writes=[cqT_t], scale=-1.0)
        self.dma(P.sp, self.cqT_scr.ap(), cqT_ap, reads=[cqT_t], writes=[self.t_cqT], sem=gs)

    def emit_softmax_attention(self, fox, scale, mla=False):
        P = self.P
        self.flush_ag(force=True)
        P.scope = f"smattn{self.wb_next}"
        self.regD_phase()
        bufs = self.attn_bufs(mla)
        NS = 3
        p_t = [self.regD_alloc(1024, BF16, f"p{i}") for i in range(NS)]
        rinv = [self.regD_alloc(2048, F32, f"rinv{i}") for i in range(2)]
        if fox:
            tmp_t = [self.regD_alloc(2048, F32, f"tmp{i}") for i in range(NS)]
            cfb = [self.regD_alloc(4096, F32, f"cfb{i}") + (self.dsem(f"cfb{i}"),) for i in range(2)]

        def load(h):
            self.attn_load(h, bufs[h % 2], mla)
            if fox:
                ap, t, sem = cfb[h % 2]
                self.dma(P.sp, ap, self.cqT_scr.ap()[h:h + 1, :].partition_broadcast(128), reads=[self.t_cqT],
                         writes=[t], sem=sem)
        load(0)
        items = []
        for h in range(NH):
            seqs = [self.key_tiles(1, False), self.key_tiles(0, False)]
            for n in range(16):
                for si, lq in ((0, 1), (1, 0)):
                    if n < len(seqs[si]):
                        G, i, u = seqs[si][n]
                        items.append((h, lq, n, G, i, u, n == len(seqs[si]) - 1))
        rot = [0]
        state = {}

        def s1(it):
            h, lq, n, G, i, u, last = it
            bf = bufs[h % 2]
            if lq == 1 and n == 3 and h + 1 < NH:
                load(h + 1)
            slot = rot[0] % NS
            rot[0] += 1
            ba = self.next_mm_bank(0, 4)
            kc0 = GCOL[G] + i * 128
            masked = u is not None
            qs = slice(lq * TB, (lq + 1) * TB)
            self.mm(self.bank(ba), bf["k"][0][:, kc0:kc0 + 128], bf["q"][0][:, qs], True, not (masked or mla),
                    reads=[bf["k"][1], bf["q"][1]], writes=[self.t_bank[ba]], signal=not (masked or mla))
            if mla:
                self.mm(self.bank(ba), bf["kr"][0][:, kc0:kc0 + 128], bf["qr"][0][:, qs], False, not masked,
                        reads=[bf["kr"][1], bf["qr"][1]], writes=[self.t_bank[ba]], signal=not masked)
            if masked:
                c0 = 385 - 128 * i
                self.mm(self.bank(ba), self.ident, self.negm[:, u, c0:c0 + TB], False, True,
                        reads=[self.t_const], writes=[self.t_bank[ba]], signal=True)
            pa, pt = p_t[slot]
            if fox:
                ta, tt_ = tmp_t[slot]
                ca, ct, _ = cfb[h % 2]
                self.stt(ta, self.bank(ba), float(scale), ca[:, qs], ALU.mult, ALU.add,
                         reads=[self.t_bank[ba], ct], writes=[tt_])
                T_ag = AG_GLOBAL.index(G) * 4 + i
                self.act(pa, ta, AF.Exp, reads=[tt_, self.t_foxc], writes=[pt],
                         bias=self.foxc[:, T_ag * NH + h:T_ag * NH + h + 1])
            else:
                self.act(pa, self.bank(ba), AF.Exp, reads=[self.t_bank[ba]], writes=[pt], scale=float(scale))
            state[(h, lq, n)] = slot

        def s2(it):
            h, lq, n, G, i, u, last = it
            slot = state.pop((h, lq, n))
            bf = bufs[h % 2]
            pa, pt = p_t[slot]
            bo = 4 + lq * 2
            bs = bo + 1
            vt = AG_GLOBAL.index(G) * 4 + i
            self.mm(self.bank(bo), bf["v"][0][:, vt * 128:(vt + 1) * 128], pa, n == 0, last,
                    reads=[bf["v"][1], pt], writes=[self.t_bank[bo]], signal=False)
            self.mm(self.bank(bs), self.ones, pa, n == 0, last,
                    reads=[pt, self.t_const], writes=[self.t_bank[bs]], signal=True)
            if last:
                ra, rt = rinv[lq]
                self.act(ra, self.bank(bs), AF.Ln, reads=[self.t_bank[bs]], writes=[rt])
                self.act(ra, ra, AF.Exp, reads=[rt], writes=[rt], scale=-1.0)
                self.tt(self.hT[:, h, lq * TB:(lq + 1) * TB], self.bank(bo), ra, ALU.mult,
                        reads=[self.t_bank[bo], rt], writes=[self.t_h[h][lq]])
                if DEBUG and h == DBG_HEAD and lq == 1:
                    d1, d1t = self.regD_alloc(2048, F32, "dbg1")
                    self.copy(P.dve, d1, self.hT[:, h, lq * TB:(lq + 1) * TB], reads=[self.t_h[h][lq]], writes=[d1t])
                    self.debug_dump("oT", d1, d1t)
                    self.copy(P.dve, d1, ra, reads=[rt, d1t], writes=[d1t])
                    self.debug_dump("rinv", d1, d1t)
                    self.copy(P.dve, d1, bf["v"][0][:, 4 * 128:8 * 128], reads=[bf["v"][1], d1t], writes=[d1t])
                    self.debug_dump("v47", d1, d1t)

        n_it = len(items)
        for k in range(n_it + 1):
            if k < n_it:
                s1(items[k])
            if 0 <= k - 1 < n_it:
                s2(items[k - 1])

    def emit_fox_layer(self, li, j):
        w_in = self.w[(li, "in")]
        self.emit_norm(li)
        self.emit_fox_gate(li)
        self.emit_qkv_generic(li, w_in, kcol=D, vcol=2 * D, qcol=0, q_evac_scale=1.0, qk_norm=(0, 1))
        self.emit_softmax_attention(True, 1.0 / math.sqrt(128.0))
        self.emit_outproj(self.w[(li, "out")])

    def emit_outproj(self, w_out):
        self.P.scope = f"outproj{self.wb_next}"
        for t in range(4):
            wi = self.load_w(self.w_tile16(w_out[t]))
            wv = self.WB[wi][:, :].rearrange("p (c n) -> p c n", n=512)
            for m in range(4):
                c = 4 * t + m
                def evac(b, tb, c=c):
                    xs = self.xT[:, c, tb * TB:(tb + 1) * TB]
                    self.tt(xs, xs, self.bank(b), ALU.add, reads=[self.t_x[c][tb], self.t_bank[b]], writes=[self.t_x[c][tb]])
                self.proj_fm(wi, wv, m * 128, evac)

    def emit_sb_layer(self, li, j):
        w_in = self.w[(li, "in")]
        self.emit_norm(li)
        self.emit_qkv_generic(li, w_in, kcol=D, vcol=2 * D, qcol=0, q_evac_scale=1.0 / math.sqrt(128.0))
        self.emit_sb_attention()
        self.emit_outproj(self.w[(li, "out")])


    def regD_subphase(self, keep, off):
        ret = dict(self.regD_retired)
        rest = []
        for t in self.t_regD:
            if any(t is k for k in keep):
                rest.append(t)
                continue
            assert t.pend == 0 and t.w != "PEND", t.name
            if t.w is not None:
                s, v = t.w
                if ret.get(s, 0) < v:
                    ret[s] = v
            for s, v in t.r.items():
                if ret.get(s, 0) < v:
                    ret[s] = v
        self.regD_retired = ret
        self.t_regD = rest
        self.regD_off = off

    def emit_sincos(self, ang, ang_t, out_ap, out_t, shift, tmp, tmp_t, nn, nn_t):
        MAGIC = 12582912.0
        TWO_PI_HI = 6.28125
        TWO_PI_LO = 2.0 * math.pi - 6.28125
        self.ts(tmp, ang, 1.0 / (2.0 * math.pi), shift / (2.0 * math.pi), ALU.mult, ALU.add, reads=[ang_t], writes=[tmp_t])
        self.ts(nn, tmp, MAGIC, None, ALU.add, None, reads=[tmp_t], writes=[nn_t])
        self.ts(nn, nn, -MAGIC, None, ALU.add, None, reads=[nn_t], writes=[nn_t])
        self.stt(tmp, nn, -TWO_PI_HI, ang, ALU.mult, ALU.add, reads=[nn_t, ang_t], writes=[tmp_t])
        self.stt(tmp, nn, -TWO_PI_LO, tmp, ALU.mult, ALU.add, reads=[nn_t, tmp_t], writes=[tmp_t])
        self.ts(tmp, tmp, float(shift), math.pi, ALU.add, ALU.min, reads=[tmp_t], writes=[tmp_t])
        self.ts(tmp, tmp, -math.pi, None, ALU.max, None, reads=[tmp_t], writes=[tmp_t])
        self.act(out_ap, tmp, AF.Sin, reads=[tmp_t], writes=[out_t])

    def emit_mla_layer(self, li, j):
        P = self.P
        w_in, w_uq, w_ukv = self.w[(li, "in")], self.w[(li, "uq")], self.w[(li, "ukv")]
        self.emit_norm(li)
        P.scope = "mla_proj"
        self.regD_phase()
        cos_ap, cos_t = self.regD_alloc(4096, F32, "cos", parts=64)
        sin_ap, sin_t = self.regD_alloc(4096, F32, "sinS", parts=64)
        krR_ap, krR_t = self.regD_alloc(4096, F32, "krR", parts=64)
        krsq_ap, krsq_t = self.regD_alloc(2048, BF16, "krsq", parts=64)
        keep = [cos_t, sin_t, krR_t, krsq_t]
        keep_off = self.regD_off
        posi_ap, posi_t = self.regD_alloc(4096, I32, "posi", parts=64)
        ang_ap, ang_t = self.regD_alloc(4096, F32, "ang", parts=64)
        tmp_ap, tmp_t = self.regD_alloc(4096, F32, "sctmp", parts=64)
        nn_ap, nn_t = self.regD_alloc(4096, F32, "scn", parts=64)
        ms = self.dsem("mla")
        self.dma(P.sp, posi_ap, self.pos_d.partition_broadcast(64), reads=[], writes=[posi_t], sem=ms)
        self.copy(P.dve, ang_ap, posi_ap, reads=[posi_t], writes=[ang_t])
        self.ts(ang_ap, ang_ap, self.cols[0:64, 18:19], None, ALU.mult, None, reads=[ang_t, self.t_const], writes=[ang_t])
        self.emit_sincos(ang_ap, ang_t, sin_ap, sin_t, 0.0, tmp_ap, tmp_t, nn_ap, nn_t)
        self.ts(sin_ap, sin_ap, self.cols[0:64, 19:20], None, ALU.mult, None, reads=[sin_t, self.t_const], writes=[sin_t])
        self.emit_sincos(ang_ap, ang_t, cos_ap, cos_t, math.pi / 2.0, tmp_ap, tmp_t, nn_ap, nn_t)
        if MLA_STOP == 1:
            return
        self.regD_subphase(keep, keep_off)
        st32_ap, st32_t0 = self.regD_alloc(20480, F32, "st32")
        st32 = st32_ap.rearrange("p (c t) -> p c t", t=TB)
        st32_t = [st32_t0] + [self.regD_tile(f"st32_{c}") for c in range(1, 10)]
        kr32_ap, kr32_t = self.regD_alloc(2048, F32, "kr32", parts=64)
        krs32_ap, krs32_t = self.regD_alloc(2048, F32, "krs32", parts=64)
        BQ, BKV = 6, 7
        for tb in range(2):
            tbs = slice(tb * TB, (tb + 1) * TB)
            for T in range(3):
                if T < 2:
                    wi = self.load_w(self.w_tile16(w_in[T]))
                    wv = self.WB[wi][:, :].rearrange("p (c n) -> p c n", n=512)
                    chunks = [(T * 4 + m, m * 128, 128) for m in range(4)]
                else:
                    v3 = lambda wb: wb[:, 0:16 * 384].rearrange("p (c n) -> p c n", n=384)
                    wi = self.load_w(self.w_tile16(w_in[2], L=6144))
                    wv = v3(self.WB[wi])
                    chunks = [(8, 0, 128), (9, 128, 128), ("kr", 256, 64), ("krs", 320, 64)]
                for ch, col0, ncols in chunks:
                    self.mla_down_chunk(wi, wv, ch, col0, ncols, tb, st32, st32_t, kr32_ap, kr32_t, krs32_ap, krs32_t, BQ, BKV)
            self.flush_deferred()
            for rs, bnk, n in ((0, BQ, 768), (1, BKV, 512)):
                self.act(self.rstd[:, rs, :], self.bank(bnk), AF.Ln, reads=[self.t_bank[bnk]], writes=[self.t_rstd[rs]],
                         bias=float(EPS), scale=1.0 / n)
                self.act(self.rstd[:, rs, :], self.rstd[:, rs, :], AF.Exp, reads=[self.t_rstd[rs]], writes=[self.t_rstd[rs]],
                         scale=-0.5)
            for ch in range(10):
                rs = 0 if ch < 6 else 1
                self.stt(self.hT[:, ch, tbs], st32[:, ch, :], self.cols[:, 8 + ch:9 + ch], self.rstd[:, rs, :], ALU.mult, ALU.mult,
                         reads=[st32_t[ch], self.t_rstd[rs], self.t_const], writes=[self.t_h[ch][tb]])
            self.tt(kr32_ap, kr32_ap, cos_ap[:, tbs], ALU.mult, reads=[kr32_t, cos_t], writes=[kr32_t])
            self.tt(krs32_ap, krs32_ap, sin_ap[:, tbs], ALU.mult, reads=[krs32_t, sin_t], writes=[krs32_t])
            self.tt(krR_ap[:, tbs], kr32_ap, krs32_ap, ALU.add, reads=[kr32_t, krs32_t], writes=[krR_t])
            self.act(krsq_ap[:, tbs], krR_ap[:, tbs], AF.Square, reads=[krR_t], writes=[krsq_t])
        if MLA_STOP == 2:
            return
        self.regD_subphase(keep, keep_off)
        kst_ap, kst_t = self.regD_alloc(8192, BF16, "kstage")
        ksr_ap, ksr_t = self.regD_alloc(8192, BF16, "kstage_r", parts=64)
        vst_ap, vst_t = self.regD_alloc(8192, BF16, "vstage")
        tA_ap, tA_t = self.regD_alloc(2048, F32, "ropeA", parts=64)
        tB_ap, tB_t = self.regD_alloc(2048, F32, "ropeB", parts=64)
        kst = kst_ap.rearrange("p (h t) -> p h t", t=TOK)
        ksr = ksr_ap.rearrange("p (h t) -> p h t", t=TOK)
        vst = vst_ap.rearrange("p (n f) -> p n f", f=512)
        kvsrc = self.hT[:, 6:10, :]
        kvsrc_t = self.t_h[6:10]
        ones64 = self.cbf[0:64, 1, :]
        ks_sem, vs_sem = self.dsem("mla_ks"), self.dsem("mla_vs")

        def headnorm_tail(b_n, rope_ap, rope_t, sq_r_ap, sq_r_t, gn_col, gr_col, out_n, out_r, out_t_n, out_t_r, s):
            b2 = self.next_mm_bank()
            self.mm(self.bank(b2), self.ones, self.sq[:, s, :], True, False, reads=[self.t_sq[s], self.t_const],
                    writes=[self.t_bank[b2]], signal=False)
            self.mm(self.bank(b2), ones64, sq_r_ap, False, True, reads=[sq_r_t, self.t_const],
                    writes=[self.t_bank[b2]], signal=True)
            rs = self.rs_rot % 2
            self.rs_rot += 1
            self.act(self.rstd[:, rs, :], self.bank(b2), AF.Ln, reads=[self.t_bank[b2]], writes=[self.t_rstd[rs]],
                     bias=float(EPS), scale=1.0 / 192.0)
            self.act(self.rstd[:, rs, :], self.rstd[:, rs, :], AF.Exp, reads=[self.t_rstd[rs]], writes=[self.t_rstd[rs]],
                     scale=-0.5)
            self.stt(out_n, self.bank(b_n), self.cols[:, gn_col:gn_col + 1], self.rstd[:, rs, :], ALU.mult, ALU.mult,
                     reads=[self.t_bank[b_n], self.t_rstd[rs], self.t_const], writes=[out_t_n])
            self.stt(out_r, rope_ap, self.cols[0:64, gr_col:gr_col + 1], self.rstd[0:64, rs, :], ALU.mult, ALU.mult,
                     reads=[rope_t, self.t_rstd[rs], self.t_const], writes=[out_t_r])

        for g in range(4):
            src = w_ukv[:, g * 1024:(g + 1) * 1024].rearrange("(c p) n -> p c n", p=128)
            wi = self.load_w([(lambda wb: wb[:, 0:4096].rearrange("p (c n) -> p c n", n=1024), src)])
            wv = self.WB[wi][:, 0:4096].rearrange("p (c n) -> p c n", n=1024)
            for m in range(4):
                def evac(b, tb, m=m):
                    tbs = slice(tb * TB, (tb + 1) * TB)
                    s = self.sq_rot % 4
                    self.sq_rot += 1
                    self.act(self.sq[:, s, :], self.bank(b), AF.Square, reads=[self.t_bank[b]], writes=[self.t_sq[s]])
                    return lambda: headnorm_tail(b, krR_ap[:, tbs], krR_t, krsq_ap[:, tbs], krsq_t, 4, 5,
                                                 kst[:, m, tbs], ksr[:, m, tbs], kst_t, ksr_t, s)
                self.proj_fm(wi, wv, m * 256, evac, nkc=4, src=kvsrc, src_t=kvsrc_t)
            self.flush_deferred()
            kd = self.kag_src_mla[g].ap().rearrange("(h x) t -> x h t", x=192)
            self.dma(P.sp, kd[0:128, :, :], kst, reads=[kst_t], writes=[self.t_kag_src[g]], sem=ks_sem)
            self.dma(P.sp, kd[128:192, :, :], ksr, reads=[ksr_t], writes=[self.t_kag_src[g]], sem=ks_sem)
            self.allgather(self.kag_src_mla[g], self.kag_dst_mla[g], self.t_kag_src[g], self.t_kag_dst[g], 2 * g)

            def evac_v(b, tt):
                self.copy(self.evac_engine(), vst[:, tt, :], self.bank(b), reads=[self.t_bank[b]], writes=[vst_t])
            self.proj_tm(wi, lambda kc, wv=wv: wv[:, kc, :].rearrange("p (h x) -> p h x", x=256)[:, :, 128:256], evac_v,
                         nkc=4, src=kvsrc, src_t=kvsrc_t, out3=True)
            vdst = self.vag_src[g].ap().rearrange("r (two f) -> (r two) f", two=2).rearrange("(n p) f -> p n f", p=128)
            self.dma(P.sp, vdst, vst, reads=[vst_t], writes=[self.t_vag_src[g]], sem=vs_sem)
            self.allgather(self.vag_src[g], self.vag_dst[g], self.t_vag_src[g], self.t_vag_dst[g], 2 * g + 1)

        self.mm_mod = 8
        for g in range(4):
            v4 = lambda wb: wb[:, 0:6144].rearrange("p (c n) -> p c n", n=1024)
            wi = self.load_w(self.w_tile16(w_uq[g], L=6144))
            wv = v4(self.WB[wi])
            pending = None
            for m in range(4):
                for tb in range(2):
                    tbs = slice(tb * TB, (tb + 1) * TB)
                    banks = []
                    for col0, ncols in ((m * 256, 128), (m * 256 + 128, 64), (m * 256 + 192, 64)):
                        b = self.next_mm_bank()
                        banks.append(b)
                        for kc in range(6):
                            self.mm(self.psum[0:ncols, b, :], wv[:, kc, col0:col0 + ncols], self.hT[:, kc, tbs],
                                    kc == 0, kc == 5, reads=[self.t_WB[wi], self.t_h[kc][tb]],
                                    writes=[self.t_bank[b]], signal=(kc == 5))
                    if pending is not None:
                        pending()
                    bn, br, bs = banks
                    self.tt(tA_ap, self.psum[0:64, br, :], cos_ap[:, tbs], ALU.mult, reads=[self.t_bank[br], cos_t], writes=[tA_t])
                    self.tt(tB_ap, self.psum[0:64, bs, :], sin_ap[:, tbs], ALU.mult, reads=[self.t_bank[bs], sin_t], writes=[tB_t])
                    self.tt(tA_ap, tA_ap, tB_ap, ALU.add, reads=[tA_t, tB_t], writes=[tA_t])
                    s = self.sq_rot % 4
                    self.sq_rot += 1
                    self.act(self.sq[:, s, :], self.bank(bn), AF.Square, reads=[self.t_bank[bn]], writes=[self.t_sq[s]])
                    s2 = self.sq_rot % 4
                    self.sq_rot += 1
                    self.act(self.sq[0:64, s2, :], tA_ap, AF.Square, reads=[tA_t], writes=[self.t_sq[s2]])
                    pending = (lambda bn=bn, s=s, s2=s2, m=m, tbs=tbs:
                               headnorm_tail(bn, tA_ap, tA_t, self.sq[0:64, s2, :], self.t_sq[s2], 2, 3,
                                             kst[:, m, tbs], ksr[:, m, tbs], kst_t, ksr_t, s))
            pending()
            qd = self.q_scr_mla.ap()[g * 768:(g + 1) * 768, :].rearrange("(h x) t -> x h t", x=192)
            self.dma(P.sp, qd[0:128, :, :], kst, reads=[kst_t], writes=[self.t_q_scr[4 * g + m] for m in range(4)], sem=ks_sem)
            self.dma(P.sp, qd[128:192, :, :], ksr, reads=[ksr_t], writes=[self.t_q_scr[4 * g + m] for m in range(4)], sem=ks_sem)
        self.mm_mod = 4
        if MLA_STOP == 3:
            return
        self.emit_softmax_attention(False, 1.0 / math.sqrt(192.0), mla=True)
        if MLA_STOP == 4:
            return
        self.emit_outproj(self.w[(li, "out")])

    def mla_down_chunk(self, wi, wv, ch, col0, ncols, tb, st32, st32_t, kr32_ap, kr32_t, krs32_ap, krs32_t, BQ, BKV):
        P = self.P
        b = self.next_mm_bank()
        tbs = slice(tb * TB, (tb + 1) * TB)
        for kc in range(NCH):
            self.mm(self.psum[0:ncols, b, :], wv[:, kc, col0:col0 + ncols], self.hT[:, kc, tbs],
                    kc == 0, kc == NCH - 1, reads=[self.t_WB[wi], self.t_h[kc][tb]],
                    writes=[self.t_bank[b]], signal=(kc == NCH - 1))
        if self.deferred is not None:
            d, self.deferred = self.deferred, None
            d()
        if ch == "kr":
            self.copy(P.dve, kr32_ap, self.psum[0:64, b, :], reads=[self.t_bank[b]], writes=[kr32_t])
            return
        if ch == "krs":
            self.copy(P.dve, krs32_ap, self.psum[0:64, b, :], reads=[self.t_bank[b]], writes=[krs32_t])
            return
        s = self.sq_rot % 4
        self.sq_rot += 1
        self.act(self.sq[:, s, :], self.bank(b), AF.Square, reads=[self.t_bank[b]], writes=[self.t_sq[s]])
        self.copy(P.dve, st32[:, ch, :], self.bank(b), reads=[self.t_bank[b], self.t_sq[s]], writes=[st32_t[ch]])
        acc = BQ if ch < 6 else BKV
        first = ch in (0, 6)
        lastc = ch in (5, 9)

        def cont():
            self.mm(self.bank(acc), self.ones, self.sq[:, s, :], first, lastc, reads=[self.t_sq[s], self.t_const],
                    writes=[self.t_bank[acc]], signal=True)
        self.deferred = cont

    def emit_mlp(self, li):
        P = self.P
        self.emit_norm(4 + li)
        P.scope = f"mlp{li}"
        self.regD_phase()
        aT = [self.regD_alloc(8192, BF16, f"aT{i}") for i in range(2)]
        sqt = [self.regD_alloc(2048, F32, f"sqt{i}") for i in range(2)]
        w1 = self.w[(li, "w1")]
        w2 = self.w[(li, "w2")]
        NG = DFF // 512
        rot = [0]

        def ffn1(g):
            wi = self.load_w(self.w_tile16(w1[g]))
            wv = self.WB[wi][:, :].rearrange("p (c n) -> p c n", n=512)
            a_ap, a_t = aT[g % 2]
            av = a_ap.rearrange("p (m t) -> p m t", t=TOK)
            for m in range(4):
                def evac(b, tb, m=m):
                    sa, st = sqt[rot[0] % 2]
                    rot[0] += 1
                    self.act(sa, self.bank(b), AF.Square, reads=[self.t_bank[b]], writes=[st])
                    self.stt(av[:, m, tb * TB:(tb + 1) * TB], self.bank(b), 0.0, sa, ALU.is_gt, ALU.mult,
                             reads=[self.t_bank[b], st], writes=[a_t])
                self.proj_fm(wi, wv, m * 128, evac)

        def ffn2(g):
            src = w2[g * 512:(g + 1) * 512, :].rearrange("(m p) n -> p m n", p=128)
            wi = self.load_w([(lambda wb: wb[:, :].rearrange("p (m n) -> p m n", n=D), src)])
            wv = self.WB[wi][:, :].rearrange("p (m n) -> p m n", n=D)
            a_ap, a_t = aT[g % 2]
            av = a_ap.rearrange("p (m t) -> p m t", t=TOK)
            for c in range(NCH):
                for tb in range(2):
                    b = 4 + self.next_mm_bank(0, 4)
                    for m in range(4):
                        self.mm(self.bank(b), wv[:, m, c * 128:(c + 1) * 128], av[:, m, tb * TB:(tb + 1) * TB],
                                m == 0, m == 3, reads=[self.t_WB[wi], a_t], writes=[self.t_bank[b]], signal=(m == 3))
                    xs = self.xT[:, c, tb * TB:(tb + 1) * TB]
                    self.tt(xs, xs, self.bank(b), ALU.add, reads=[self.t_x[c][tb], self.t_bank[b]], writes=[self.t_x[c][tb]])

        ffn1(0)
        for g in range(NG):
            if g + 1 < NG:
                ffn1(g + 1)
            ffn2(g)

    def emit_output(self):
        P = self.P
        P.scope = "output"
        self.regD_phase()
        osem = self.dsem("out")
        st = [self.regD_alloc(8192, F32, f"ostage{i}") for i in range(2)]
        for tt in range(8):
            o_ap, o_t = st[tt % 2]
            for c4 in range(4):
                b = self.next_mm_bank()
                for k in range(4):
                    c = c4 * 4 + k
                    def fn(e, o=self.psum[:, b, k * 128:(k + 1) * 128], i=self.xT[:, c, tt * 128:(tt + 1) * 128]):
                        return e.transpose(o, i, self.id32)
                    P.emit(P.pe, fn, reads=[self.t_x[c][tt // 4], self.t_const], writes=[self.t_bank[b]], signal=(k == 3))
                self.copy(self.evac_engine(), o_ap[:, c4 * 512:(c4 + 1) * 512], self.bank(b),
                          reads=[self.t_bank[b]], writes=[o_t])
            self.dma(P.sp, self.out[tt * 128:(tt + 1) * 128, :], o_ap, reads=[o_t], writes=[], sem=osem)
        fin = [(osem, osem.n)]
        if DEBUG and "dbg" in self.misc_sems:
            fin.append((self.misc_sems["dbg"], self.misc_sems["dbg"].n))
        return fin


def _bf16(a):
    return np.asarray(a, dtype=np.float32).astype(ml_dtypes.bfloat16)


def _negmask(rank):
    types = [["diag", "zero", "full", "diag"], ["full", "diag", "diag", "zero"]][rank]
    out = np.zeros((128, 4, MASKW), np.float32)
    k = np.arange(128)[:, None]
    for u, ty in enumerate(types):
        if ty == "zero":
            out[:, u, :] = NEG
        elif ty == "diag":
            out[:, u, 0] = NEG
            s = np.arange(MASKW - 1)[None, :]
            m = np.where(s < 384, NEG, np.where(s < 512, np.where(k > (s - 384), NEG, 0.0), 0.0))
            out[:, u, 1:] = m
    return _bf16(out.reshape(128, 4 * MASKW))


def _consts_bf():
    j = np.arange(128)[:, None]
    s = np.arange(128)[None, :]
    ident = (j == s).astype(np.float32)
    ones = np.ones((128, 128), np.float32)
    negutri = np.where(j >= s, -1.0, 0.0).astype(np.float32)
    negones = -ones
    return _bf16(np.concatenate([ident, ones, negutri, negones], axis=1))


def _col_layout(v):
    v = np.asarray(v, np.float32)
    return np.ascontiguousarray(v.reshape(-1, 128).T)


_BUILD_CACHE = {}


def _get_nc(layers):
    key = tuple(layers)
    if key not in _BUILD_CACHE:
        _BUILD_CACHE[key] = Builder(list(layers))
    return _BUILD_CACHE[key].nc


def _own_rows(rank):
    blocks = [0, 3] if rank == 0 else [1, 2]
    return np.concatenate([np.arange(b * TB, (b + 1) * TB) for b in blocks])


def _selw(rank):
    mine = [0, 3] if rank == 0 else [1, 2]
    out = np.zeros((128, 8, 128), np.float32)
    for jb in range(4):
        for l in range(2):
            if AG_GLOBAL[jb] < mine[l]:
                out[:, jb * 2 + l, :] = 1.0
    return out.reshape(128, 8 * 128)


def _tile_w(W, ncols=512):
    W = np.asarray(W, np.float32)
    K, N = W.shape
    return np.ascontiguousarray(W.reshape(K // 128, 128, N // ncols, ncols).transpose(2, 1, 0, 3).reshape(N // ncols, 128, -1))


def run_layers(layers, x, inputs):
    nc = _get_nc(layers)
    gains = np.zeros((128, 128), np.float32)
    for n in range(4):
        gains[:, n * 16:(n + 1) * 16] = _col_layout(inputs["mix_norm"][n])
        gains[:, (4 + n) * 16:(5 + n) * 16] = _col_layout(inputs["mlp_norm"][n])
    cols = np.zeros((128, 64), np.float32)
    cols[:, 0] = inputs["fox_q_gain"][0]
    cols[:, 1] = inputs["fox_k_gain"][0]
    cols[:, 2] = inputs["mla_q_gain"][0][:128]
    cols[:64, 3] = inputs["mla_q_gain"][0][128:]
    cols[:, 4] = inputs["mla_k_gain"][0][:128]
    cols[:64, 5] = inputs["mla_k_gain"][0][128:]
    cols[:, 8:14] = _col_layout(inputs["mla_q_norm"][0])
    cols[:, 14:18] = _col_layout(inputs["mla_kv_norm"][0])
    half = 32
    invf = (10000.0 ** (-np.arange(0, half, dtype=np.float32) * 2.0 / 64.0)).astype(np.float32)
    cols[:64, 18] = np.concatenate([invf, invf])
    cols[:64, 19] = np.concatenate([-np.ones(half, np.float32), np.ones(half, np.float32)])
    cols[:, 20] = -math.pi
    bfb = np.ascontiguousarray(np.broadcast_to(np.tile(np.asarray(inputs["fox_b_f"][0], np.float32), 8)[None, :], (128, 128)))
    j = np.arange(128)[:, None]
    f = np.arange(128)[None, :]
    c32 = np.concatenate([np.eye(128, dtype=np.float32), np.ones((128, 128), np.float32),
                          (j <= f).astype(np.float32)], axis=1)
    cbf = _consts_bf()
    shared = {}
    for li in layers:
        kind, jj = li % 3, li // 3
        if kind == 0:
            shared[f"w{li}_in"] = _tile_w(inputs["sb_w_in"][jj])
            shared[f"w{li}_out"] = _tile_w(inputs["sb_w_out"][jj])
        elif kind == 1:
            wf = np.asarray(inputs["fox_w_in"][jj], np.float32)
            shared[f"w{li}_in"] = _tile_w(wf[:, :3 * D])
            shared[f"w{li}_gate"] = np.ascontiguousarray(wf[:, 3 * D:].reshape(16, 128, NH).transpose(1, 0, 2).reshape(128, 16 * NH))
            shared[f"w{li}_out"] = _tile_w(inputs["fox_w_out"][jj])
        else:
            wm = np.asarray(inputs["mla_w_in"][jj], np.float32)
            t01 = _tile_w(wm[:, :1024])
            t2 = _tile_w(np.concatenate([wm[:, 1024:1344], wm[:, 1312:1344], wm[:, 1280:1312]], axis=1), ncols=384)[0]
            t2 = np.concatenate([t2, np.zeros((128, 8192 - t2.shape[1]), np.float32)], axis=1)
            shared[f"w{li}_in"] = np.ascontiguousarray(np.concatenate([t01, t2[None]], axis=0))
            wq = np.asarray(inputs["mla_w_uq"][jj], np.float32).reshape(768, NH, 192)
            wq = np.concatenate([wq, wq[:, :, 160:192], wq[:, :, 128:160]], axis=2)
            shared[f"w{li}_uq"] = _tile_w(wq.reshape(768, NH * 256), ncols=1024)
            shared[f"w{li}_ukv"] = np.ascontiguousarray(inputs["mla_w_ukv"][jj])
            shared[f"w{li}_out"] = _tile_w(inputs["mla_w_out"][jj])
        if not SKIP_MLP:
            shared[f"w{li}_w1"] = _tile_w(inputs["mlp_w1"][li])
            shared[f"w{li}_w2"] = np.ascontiguousarray(inputs["mlp_w2"][li])
    pos = np.asarray(inputs["positions"])
    in_maps = []
    for c in range(8):
        b, r = c // 2, c % 2
        m = {
            "x_in": np.ascontiguousarray(x[b][_own_rows(r)]),
            "negm": _negmask(r),
            "cbf": cbf,
            "c32": c32,
            "bfb": bfb,
            "selw": _selw(r),
            "gains": gains,
            "cols": cols,
        }
        if any(l % 3 == 2 for l in layers):
            m["pos"] = np.ascontiguousarray(pos[b][_own_rows(r)][None, :].astype(np.int32))
        m.update(shared)
        in_maps.append(m)
    res = run_bass_kernel_spmd(nc, in_maps, core_ids=list(range(8)))
    out = np.empty((4, 2048, D), np.float32)
    for c in range(8):
        b, r = c // 2, c % 2
        out[b][_own_rows(r)] = np.asarray(res.results[c]["out"])
    if DEBUG:
        global LAST_DBG
        LAST_DBG = [np.asarray(res.results[c]["dbg"]) for c in range(8)]
    return out


def kernel(**inputs):
    inputs = {k: np.asarray(v) for k, v in inputs.items()}
    x = np.asarray(inputs["x"], np.float32)
    return run_layers([0, 1, 2, 3], x, inputs)
```

```python
import math
import numpy as np
import ml_dtypes
import concourse.bass as bass
import concourse.mybir as mybir
from concourse.bass_utils import run_bass_kernel_spmd

F32 = mybir.dt.float32
BF16 = mybir.dt.bfloat16
I32 = mybir.dt.int32
U8 = mybir.dt.uint8
AF = mybir.ActivationFunctionType
ALU = mybir.AluOpType

D = 2048
NCH = 16
TOK = 1024
TB = 512
DFF = 8192
NH = 16
EPS = 1e-6
NEG = -30000.0
GROUPS = [[0, 1], [2, 3], [4, 5], [6, 7]]
AG_GLOBAL = [0, 3, 1, 2]
GCOL = {0: 0, 3: 512, 1: 1024, 2: 1536}
MASKW = 898
DEBUG = False
PROFILE_SCOPES = False
SKIP_MLP = False
MLA_STOP = None
DBG_HEAD = 0


class Sem:
    def __init__(self, h, name):
        self.h = h
        self.n = 0
        self.name = name


class Tl:
    __slots__ = ("w", "r", "name", "pend")

    def __init__(self, name=""):
        self.w = None
        self.r = {}
        self.name = name
        self.pend = 0


class Eng:
    def __init__(self, name, sem):
        self.name = name
        self.sem = sem
        self.ops = []
        self.waited = {}


class Prog:
    def __init__(self, nc, stack):
        self.nc = nc
        self.stack = stack
        self.nsem = 0
        self.pe = Eng("pe", self.new_sem("pe"))
        self.act = Eng("act", self.new_sem("act"))
        self.dve = Eng("dve", self.new_sem("dve"))
        self.pool = Eng("pool", self.new_sem("pool"))
        self.sp = Eng("sp", self.new_sem("sp"))
        self.pend_r = []
        self.pend_w = []
        self.scope = None

    def new_sem(self, name):
        self.nsem += 1
        h = self.stack.enter_context(self.nc.semaphore(f"s{self.nsem}_{name}"))
        return Sem(h, name)

    def emit(self, eng, fn, reads=(), writes=(), dsem=None, signal=True, inc=None):
        waits = {}
        is_pe = eng is self.pe

        def need(ev):
            if ev is None:
                return
            s, v = ev
            if is_pe and s is self.pe.sem:
                return
            if waits.get(s, 0) < v:
                waits[s] = v

        for t in reads:
            if not is_pe:
                assert t.w != "PEND", f"read of PE-pending tile {t.name}"
            if t.w != "PEND":
                need(t.w)
        for t in writes:
            if not is_pe:
                assert t.pend == 0, f"write to tile {t.name} with unsignaled PE access"
                assert t.w != "PEND"
            if t.w != "PEND":
                need(t.w)
            for s, v in t.r.items():
                need((s, v))
        wl = [(s, v) for s, v in waits.items() if eng.waited.get(s, 0) < v]
        for s, v in wl:
            eng.waited[s] = v
        if dsem is not None:
            k = 16 if inc is None else inc
            dsem.n += k
            ev = (dsem, dsem.n)
            incr = (dsem, k)
        elif signal:
            eng.sem.n += 1
            ev = (eng.sem, eng.sem.n)
            incr = (eng.sem, 1)
        else:
            ev = None
            incr = None
        eng.ops.append((wl, fn, incr, self.scope))
        if is_pe and ev is None:
            for t in reads:
                t.pend += 1
                self.pend_r.append(t)
            for t in writes:
                t.r = {}
                if t.w != "PEND":
                    t.w = "PEND"
                    t.pend += 1
                    self.pend_w.append(t)
            return None
        if is_pe:
            for t in self.pend_r:
                t.pend -= 1
                if t.r.get(ev[0], 0) < ev[1]:
                    t.r[ev[0]] = ev[1]
            self.pend_r = []
            for t in self.pend_w:
                t.pend -= 1
                t.w = ev
            self.pend_w = []
        for t in reads:
            if t.r.get(ev[0], 0) < ev[1]:
                t.r[ev[0]] = ev[1]
        for t in writes:
            t.w = ev
            t.r = {}
        return ev

    def flush(self, block, final_waits):
        def run(e, eng, extra=()):
            cur = None
            sid = None
            for wl, fn, incr, scope in eng.ops:
                if PROFILE_SCOPES and scope != cur:
                    if cur is not None:
                        self.nc.leave_named_scope(cur, sid, False)
                    cur = scope
                    if cur is not None:
                        sid, _ = self.nc.enter_named_scope(cur, False)
                for s, v in wl:
                    e.wait_ge(s.h, v)
                ins = fn(e)
                if incr is not None:
                    ins.then_inc(incr[0].h, incr[1])
            if PROFILE_SCOPES and cur is not None:
                self.nc.leave_named_scope(cur, sid, False)
            for s, v in extra:
                e.wait_ge(s.h, v)

        @block.tensor
        def _(e):
            run(e, self.pe)

        @block.scalar
        def _(e):
            run(e, self.act)

        @block.vector
        def _(e):
            run(e, self.dve)

        @block.gpsimd
        def _(e):
            run(e, self.pool)

        @block.sync
        def _(e):
            run(e, self.sp, final_waits)


class Builder:
    def __init__(self, layers, debug_mid=False):
        self.layers = layers
        self.nc = bass.Bass("TRN2", target_bir_lowering=False)
        self.build()

    def dram_in(self, name, shape, dt):
        return self.nc.dram_tensor(name, list(shape), dt, kind="ExternalInput").ap()

    def build(self):
        import contextlib
        nc = self.nc
        with contextlib.ExitStack() as stack:
            self.stack = stack
            P = self.P = Prog(nc, stack)
            self.x_in = self.dram_in("x_in", [TOK, D], F32)
            self.out = nc.dram_tensor("out", [TOK, D], F32, kind="ExternalOutput").ap()
            self.dbg = nc.dram_tensor("dbg", [128, 4096], F32, kind="ExternalOutput").ap() if DEBUG else None
            self.dbg_off = 0
            self.dbg_map = {}
            self.negm_d = self.dram_in("negm", [128, 4 * MASKW], BF16)
            self.cbf_d = self.dram_in("cbf", [128, 4 * 128], BF16)
            self.gains_d = self.dram_in("gains", [128, 128], F32)
            self.cols_d = self.dram_in("cols", [128, 64], F32)
            self.c32_d = self.dram_in("c32", [128, 3 * 128], F32)
            self.bfb_d = self.dram_in("bfb", [128, 128], F32)
            self.selw_d = self.dram_in("selw", [128, 8 * 128], F32)
            self.w = {}
            for li in self.layers:
                kind = li % 3
                if kind == 0:
                    self.w[(li, "in")] = self.dram_in(f"w{li}_in", [12, 128, 8192], F32)
                elif kind == 1:
                    self.w[(li, "in")] = self.dram_in(f"w{li}_in", [12, 128, 8192], F32)
                    self.w[(li, "gate")] = self.dram_in(f"w{li}_gate", [128, 256], F32)
                else:
                    self.w[(li, "in")] = self.dram_in(f"w{li}_in", [3, 128, 8192], F32)
                    self.w[(li, "uq")] = self.dram_in(f"w{li}_uq", [4, 128, 6144], F32)
                    self.w[(li, "ukv")] = self.dram_in(f"w{li}_ukv", [512, 4096], F32)
                self.w[(li, "out")] = self.dram_in(f"w{li}_out", [4, 128, 8192], F32)
                if not SKIP_MLP:
                    self.w[(li, "w1")] = self.dram_in(f"w{li}_w1", [16, 128, 8192], F32)
                    self.w[(li, "w2")] = self.dram_in(f"w{li}_w2", [DFF, D], F32)
            self.kag_src = [nc.dram_tensor(f"kag_src{g}", [512, TOK], BF16) for g in range(4)]
            self.kag_dst = [nc.dram_tensor(f"kag_dst{g}", [1024, TOK], BF16) for g in range(4)]
            self.vag_src = [nc.dram_tensor(f"vag_src{g}", [512, TOK], BF16) for g in range(4)]
            self.vag_dst = [nc.dram_tensor(f"vag_dst{g}", [1024, TOK], BF16) for g in range(4)]
            self.kv_src = [nc.dram_tensor(f"kv_src{g}", [1024, TOK], BF16) for g in range(4)]
            self.kv_dst = [nc.dram_tensor(f"kv_dst{g}", [2048, TOK], BF16) for g in range(4)]
            self.q_scr = nc.dram_tensor("q_scr", [NH * 128, TOK], BF16)
            self.q_scr_mla = nc.dram_tensor("q_scr_mla", [NH * 192, TOK], BF16)
            self.kag_src_mla = [nc.dram_tensor(f"kag_src_mla{g}", [768, TOK], BF16) for g in range(4)]
            self.kag_dst_mla = [nc.dram_tensor(f"kag_dst_mla{g}", [1536, TOK], BF16) for g in range(4)]
            if any(l % 3 == 2 for l in self.layers):
                self.pos_d = self.dram_in("pos", [1, TOK], I32)
            self.t_kag_src = [Tl(f"kag_src{g}") for g in range(4)]
            self.t_kag_dst = [Tl(f"kag_dst{g}") for g in range(4)]
            self.t_vag_src = [Tl(f"vag_src{g}") for g in range(4)]
            self.t_vag_dst = [Tl(f"vag_dst{g}") for g in range(4)]
            self.t_q_scr = [Tl(f"q_scr{h}") for h in range(NH)]
            self.cc_sems = [P.new_sem(f"cc{i}") for i in range(9)]
            self.lf_src = nc.dram_tensor("lf_src", [TOK, NH], F32)
            self.lf_dst = nc.dram_tensor("lf_dst", [2 * TOK, NH], F32)
            self.cqT_scr = nc.dram_tensor("cqT_scr", [NH, TOK], F32)
            self.t_lf_src, self.t_lf_dst, self.t_cqT = Tl("lf_src"), Tl("lf_dst"), Tl("cqT")
            sb = lambda name, shape, dt: stack.enter_context(nc.sbuf_tensor(name, shape, dt))
            self.xT = sb("xT", [128, NCH, TOK], F32)
            self.hT = sb("hT", [128, NCH, TOK], BF16)
            self.WB = [sb(f"WB{i}", [128, 8192], BF16) for i in range(3)]
            self.regD = sb("regD", [128, 45056], U8)
            self.negm = sb("negm_sb", [128, 4, MASKW], BF16)
            self.cbf = sb("cbf_sb", [128, 4, 128], BF16)
            self.c32 = sb("c32_sb", [128, 3, 128], F32)
            self.foxc = sb("foxc_sb", [128, 256], F32)
            self.t_foxc = Tl("foxc")
            self.gains = sb("gains_sb", [128, 128], F32)
            self.cols = sb("cols_sb", [128, 64], F32)
            self.sq = sb("sq_sb", [128, 4, TB], BF16)
            self.rstd = sb("rstd_sb", [128, 2, TB], F32)
            self.psum = stack.enter_context(nc.psum_tensor("ps", [128, 8, TB], F32))
            self.t_x = [[Tl(f"x{c}_{tb}") for tb in range(2)] for c in range(NCH)]
            self.t_h = [[Tl(f"h{c}_{tb}") for tb in range(2)] for c in range(NCH)]
            self.t_WB = [Tl(f"WB{i}") for i in range(3)]
            self.wb_sem = [P.new_sem(f"wb{i}") for i in range(3)]
            self.wb_next = 0
            self.t_bank = [Tl(f"bank{i}") for i in range(8)]
            self.t_sq = [Tl(f"sq{i}") for i in range(4)]
            self.t_rstd = [Tl(f"rstd{i}") for i in range(2)]
            self.t_const = Tl("const")
            self.t_negm = Tl("negm")
            self.t_regD = []
            self.regD_retired = {}
            self.mm_rot = 0
            self.ag_pending = []
            self.mm_mod = 4
            self.deferred = None
            self.ev_rot = 0
            self.sq_rot = 0
            self.rs_rot = 0
            self.misc_sems = {}

            self.id32 = self.c32[:, 0, :]
            self.ones32 = self.c32[:, 1, :]
            self.tri32 = self.c32[:, 2, :]
            self.ident = self.cbf[:, 0, :]
            self.ones = self.cbf[:, 1, :]
            self.negutri = self.cbf[:, 2, :]
            self.negones = self.cbf[:, 3, :]

            P.scope = "setup"
            self.emit_setup()
            for li in self.layers:
                kind, j = li % 3, li // 3
                if kind == 0:
                    self.emit_sb_layer(li, j)
                elif kind == 1:
                    self.emit_fox_layer(li, j)
                else:
                    self.emit_mla_layer(li, j)
                if not SKIP_MLP:
                    self.emit_mlp(li)
            fin = self.emit_output()
            with nc.Block() as block:
                P.flush(block, fin)

    def debug_dump(self, name, ap, t, parts=128):
        if not DEBUG:
            return
        n = ap.shape[-1]
        self.dbg_map[name] = (self.dbg_off, n, parts)
        self.dma(self.P.sp, self.dbg[0:parts, self.dbg_off:self.dbg_off + n], ap, reads=[t], writes=[], sem=self.dsem("dbg"))
        self.dbg_off += n

    def dsem(self, name):
        if name not in self.misc_sems:
            self.misc_sems[name] = self.P.new_sem(name)
        return self.misc_sems[name]

    def regD_phase(self):
        ret = dict(self.regD_retired)
        for t in self.t_regD:
            assert t.pend == 0 and t.w != "PEND", t.name
            if t.w is not None:
                s, v = t.w
                if ret.get(s, 0) < v:
                    ret[s] = v
            for s, v in t.r.items():
                if ret.get(s, 0) < v:
                    ret[s] = v
        self.regD_retired = ret
        self.t_regD = []
        self.regD_off = 0

    def regD_alloc(self, nbytes, dt, name, parts=128):
        off = self.regD_off
        assert off % 4 == 0
        self.regD_off += nbytes
        assert self.regD_off <= 45056, (name, self.regD_off)
        ap = self.regD[0:parts, off:off + nbytes].bitcast(dt)
        t = Tl(name)
        t.r = dict(self.regD_retired)
        self.t_regD.append(t)
        return ap, t

    def regD_tile(self, name):
        t = Tl(name)
        t.r = dict(self.regD_retired)
        self.t_regD.append(t)
        return t

    def bank(self, i):
        return self.psum[:, i, :]

    def next_mm_bank(self, lo=0, n=None):
        n = n or self.mm_mod
        b = lo + self.mm_rot % n
        self.mm_rot += 1
        return b

    def evac_engine(self):
        self.ev_rot += 1
        return self.P.act if self.ev_rot % 2 == 0 else self.P.dve

    def mm(self, out_ap, lhsT, rhs, start, stop, reads, writes, signal):
        def fn(e, out_ap=out_ap, lhsT=lhsT, rhs=rhs, start=start, stop=stop):
            return e.matmul(out_ap, lhsT=lhsT, rhs=rhs, start=start, stop=stop)
        return self.P.emit(self.P.pe, fn, reads=reads, writes=writes, signal=signal)

    def act(self, out, in_, func, reads, writes, bias=None, scale=None):
        def fn(e, out=out, in_=in_, func=func, bias=bias, scale=scale):
            kw = {}
            if bias is not None:
                kw["bias"] = bias
            if scale is not None:
                kw["scale"] = scale
            return e.activation(out=out, in_=in_, func=func, **kw)
        return self.P.emit(self.P.act, fn, reads=reads, writes=writes)

    def copy(self, eng, out, in_, reads, writes):
        if eng is self.P.act:
            return self.act(out, in_, AF.Copy, reads, writes)
        def fn(e, out=out, in_=in_):
            return e.tensor_copy(out=out, in_=in_)
        return self.P.emit(eng, fn, reads=reads, writes=writes)

    def tt(self, out, in0, in1, op, reads, writes, eng=None):
        def fn(e, out=out, in0=in0, in1=in1, op=op):
            return e.tensor_tensor(out=out, in0=in0, in1=in1, op=op)
        return self.P.emit(eng or self.P.dve, fn, reads=reads, writes=writes)

    def ts(self, out, in0, s1, s2, op0, op1, reads, writes):
        def fn(e, out=out, in0=in0, s1=s1, s2=s2, op0=op0, op1=op1):
            if op1 is None:
                return e.tensor_scalar(out=out, in0=in0, scalar1=s1, scalar2=None, op0=op0)
            return e.tensor_scalar(out=out, in0=in0, scalar1=s1, scalar2=s2, op0=op0, op1=op1)
        return self.P.emit(self.P.dve, fn, reads=reads, writes=writes)

    def stt(self, out, in0, scalar, in1, op0, op1, reads, writes):
        def fn(e, out=out, in0=in0, scalar=scalar, in1=in1, op0=op0, op1=op1):
            return e.scalar_tensor_tensor(out=out, in0=in0, scalar=scalar, in1=in1, op0=op0, op1=op1)
        return self.P.emit(self.P.dve, fn, reads=reads, writes=writes)

    def dma(self, q, out, in_, reads, writes, sem):
        def fn(e, out=out, in_=in_):
            return e.dma_start(out=out, in_=in_)
        return self.P.emit(q, fn, reads=reads, writes=writes, dsem=sem)

    def load_w(self, views):
        i = self.wb_next % 3
        self.wb_next += 1
        for k, (dst_fn, src) in enumerate(views):
            self.dma(self.P.pool, dst_fn(self.WB[i]), src, reads=[], writes=[self.t_WB[i]], sem=self.wb_sem[i])
        for item in self.ag_pending:
            item[0] -= 1
        self.flush_ag()
        return i

    def allgather(self, src, dst, t_src, t_dst, slot, defer=True):
        sem = self.cc_sems[slot]

        t_srcs = t_src if isinstance(t_src, list) else [t_src]
        t_dsts = t_dst if isinstance(t_dst, list) else [t_dst]

        def emit_now(src=src, dst=dst, sem=sem):
            def fn(e):
                return e.collective_compute("AllGather", ALU.bypass, replica_groups=GROUPS,
                                            ins=[src.ap().opt()], outs=[dst.ap().opt()])
            return self.P.emit(self.P.pool, fn, reads=t_srcs, writes=t_dsts, dsem=sem, inc=1)
        if not defer:
            return emit_now()
        self.ag_pending.append([2, emit_now])

    def flush_ag(self, force=False):
        keep = []
        for item in self.ag_pending:
            if force or item[0] <= 0:
                item[1]()
            else:
                keep.append(item)
        self.ag_pending = keep

    def emit_setup(self):
        P = self.P
        cs = self.dsem("const")
        for dst, src in ((self.negm[:].rearrange("p a b -> p (a b)"), self.negm_d), (self.cbf[:].rearrange("p a b -> p (a b)"), self.cbf_d),
                         (self.c32[:].rearrange("p a b -> p (a b)"), self.c32_d), (self.gains[:], self.gains_d), (self.cols[:], self.cols_d)):
            self.dma(P.sp, dst, src, reads=[], writes=[self.t_const], sem=cs)
        self.regD_phase()
        xs = []
        for j in range(4):
            ap, t = self.regD_alloc(8192, F32, f"xstage{j}")
            xs.append((ap, t, P.new_sem(f"xs{j}")))
        for tb in range(2):
            for j in range(4):
                ap, t, sem = xs[j]
                r0 = tb * TB + j * 128
                self.dma(P.sp, ap, self.x_in[r0:r0 + 128, :], reads=[], writes=[t], sem=sem)
            for c in range(NCH):
                b = self.next_mm_bank()
                for j in range(4):
                    ap, t, sem = xs[j]
                    def fn(e, o=self.psum[:, b, j * 128:(j + 1) * 128], i=ap[:, c * 128:(c + 1) * 128]):
                        return e.transpose(o, i, self.id32)
                    P.emit(P.pe, fn, reads=[t, self.t_const], writes=[self.t_bank[b]], signal=(j == 3))
                self.copy(self.evac_engine(), self.xT[:, c, tb * TB:(tb + 1) * TB], self.bank(b),
                          reads=[self.t_bank[b]], writes=[self.t_x[c][tb]])

    def emit_norm(self, gi):
        P = self.P
        P.scope = f"norm{gi}"
        for tb in range(2):
            b = self.next_mm_bank()
            for c in range(NCH):
                s = self.sq_rot % 4
                self.sq_rot += 1
                self.act(self.sq[:, s, :], self.xT[:, c, tb * TB:(tb + 1) * TB], AF.Square,
                         reads=[self.t_x[c][tb]], writes=[self.t_sq[s]])
                self.mm(self.bank(b), self.ones, self.sq[:, s, :], c == 0, c == NCH - 1,
                        reads=[self.t_sq[s], self.t_const], writes=[self.t_bank[b]], signal=True)
            self.act(self.rstd[:, tb, :], self.bank(b), AF.Ln, reads=[self.t_bank[b]], writes=[self.t_rstd[tb]],
                     bias=float(EPS), scale=1.0 / D)
            self.act(self.rstd[:, tb, :], self.rstd[:, tb, :], AF.Exp, reads=[self.t_rstd[tb]], writes=[self.t_rstd[tb]],
                     scale=-0.5)
            for c in range(NCH):
                self.stt(self.hT[:, c, tb * TB:(tb + 1) * TB], self.xT[:, c, tb * TB:(tb + 1) * TB],
                         self.gains[:, gi * 16 + c:gi * 16 + c + 1], self.rstd[:, tb, :], ALU.mult, ALU.mult,
                         reads=[self.t_x[c][tb], self.t_rstd[tb], self.t_const], writes=[self.t_h[c][tb]])

    def proj_fm(self, wi, wview, col0, evac, nkc=NCH, src=None, src_t=None, ncols=128, extra_reads=()):
        src = src if src is not None else self.hT
        src_t = src_t if src_t is not None else self.t_h
        for tb in range(2):
            b = self.next_mm_bank()
            for kc in range(nkc):
                self.mm(self.psum[0:ncols, b, :], wview[:, kc, col0:col0 + ncols], src[:, kc, tb * TB:(tb + 1) * TB],
                        kc == 0, kc == nkc - 1, reads=[self.t_WB[wi], src_t[kc][tb]] + list(extra_reads),
                        writes=[self.t_bank[b]], signal=(kc == nkc - 1))
            if self.deferred is not None:
                d, self.deferred = self.deferred, None
                d()
            r = evac(b, tb)
            self.deferred = r if callable(r) else None

    def flush_deferred(self):
        if self.deferred is not None:
            d, self.deferred = self.deferred, None
            d()

    def proj_tm(self, wi, wview_cols, evac, nkc=NCH, src=None, src_t=None, out3=False):
        src = src if src is not None else self.hT
        src_t = src_t if src_t is not None else self.t_h
        for tt in range(8):
            b = self.next_mm_bank()
            for kc in range(nkc):
                o = self.bank(b).rearrange("p (h x) -> p h x", x=128) if out3 else self.bank(b)
                self.mm(o, src[:, kc, tt * 128:(tt + 1) * 128], wview_cols(kc),
                        kc == 0, kc == nkc - 1, reads=[self.t_WB[wi], src_t[kc][tt // 4]],
                        writes=[self.t_bank[b]], signal=(kc == nkc - 1))
            evac(b, tt)

    def w_tile16(self, src_tile, L=8192):
        return [(lambda wb: wb[:, 0:L].rearrange("p (a b) -> p a b", b=2048),
                 src_tile[:, 0:L].rearrange("p (a b) -> p a b", b=2048))]

    def emit_qkv_generic(self, li, w_in, kcol, vcol, qcol, q_evac_scale, qk_norm=None):
        P = self.P
        P.scope = f"qkv{li}"
        self.regD_phase()
        stages = []
        for i in range(3):
            ap, t = self.regD_alloc(8192, BF16, f"stage{i}")
            stages.append((ap, t, self.dsem(f"stage{i}")))
        srot = [0]

        def next_stage():
            s = stages[srot[0] % 3]
            srot[0] += 1
            return s

        def q_group(g):
            wi = self.load_w(self.w_tile16(w_in[qcol // 512 + g]))
            wv = self.WB[wi][:, :].rearrange("p (c n) -> p c n", n=512)
            st_ap, st_t, st_sem = next_stage()
            stv = st_ap.rearrange("p (h t) -> p h t", t=TOK)
            for m in range(4):
                def evac(b, tb, m=m, stv=stv, st_t=st_t):
                    o = stv[:, m, tb * TB:(tb + 1) * TB]
                    if qk_norm is not None:
                        return self.evac_headnorm(b, o, st_t, qk_norm[0], 128)
                    self.act(o, self.bank(b), AF.Copy, reads=[self.t_bank[b]], writes=[st_t], scale=q_evac_scale)
                self.proj_fm(wi, wv, m * 128, evac)
            self.flush_deferred()
            qdst = self.q_scr.ap()[g * 512:(g + 1) * 512, :].rearrange("(h d) t -> d h t", d=128)
            self.dma(P.sp, qdst, stv, reads=[st_t], writes=[self.t_q_scr[4 * g + m] for m in range(4)], sem=st_sem)

        for g in range(4):
            wi = self.load_w(self.w_tile16(w_in[kcol // 512 + g]))
            wv = self.WB[wi][:, :].rearrange("p (c n) -> p c n", n=512)
            st_ap, st_t, st_sem = next_stage()
            stv = st_ap.rearrange("p (h t) -> p h t", t=TOK)
            for m in range(4):
                def evac(b, tb, m=m, stv=stv, st_t=st_t):
                    if qk_norm is not None:
                        return self.evac_headnorm(b, stv[:, m, tb * TB:(tb + 1) * TB], st_t, qk_norm[1], 128)
                    self.copy(self.evac_engine(), stv[:, m, tb * TB:(tb + 1) * TB], self.bank(b),
                              reads=[self.t_bank[b]], writes=[st_t])
                self.proj_fm(wi, wv, m * 128, evac)
            self.flush_deferred()
            self.dma(P.sp, self.kv_src[g].ap()[0:512, :].rearrange("(h d) t -> d h t", d=128), stv,
                     reads=[st_t], writes=[self.t_kag_src[g]], sem=st_sem)
            wi = self.load_w(self.w_tile16(w_in[vcol // 512 + g]))
            wv = self.WB[wi][:, :].rearrange("p (c n) -> p c n", n=512)
            st_ap, st_t, st_sem = next_stage()
            stv = st_ap.rearrange("p (n f) -> p n f", f=512)

            def evac_v(b, tt, stv=stv, st_t=st_t):
                self.copy(self.evac_engine(), stv[:, tt, :], self.bank(b), reads=[self.t_bank[b]], writes=[st_t])
            self.proj_tm(wi, lambda kc, wv=wv: wv[:, kc, :], evac_v)
            vdst = self.kv_src[g].ap()[512:1024, :].rearrange("r (two f) -> (r two) f", two=2).rearrange("(n p) f -> p n f", p=128)
            self.dma(P.sp, vdst, stv, reads=[st_t], writes=[self.t_vag_src[g]], sem=st_sem)
            self.allgather(self.kv_src[g], self.kv_dst[g], [self.t_kag_src[g], self.t_vag_src[g]],
                           [self.t_kag_dst[g], self.t_vag_dst[g]], 2 * g)
            q_group(g)

    def attn_load(self, h, bf, mla=False):
        P = self.P
        g, m = h // 4, h % 4
        q_ap, q_t, sq_ = bf["q"]
        k_ap, k_t, sk_ = bf["k"]
        v_ap, v_t, sv_ = bf["v"]
        HR = 192 if mla else 128
        qsrc = (self.q_scr_mla if mla else self.q_scr).ap()
        self.dma(P.sp, q_ap, qsrc[h * HR:h * HR + 128, :], reads=[self.t_q_scr[h]], writes=[q_t], sem=sq_)
        if mla:
            ksrc = self.kag_dst_mla[g].ap().rearrange("(r x) t -> x r t", r=2)
        else:
            ksrc = self.kv_dst[g].ap().rearrange("(r x) t -> x r t", r=2)
        self.dma(P.sp, k_ap.rearrange("p (r t) -> p r t", r=2), ksrc[m * HR:m * HR + 128, :, :],
                 reads=[self.t_kag_dst[g]], writes=[k_t], sem=sk_)
        if mla:
            qr_ap, qr_t, sqr_ = bf["qr"]
            kr_ap, kr_t, skr_ = bf["kr"]
            self.dma(P.sp, qr_ap, qsrc[h * HR + 128:(h + 1) * HR, :], reads=[self.t_q_scr[h]], writes=[qr_t], sem=sqr_)
            self.dma(P.sp, kr_ap.rearrange("p (r t) -> p r t", r=2), ksrc[m * HR + 128:(m + 1) * HR, :, :],
                     reads=[self.t_kag_dst[g]], writes=[kr_t], sem=skr_)
        if mla:
            vsrc = self.vag_dst[g].ap().rearrange("(r x) (two f) -> r (x two) f", r=2, two=2)
        else:
            vsrc = self.kv_dst[g].ap().rearrange("(r x) t -> r x t", r=2)[:, 512:1024, :]
            vsrc = vsrc.rearrange("r x (two f) -> r (x two) f", two=2)
        vsrc = vsrc.rearrange("r (n p) f -> p r n f", p=128)[:, :, :, m * 128:(m + 1) * 128]
        for r in range(2):
            self.dma(P.sp, v_ap.rearrange("p (r n f) -> p r n f", r=2, f=128)[:, r, :, :], vsrc[:, r, :, :],
                     reads=[self.t_vag_dst[g]], writes=[v_t], sem=sv_)

    def attn_bufs(self, mla=False):
        bufs = []
        for i in range(2):
            d = {}
            d["q"] = self.regD_alloc(2048, BF16, f"q{i}") + (self.dsem(f"aq{i}"),)
            d["k"] = self.regD_alloc(4096, BF16, f"k{i}") + (self.dsem(f"ak{i}"),)
            d["v"] = self.regD_alloc(4096, BF16, f"v{i}") + (self.dsem(f"av{i}"),)
            if mla:
                d["qr"] = self.regD_alloc(2048, BF16, f"qr{i}", parts=64) + (self.dsem(f"aqr{i}"),)
                d["kr"] = self.regD_alloc(4096, BF16, f"kr{i}", parts=64) + (self.dsem(f"akr{i}"),)
            bufs.append(d)
        return bufs

    def key_tiles(self, lq, reverse):
        blocks = [(0, 0), (1, 1)] if lq == 0 else [(0, None), (1, None), (2, 2), (3, 3)]
        res = [(G, i, u) for (G, u) in blocks for i in range(4)]
        return res[::-1] if reverse else res

    def emit_sb_attention(self):
        P = self.P
        self.flush_ag(force=True)
        P.scope = f"sbattn{self.wb_next}"
        self.regD_phase()
        bufs = self.attn_bufs()
        NS = 3
        e_t = [self.regD_alloc(2048, F32, f"e{i}") for i in range(NS)]
        sp_t = [self.regD_alloc(1024, BF16, f"sp{i}") for i in range(NS)]
        p_t = [self.regD_alloc(1024, BF16, f"p{i}") for i in range(NS)]
        sps32 = [self.regD_alloc(2048, F32, f"sps32_{i}") for i in range(2)]
        spsbf = [self.regD_alloc(1024, BF16, f"spsbf_{i}") for i in range(2)]
        self.attn_load(0, bufs[0])
        items = []
        for h in range(NH):
            seqs = [self.key_tiles(1, True), self.key_tiles(0, True)]
            for n in range(16):
                for si, lq in ((0, 1), (1, 0)):
                    if n < len(seqs[si]):
                        G, i, u = seqs[si][n]
                        items.append((h, lq, n, G, i, u, n == len(seqs[si]) - 1))
        rot = [0]
        state = {}

        def s1(it):
            h, lq, n, G, i, u, last = it
            bf = bufs[h % 2]
            q_ap, q_t, k_ap, k_t = bf["q"][0], bf["q"][1], bf["k"][0], bf["k"][1]
            if lq == 1 and n == 3 and h + 1 < NH:
                self.attn_load(h + 1, bufs[(h + 1) % 2])
            slot = rot[0] % NS
            rot[0] += 1
            ba = self.next_mm_bank(0, 4)
            kc0 = GCOL[G] + i * 128
            masked = u is not None
            self.mm(self.bank(ba), k_ap[:, kc0:kc0 + 128], q_ap[:, lq * TB:(lq + 1) * TB], True, not masked,
                    reads=[k_t, q_t], writes=[self.t_bank[ba]], signal=not masked)
            if masked:
                c0 = 384 - 128 * i
                self.mm(self.bank(ba), self.ident, self.negm[:, u, c0:c0 + TB], False, True,
                        reads=[self.t_const], writes=[self.t_bank[ba]], signal=True)
            ea, et = e_t[slot]
            sa, st = sp_t[slot]
            self.act(ea, self.bank(ba), AF.Exp, reads=[self.t_bank[ba]], writes=[et])
            self.act(sa, ea, AF.Ln, reads=[et], writes=[st], bias=1.0)
            state[(h, lq, n)] = (slot, ba)

        def s2(it):
            h, lq, n, G, i, u, last = it
            slot, ba = state[(h, lq, n)]
            sa, st = sp_t[slot]
            pa, pt = p_t[slot]
            s32a, s32t = sps32[lq]
            sbfa, sbft = spsbf[lq]
            self.mm(self.bank(ba), self.negutri, sa, False, n == 0, reads=[st, self.t_const],
                    writes=[self.t_bank[ba]], signal=(n == 0))
            if n > 0:
                self.mm(self.bank(ba), self.negones, sbfa, False, True, reads=[sbft, self.t_const],
                        writes=[self.t_bank[ba]], signal=True)
            self.act(pa, self.bank(ba), AF.Exp, reads=[self.t_bank[ba]], writes=[pt])
            if not last:
                if n == 0:
                    self.copy(P.dve, s32a, sa, reads=[st], writes=[s32t])
                else:
                    self.tt(s32a, s32a, sa, ALU.add, reads=[s32t, st], writes=[s32t])
                self.copy(P.dve, sbfa, s32a, reads=[s32t], writes=[sbft])

        def s3(it):
            h, lq, n, G, i, u, last = it
            slot, ba = state.pop((h, lq, n))
            v_ap, v_t = bufs[h % 2]["v"][0], bufs[h % 2]["v"][1]
            pa, pt = p_t[slot]
            bo = 4 + (h % 2) * 2 + lq
            vt = AG_GLOBAL.index(G) * 4 + i
            self.mm(self.bank(bo), v_ap[:, vt * 128:(vt + 1) * 128], pa, n == 0, last,
                    reads=[v_t, pt], writes=[self.t_bank[bo]], signal=True)
            if last:
                self.copy(self.evac_engine(), self.hT[:, h, lq * TB:(lq + 1) * TB], self.bank(bo),
                          reads=[self.t_bank[bo]], writes=[self.t_h[h][lq]])

        n_it = len(items)
        for k in range(n_it + 2):
            if k < n_it:
                s1(items[k])
            if 0 <= k - 1 < n_it:
                s2(items[k - 1])
            if 0 <= k - 2 < n_it:
                s3(items[k - 2])


    def evac_headnorm(self, b, out_ap, out_t, gcol, nfeat):
        s = self.sq_rot % 4
        self.sq_rot += 1
        self.act(self.sq[:, s, :], self.bank(b), AF.Square, reads=[self.t_bank[b]], writes=[self.t_sq[s]])

        def cont():
            b2 = self.next_mm_bank()
            self.mm(self.bank(b2), self.ones, self.sq[:, s, :], True, True, reads=[self.t_sq[s], self.t_const],
                    writes=[self.t_bank[b2]], signal=True)
            rs = self.rs_rot % 2
            self.rs_rot += 1
            self.act(self.rstd[:, rs, :], self.bank(b2), AF.Ln, reads=[self.t_bank[b2]], writes=[self.t_rstd[rs]],
                     bias=float(EPS), scale=1.0 / nfeat)
            self.act(self.rstd[:, rs, :], self.rstd[:, rs, :], AF.Exp, reads=[self.t_rstd[rs]], writes=[self.t_rstd[rs]],
                     scale=-0.5)
            self.stt(out_ap, self.bank(b), self.cols[:, gcol:gcol + 1], self.rstd[:, rs, :], ALU.mult, ALU.mult,
                     reads=[self.t_bank[b], self.t_rstd[rs], self.t_const], writes=[out_t])
        return cont

    def emit_fox_gate(self, li):
        P = self.P
        P.scope = "foxgate"
        w_in = self.w[(li, "in")]
        self.regD_phase()
        L_ap, L_t = self.regD_alloc(512, F32, "Lown")
        La_ap, La_t = self.regD_alloc(1024, F32, "Lall")
        cq_ap, cq_t = self.regD_alloc(512, F32, "Cq")
        cqT_ap, cqT_t = self.regD_alloc(4096, F32, "CqT", parts=16)
        bfb_ap, bfb_t = self.regD_alloc(512, F32, "bfb")
        selw_ap, selw_t = self.regD_alloc(4096, F32, "selw")
        gs = self.dsem("gate")
        self.dma(P.sp, bfb_ap, self.bfb_d, reads=[], writes=[bfb_t], sem=gs)
        self.dma(P.sp, selw_ap, self.selw_d, reads=[], writes=[selw_t], sem=gs)
        selw = selw_ap.rearrange("p (a b) -> p a b", b=128)
        wi = self.load_w([(lambda wb: wb[:, 0:256], self.w[(li, "gate")])])
        wv = self.WB[wi][:, 0:256].rearrange("p (c n) -> p c n", n=NH)
        b = self.next_mm_bank()
        for tt in range(8):
            for c in range(NCH):
                self.mm(self.psum[:, b, tt * NH:(tt + 1) * NH], self.hT[:, c, tt * 128:(tt + 1) * 128], wv[:, c, :],
                        c == 0, c == NCH - 1, reads=[self.t_WB[wi], self.t_h[c][tt // 4]], writes=[self.t_bank[b]],
                        signal=(c == NCH - 1))
        self.tt(L_ap, self.psum[:, b, 0:128], bfb_ap, ALU.add, reads=[self.t_bank[b], bfb_t], writes=[L_t])
        self.act(L_ap, L_ap, AF.Exp, reads=[L_t], writes=[L_t], scale=-1.0)
        self.act(L_ap, L_ap, AF.Ln, reads=[L_t], writes=[L_t], bias=1.0)
        self.dma(P.sp, self.lf_src.ap().rearrange("(n p) h -> p n h", p=128), L_ap.rearrange("p (n h) -> p n h", h=NH),
                 reads=[L_t], writes=[self.t_lf_src], sem=gs)
        self.allgather(self.lf_src, self.lf_dst, self.t_lf_src, self.t_lf_dst, 8, defer=False)
        self.dma(P.sp, La_ap.rearrange("p (n h) -> p n h", h=NH), self.lf_dst.ap().rearrange("(n p) h -> p n h", p=128),
                 reads=[self.t_lf_dst], writes=[La_t], sem=gs)
        Lall = La_ap.rearrange("p (n h) -> p n h", h=NH)
        Lown = L_ap.rearrange("p (n h) -> p n h", h=NH)
        bk = self.next_mm_bank()
        gt = lambda T: AG_GLOBAL[T // 4] * 4 + T % 4
        for T in range(16):
            terms = [(self.ones32, Tp) for Tp in range(16) if gt(Tp) < gt(T)] + [(self.tri32, T)]
            for k, (lt, Tp) in enumerate(terms):
                self.mm(self.psum[:, bk, T * NH:(T + 1) * NH], lt, Lall[:, Tp, :], k == 0, k == len(terms) - 1,
                        reads=[La_t, self.t_const], writes=[self.t_bank[bk]], signal=(k == len(terms) - 1))
        self.copy(P.dve, self.foxc[:, :], self.psum[:, bk, 0:256], reads=[self.t_bank[bk]], writes=[self.t_foxc])
        self.debug_dump("Lown", L_ap, L_t)
        self.debug_dump("Lall", La_ap, La_t)
        self.debug_dump("Ck", self.foxc[:, :], self.t_foxc)
        bq = self.next_mm_bank()
        for t in range(8):
            l = t // 4
            terms = [(self.ones32, Lown[:, tp, :], L_t) for tp in range(l * 4, t)] + [(self.tri32, Lown[:, t, :], L_t)]
            terms += [(selw[:, jb * 2 + l, :], Lall[:, jb * 4 + i, :], La_t) for jb in range(4) for i in range(4)]
            for k, (lt, rh, rt) in enumerate(terms):
                self.mm(self.psum[:, bq, t * NH:(t + 1) * NH], lt, rh, k == 0, k == len(terms) - 1,
                        reads=[rt, selw_t, self.t_const], writes=[self.t_bank[bq]], signal=(k == len(terms) - 1))
        self.copy(P.dve, cq_ap, self.psum[:, bq, 0:128], reads=[self.t_bank[bq]], writes=[cq_t])
        self.debug_dump("Cq", cq_ap, cq_t)
        for half in range(2):
            bt = self.next_mm_bank()
            for k in range(4):
                t = half * 4 + k
                def fn(e, o=self.psum[0:NH, bt, k * 128:(k + 1) * 128], i=cq_ap[:, t * NH:(t + 1) * NH]):
                    return e.transpose(o, i, self.id32)
                P.emit(P.pe, fn, reads=[cq_t, self.t_const], writes=[self.t_bank[bt]], signal=(k == 3))
            self.act(cqT_ap[:, half * TB:(half + 1) * TB], self.psum[0:NH, bt, :], AF.Copy,
                     reads=[self.t_bank[bt]], writes=[cqT_t], scale=-1.0)
        self.dma(P.sp, self.cqT_scr.ap(), cqT_ap, reads=[cqT_t], writes=[self.t_cqT], sem=gs)

    def emit_softmax_attention(self, fox, scale, mla=False):
        P = self.P
        self.flush_ag(force=True)
        P.scope = f"smattn{self.wb_next}"
        self.regD_phase()
        bufs = self.attn_bufs(mla)
        NS = 3
        p_t = [self.regD_alloc(1024, BF16, f"p{i}") for i in range(NS)]
        rinv = [self.regD_alloc(2048, F32, f"rinv{i}") for i in range(2)]
        if fox:
            tmp_t = [self.regD_alloc(2048, F32, f"tmp{i}") for i in range(NS)]
            cfb = [self.regD_alloc(4096, F32, f"cfb{i}") + (self.dsem(f"cfb{i}"),) for i in range(2)]

        def load(h):
            self.attn_load(h, bufs[h % 2], mla)
            if fox:
                ap, t, sem = cfb[h % 2]
                self.dma(P.sp, ap, self.cqT_scr.ap()[h:h + 1, :].partition_broadcast(128), reads=[self.t_cqT],
                         writes=[t], sem=sem)
        load(0)
        items = []
        for h in range(NH):
            seqs = [self.key_tiles(1, False), self.key_tiles(0, False)]
            for n in range(16):
                for si, lq in ((0, 1), (1, 0)):
                    if n < len(seqs[si]):
                        G, i, u = seqs[si][n]
                        items.append((h, lq, n, G, i, u, n == len(seqs[si]) - 1))
        rot = [0]
        state = {}

        def s1(it):
            h, lq, n, G, i, u, last = it
            bf = bufs[h % 2]
            if lq == 1 and n == 3 and h + 1 < NH:
                load(h + 1)
            slot = rot[0] % NS
            rot[0] += 1
            ba = self.next_mm_bank(0, 4)
            kc0 = GCOL[G] + i * 128
            masked = u is not None
            qs = slice(lq * TB, (lq + 1) * TB)
            self.mm(self.bank(ba), bf["k"][0][:, kc0:kc0 + 128], bf["q"][0][:, qs], True, not (masked or mla),
                    reads=[bf["k"][1], bf["q"][1]], writes=[self.t_bank[ba]], signal=not (masked or mla))
            if mla:
                self.mm(self.bank(ba), bf["kr"][0][:, kc0:kc0 + 128], bf["qr"][0][:, qs], False, not masked,
                        reads=[bf["kr"][1], bf["qr"][1]], writes=[self.t_bank[ba]], signal=not masked)
            if masked:
                c0 = 385 - 128 * i
                self.mm(self.bank(ba), self.ident, self.negm[:, u, c0:c0 + TB], False, True,
                        reads=[self.t_const], writes=[self.t_bank[ba]], signal=True)
            pa, pt = p_t[slot]
            if fox:
                ta, tt_ = tmp_t[slot]
                ca, ct, _ = cfb[h % 2]
                self.stt(ta, self.bank(ba), float(scale), ca[:, qs], ALU.mult, ALU.add,
                         reads=[self.t_bank[ba], ct], writes=[tt_])
                T_ag = AG_GLOBAL.index(G) * 4 + i
                self.act(pa, ta, AF.Exp, reads=[tt_, self.t_foxc], writes=[pt],
                         bias=self.foxc[:, T_ag * NH + h:T_ag * NH + h + 1])
            else:
                self.act(pa, self.bank(ba), AF.Exp, reads=[self.t_bank[ba]], writes=[pt], scale=float(scale))
            state[(h, lq, n)] = slot

        def s2(it):
            h, lq, n, G, i, u, last = it
            slot = state.pop((h, lq, n))
            bf = bufs[h % 2]
            pa, pt = p_t[slot]
            bo = 4 + lq * 2
            bs = bo + 1
            vt = AG_GLOBAL.index(G) * 4 + i
            self.mm(self.bank(bo), bf["v"][0][:, vt * 128:(vt + 1) * 128], pa, n == 0, last,
                    reads=[bf["v"][1], pt], writes=[self.t_bank[bo]], signal=False)
            self.mm(self.bank(bs), self.ones, pa, n == 0, last,
                    reads=[pt, self.t_const], writes=[self.t_bank[bs]], signal=True)
            if last:
                ra, rt = rinv[lq]
                self.act(ra, self.bank(bs), AF.Ln, reads=[self.t_bank[bs]], writes=[rt])
                self.act(ra, ra, AF.Exp, reads=[rt], writes=[rt], scale=-1.0)
                self.tt(self.hT[:, h, lq * TB:(lq + 1) * TB], self.bank(bo), ra, ALU.mult,
                        reads=[self.t_bank[bo], rt], writes=[self.t_h[h][lq]])
                if DEBUG and h == DBG_HEAD and lq == 1:
                    d1, d1t = self.regD_alloc(2048, F32, "dbg1")
                    self.copy(P.dve, d1, self.hT[:, h, lq * TB:(lq + 1) * TB], reads=[self.t_h[h][lq]], writes=[d1t])
                    self.debug_dump("oT", d1, d1t)
                    self.copy(P.dve, d1, ra, reads=[rt, d1t], writes=[d1t])
                    self.debug_dump("rinv", d1, d1t)
                    self.copy(P.dve, d1, bf["v"][0][:, 4 * 128:8 * 128], reads=[bf["v"][1], d1t], writes=[d1t])
                    self.debug_dump("v47", d1, d1t)

        n_it = len(items)
        for k in range(n_it + 1):
            if k < n_it:
                s1(items[k])
            if 0 <= k - 1 < n_it:
                s2(items[k - 1])

    def emit_fox_layer(self, li, j):
        w_in = self.w[(li, "in")]
        self.emit_norm(li)
        self.emit_fox_gate(li)
        self.emit_qkv_generic(li, w_in, kcol=D, vcol=2 * D, qcol=0, q_evac_scale=1.0, qk_norm=(0, 1))
        self.emit_softmax_attention(True, 1.0 / math.sqrt(128.0))
        self.emit_outproj(self.w[(li, "out")])

    def emit_outproj(self, w_out):
        self.P.scope = f"outproj{self.wb_next}"
        for t in range(4):
            wi = self.load_w(self.w_tile16(w_out[t]))
            wv = self.WB[wi][:, :].rearrange("p (c n) -> p c n", n=512)
            for m in range(4):
                c = 4 * t + m
                def evac(b, tb, c=c):
                    xs = self.xT[:, c, tb * TB:(tb + 1) * TB]
                    self.tt(xs, xs, self.bank(b), ALU.add, reads=[self.t_x[c][tb], self.t_bank[b]], writes=[self.t_x[c][tb]])
                self.proj_fm(wi, wv, m * 128, evac)

    def emit_sb_layer(self, li, j):
        w_in = self.w[(li, "in")]
        self.emit_norm(li)
        self.emit_qkv_generic(li, w_in, kcol=D, vcol=2 * D, qcol=0, q_evac_scale=1.0 / math.sqrt(128.0))
        self.emit_sb_attention()
        self.emit_outproj(self.w[(li, "out")])


    def regD_subphase(self, keep, off):
        ret = dict(self.regD_retired)
        rest = []
        for t in self.t_regD:
            if any(t is k for k in keep):
                rest.append(t)
                continue
            assert t.pend == 0 and t.w != "PEND", t.name
            if t.w is not None:
                s, v = t.w
                if ret.get(s, 0) < v:
                    ret[s] = v
            for s, v in t.r.items():
                if ret.get(s, 0) < v:
                    ret[s] = v
        self.regD_retired = ret
        self.t_regD = rest
        self.regD_off = off

    def emit_sincos(self, ang, ang_t, out_ap, out_t, shift, tmp, tmp_t, nn, nn_t):
        MAGIC = 12582912.0
        TWO_PI_HI = 6.28125
        TWO_PI_LO = 2.0 * math.pi - 6.28125
        self.ts(tmp, ang, 1.0 / (2.0 * math.pi), shift / (2.0 * math.pi), ALU.mult, ALU.add, reads=[ang_t], writes=[tmp_t])
        self.ts(nn, tmp, MAGIC, None, ALU.add, None, reads=[tmp_t], writes=[nn_t])
        self.ts(nn, nn, -MAGIC, None, ALU.add, None, reads=[nn_t], writes=[nn_t])
        self.stt(tmp, nn, -TWO_PI_HI, ang, ALU.mult, ALU.add, reads=[nn_t, ang_t], writes=[tmp_t])
        self.stt(tmp, nn, -TWO_PI_LO, tmp, ALU.mult, ALU.add, reads=[nn_t, tmp_t], writes=[tmp_t])
        self.ts(tmp, tmp, float(shift), math.pi, ALU.add, ALU.min, reads=[tmp_t], writes=[tmp_t])
        self.ts(tmp, tmp, -math.pi, None, ALU.max, None, reads=[tmp_t], writes=[tmp_t])
        self.act(out_ap, tmp, AF.Sin, reads=[tmp_t], writes=[out_t])

    def emit_mla_layer(self, li, j):
        P = self.P
        w_in, w_uq, w_ukv = self.w[(li, "in")], self.w[(li, "uq")], self.w[(li, "ukv")]
        self.emit_norm(li)
        P.scope = "mla_proj"
        self.regD_phase()
        cos_ap, cos_t = self.regD_alloc(4096, F32, "cos", parts=64)
        sin_ap, sin_t = self.regD_alloc(4096, F32, "sinS", parts=64)
        krR_ap, krR_t = self.regD_alloc(4096, F32, "krR", parts=64)
        krsq_ap, krsq_t = self.regD_alloc(2048, BF16, "krsq", parts=64)
        keep = [cos_t, sin_t, krR_t, krsq_t]
        keep_off = self.regD_off
        posi_ap, posi_t = self.regD_alloc(4096, I32, "posi", parts=64)
        ang_ap, ang_t = self.regD_alloc(4096, F32, "ang", parts=64)
        tmp_ap, tmp_t = self.regD_alloc(4096, F32, "sctmp", parts=64)
        nn_ap, nn_t = self.regD_alloc(4096, F32, "scn", parts=64)
        ms = self.dsem("mla")
        self.dma(P.sp, posi_ap, self.pos_d.partition_broadcast(64), reads=[], writes=[posi_t], sem=ms)
        self.copy(P.dve, ang_ap, posi_ap, reads=[posi_t], writes=[ang_t])
        self.ts(ang_ap, ang_ap, self.cols[0:64, 18:19], None, ALU.mult, None, reads=[ang_t, self.t_const], writes=[ang_t])
        self.emit_sincos(ang_ap, ang_t, sin_ap, sin_t, 0.0, tmp_ap, tmp_t, nn_ap, nn_t)
        self.ts(sin_ap, sin_ap, self.cols[0:64, 19:20], None, ALU.mult, None, reads=[sin_t, self.t_const], writes=[sin_t])
        self.emit_sincos(ang_ap, ang_t, cos_ap, cos_t, math.pi / 2.0, tmp_ap, tmp_t, nn_ap, nn_t)
        if MLA_STOP == 1:
            return
        self.regD_subphase(keep, keep_off)
        st32_ap, st32_t0 = self.regD_alloc(20480, F32, "st32")
        st32 = st32_ap.rearrange("p (c t) -> p c t", t=TB)
        st32_t = [st32_t0] + [self.regD_tile(f"st32_{c}") for c in range(1, 10)]
        kr32_ap, kr32_t = self.regD_alloc(2048, F32, "kr32", parts=64)
        krs32_ap, krs32_t = self.regD_alloc(2048, F32, "krs32", parts=64)
        BQ, BKV = 6, 7
        for tb in range(2):
            tbs = slice(tb * TB, (tb + 1) * TB)
            for T in range(3):
                if T < 2:
                    wi = self.load_w(self.w_tile16(w_in[T]))
                    wv = self.WB[wi][:, :].rearrange("p (c n) -> p c n", n=512)
                    chunks = [(T * 4 + m, m * 128, 128) for m in range(4)]
                else:
                    v3 = lambda wb: wb[:, 0:16 * 384].rearrange("p (c n) -> p c n", n=384)
                    wi = self.load_w(self.w_tile16(w_in[2], L=6144))
                    wv = v3(self.WB[wi])
                    chunks = [(8, 0, 128), (9, 128, 128), ("kr", 256, 64), ("krs", 320, 64)]
                for ch, col0, ncols in chunks:
                    self.mla_down_chunk(wi, wv, ch, col0, ncols, tb, st32, st32_t, kr32_ap, kr32_t, krs32_ap, krs32_t, BQ, BKV)
            self.flush_deferred()
            for rs, bnk, n in ((0, BQ, 768), (1, BKV, 512)):
                self.act(self.rstd[:, rs, :], self.bank(bnk), AF.Ln, reads=[self.t_bank[bnk]], writes=[self.t_rstd[rs]],
                         bias=float(EPS), scale=1.0 / n)
                self.act(self.rstd[:, rs, :], self.rstd[:, rs, :], AF.Exp, reads=[self.t_rstd[rs]], writes=[self.t_rstd[rs]],
                         scale=-0.5)
            for ch in range(10):
                rs = 0 if ch < 6 else 1
                self.stt(self.hT[:, ch, tbs], st32[:, ch, :], self.cols[:, 8 + ch:9 + ch], self.rstd[:, rs, :], ALU.mult, ALU.mult,
                         reads=[st32_t[ch], self.t_rstd[rs], self.t_const], writes=[self.t_h[ch][tb]])
            self.tt(kr32_ap, kr32_ap, cos_ap[:, tbs], ALU.mult, reads=[kr32_t, cos_t], writes=[kr32_t])
            self.tt(krs32_ap, krs32_ap, sin_ap[:, tbs], ALU.mult, reads=[krs32_t, sin_t], writes=[krs32_t])
            self.tt(krR_ap[:, tbs], kr32_ap, krs32_ap, ALU.add, reads=[kr32_t, krs32_t], writes=[krR_t])
            self.act(krsq_ap[:, tbs], krR_ap[:, tbs], AF.Square, reads=[krR_t], writes=[krsq_t])
        if MLA_STOP == 2:
            return
        self.regD_subphase(keep, keep_off)
        kst_ap, kst_t = self.regD_alloc(8192, BF16, "kstage")
        ksr_ap, ksr_t = self.regD_alloc(8192, BF16, "kstage_r", parts=64)
        vst_ap, vst_t = self.regD_alloc(8192, BF16, "vstage")
        tA_ap, tA_t = self.regD_alloc(2048, F32, "ropeA", parts=64)
        tB_ap, tB_t = self.regD_alloc(2048, F32, "ropeB", parts=64)
        kst = kst_ap.rearrange("p (h t) -> p h t", t=TOK)
        ksr = ksr_ap.rearrange("p (h t) -> p h t", t=TOK)
        vst = vst_ap.rearrange("p (n f) -> p n f", f=512)
        kvsrc = self.hT[:, 6:10, :]
        kvsrc_t = self.t_h[6:10]
        ones64 = self.cbf[0:64, 1, :]
        ks_sem, vs_sem = self.dsem("mla_ks"), self.dsem("mla_vs")

        def headnorm_tail(b_n, rope_ap, rope_t, sq_r_ap, sq_r_t, gn_col, gr_col, out_n, out_r, out_t_n, out_t_r, s):
            b2 = self.next_mm_bank()
            self.mm(self.bank(b2), self.ones, self.sq[:, s, :], True, False, reads=[self.t_sq[s], self.t_const],
                    writes=[self.t_bank[b2]], signal=False)
            self.mm(self.bank(b2), ones64, sq_r_ap, False, True, reads=[sq_r_t, self.t_const],
                    writes=[self.t_bank[b2]], signal=True)
            rs = self.rs_rot % 2
            self.rs_rot += 1
            self.act(self.rstd[:, rs, :], self.bank(b2), AF.Ln, reads=[self.t_bank[b2]], writes=[self.t_rstd[rs]],
                     bias=float(EPS), scale=1.0 / 192.0)
            self.act(self.rstd[:, rs, :], self.rstd[:, rs, :], AF.Exp, reads=[self.t_rstd[rs]], writes=[self.t_rstd[rs]],
                     scale=-0.5)
            self.stt(out_n, self.bank(b_n), self.cols[:, gn_col:gn_col + 1], self.rstd[:, rs, :], ALU.mult, ALU.mult,
                     reads=[self.t_bank[b_n], self.t_rstd[rs], self.t_const], writes=[out_t_n])
            self.stt(out_r, rope_ap, self.cols[0:64, gr_col:gr_col + 1], self.rstd[0:64, rs, :], ALU.mult, ALU.mult,
                     reads=[rope_t, self.t_rstd[rs], self.t_const], writes=[out_t_r])

        for g in range(4):
            src = w_ukv[:, g * 1024:(g + 1) * 1024].rearrange("(c p) n -> p c n", p=128)
            wi = self.load_w([(lambda wb: wb[:, 0:4096].rearrange("p (c n) -> p c n", n=1024), src)])
            wv = self.WB[wi][:, 0:4096].rearrange("p (c n) -> p c n", n=1024)
            for m in range(4):
                def evac(b, tb, m=m):
                    tbs = slice(tb * TB, (tb + 1) * TB)
                    s = self.sq_rot % 4
                    self.sq_rot += 1
                    self.act(self.sq[:, s, :], self.bank(b), AF.Square, reads=[self.t_bank[b]], writes=[self.t_sq[s]])
                    return lambda: headnorm_tail(b, krR_ap[:, tbs], krR_t, krsq_ap[:, tbs], krsq_t, 4, 5,
                                                 kst[:, m, tbs], ksr[:, m, tbs], kst_t, ksr_t, s)
                self.proj_fm(wi, wv, m * 256, evac, nkc=4, src=kvsrc, src_t=kvsrc_t)
            self.flush_deferred()
            kd = self.kag_src_mla[g].ap().rearrange("(h x) t -> x h t", x=192)
            self.dma(P.sp, kd[0:128, :, :], kst, reads=[kst_t], writes=[self.t_kag_src[g]], sem=ks_sem)
            self.dma(P.sp, kd[128:192, :, :], ksr, reads=[ksr_t], writes=[self.t_kag_src[g]], sem=ks_sem)
            self.allgather(self.kag_src_mla[g], self.kag_dst_mla[g], self.t_kag_src[g], self.t_kag_dst[g], 2 * g)

            def evac_v(b, tt):
                self.copy(self.evac_engine(), vst[:, tt, :], self.bank(b), reads=[self.t_bank[b]], writes=[vst_t])
            self.proj_tm(wi, lambda kc, wv=wv: wv[:, kc, :].rearrange("p (h x) -> p h x", x=256)[:, :, 128:256], evac_v,
                         nkc=4, src=kvsrc, src_t=kvsrc_t, out3=True)
            vdst = self.vag_src[g].ap().rearrange("r (two f) -> (r two) f", two=2).rearrange("(n p) f -> p n f", p=128)
            self.dma(P.sp, vdst, vst, reads=[vst_t], writes=[self.t_vag_src[g]], sem=vs_sem)
            self.allgather(self.vag_src[g], self.vag_dst[g], self.t_vag_src[g], self.t_vag_dst[g], 2 * g + 1)

        self.mm_mod = 8
        for g in range(4):
            v4 = lambda wb: wb[:, 0:6144].rearrange("p (c n) -> p c n", n=1024)
            wi = self.load_w(self.w_tile16(w_uq[g], L=6144))
            wv = v4(self.WB[wi])
            pending = None
            for m in range(4):
                for tb in range(2):
                    tbs = slice(tb * TB, (tb + 1) * TB)
                    banks = []
                    for col0, ncols in ((m * 256, 128), (m * 256 + 128, 64), (m * 256 + 192, 64)):
                        b = self.next_mm_bank()
                        banks.append(b)
                        for kc in range(6):
                            self.mm(self.psum[0:ncols, b, :], wv[:, kc, col0:col0 + ncols], self.hT[:, kc, tbs],
                                    kc == 0, kc == 5, reads=[self.t_WB[wi], self.t_h[kc][tb]],
                                    writes=[self.t_bank[b]], signal=(kc == 5))
                    if pending is not None:
                        pending()
                    bn, br, bs = banks
                    self.tt(tA_ap, self.psum[0:64, br, :], cos_ap[:, tbs], ALU.mult, reads=[self.t_bank[br], cos_t], writes=[tA_t])
                    self.tt(tB_ap, self.psum[0:64, bs, :], sin_ap[:, tbs], ALU.mult, reads=[self.t_bank[bs], sin_t], writes=[tB_t])
                    self.tt(tA_ap, tA_ap, tB_ap, ALU.add, reads=[tA_t, tB_t], writes=[tA_t])
                    s = self.sq_rot % 4
                    self.sq_rot += 1
                    self.act(self.sq[:, s, :], self.bank(bn), AF.Square, reads=[self.t_bank[bn]], writes=[self.t_sq[s]])
                    s2 = self.sq_rot % 4
                    self.sq_rot += 1
                    self.act(self.sq[0:64, s2, :], tA_ap, AF.Square, reads=[tA_t], writes=[self.t_sq[s2]])
                    pending = (lambda bn=bn, s=s, s2=s2, m=m, tbs=tbs:
                               headnorm_tail(bn, tA_ap, tA_t, self.sq[0:64, s2, :], self.t_sq[s2], 2, 3,
                                             kst[:, m, tbs], ksr[:, m, tbs], kst_t, ksr_t, s))
            pending()
            qd = self.q_scr_mla.ap()[g * 768:(g + 1) * 768, :].rearrange("(h x) t -> x h t", x=192)
            self.dma(P.sp, qd[0:128, :, :], kst, reads=[kst_t], writes=[self.t_q_scr[4 * g + m] for m in range(4)], sem=ks_sem)
            self.dma(P.sp, qd[128:192, :, :], ksr, reads=[ksr_t], writes=[self.t_q_scr[4 * g + m] for m in range(4)], sem=ks_sem)
        self.mm_mod = 4
        if MLA_STOP == 3:
            return
        self.emit_softmax_attention(False, 1.0 / math.sqrt(192.0), mla=True)
        if MLA_STOP == 4:
            return
        self.emit_outproj(self.w[(li, "out")])

    def mla_down_chunk(self, wi, wv, ch, col0, ncols, tb, st32, st32_t, kr32_ap, kr32_t, krs32_ap, krs32_t, BQ, BKV):
        P = self.P
        b = self.next_mm_bank()
        tbs = slice(tb * TB, (tb + 1) * TB)
        for kc in range(NCH):
            self.mm(self.psum[0:ncols, b, :], wv[:, kc, col0:col0 + ncols], self.hT[:, kc, tbs],
                    kc == 0, kc == NCH - 1, reads=[self.t_WB[wi], self.t_h[kc][tb]],
                    writes=[self.t_bank[b]], signal=(kc == NCH - 1))
        if self.deferred is not None:
            d, self.deferred = self.deferred, None
            d()
        if ch == "kr":
            self.copy(P.dve, kr32_ap, self.psum[0:64, b, :], reads=[self.t_bank[b]], writes=[kr32_t])
            return
        if ch == "krs":
            self.copy(P.dve, krs32_ap, self.psum[0:64, b, :], reads=[self.t_bank[b]], writes=[krs32_t])
            return
        s = self.sq_rot % 4
        self.sq_rot += 1
        self.act(self.sq[:, s, :], self.bank(b), AF.Square, reads=[self.t_bank[b]], writes=[self.t_sq[s]])
        self.copy(P.dve, st32[:, ch, :], self.bank(b), reads=[self.t_bank[b], self.t_sq[s]], writes=[st32_t[ch]])
        acc = BQ if ch < 6 else BKV
        first = ch in (0, 6)
        lastc = ch in (5, 9)

        def cont():
            self.mm(self.bank(acc), self.ones, self.sq[:, s, :], first, lastc, reads=[self.t_sq[s], self.t_const],
                    writes=[self.t_bank[acc]], signal=True)
        self.deferred = cont

    def emit_mlp(self, li):
        P = self.P
        self.emit_norm(4 + li)
        P.scope = f"mlp{li}"
        self.regD_phase()
        aT = [self.regD_alloc(8192, BF16, f"aT{i}") for i in range(2)]
        sqt = [self.regD_alloc(2048, F32, f"sqt{i}") for i in range(2)]
        w1 = self.w[(li, "w1")]
        w2 = self.w[(li, "w2")]
        NG = DFF // 512
        rot = [0]

        def ffn1(g):
            wi = self.load_w(self.w_tile16(w1[g]))
            wv = self.WB[wi][:, :].rearrange("p (c n) -> p c n", n=512)
            a_ap, a_t = aT[g % 2]
            av = a_ap.rearrange("p (m t) -> p m t", t=TOK)
            for m in range(4):
                def evac(b, tb, m=m):
                    sa, st = sqt[rot[0] % 2]
                    rot[0] += 1
                    self.act(sa, self.bank(b), AF.Square, reads=[self.t_bank[b]], writes=[st])
                    self.stt(av[:, m, tb * TB:(tb + 1) * TB], self.bank(b), 0.0, sa, ALU.is_gt, ALU.mult,
                             reads=[self.t_bank[b], st], writes=[a_t])
                self.proj_fm(wi, wv, m * 128, evac)

        def ffn2(g):
            src = w2[g * 512:(g + 1) * 512, :].rearrange("(m p) n -> p m n", p=128)
            wi = self.load_w([(lambda wb: wb[:, :].rearrange("p (m n) -> p m n", n=D), src)])
            wv = self.WB[wi][:, :].rearrange("p (m n) -> p m n", n=D)
            a_ap, a_t = aT[g % 2]
            av = a_ap.rearrange("p (m t) -> p m t", t=TOK)
            for c in range(NCH):
                for tb in range(2):
                    b = 4 + self.next_mm_bank(0, 4)
                    for m in range(4):
                        self.mm(self.bank(b), wv[:, m, c * 128:(c + 1) * 128], av[:, m, tb * TB:(tb + 1) * TB],
                                m == 0, m == 3, reads=[self.t_WB[wi], a_t], writes=[self.t_bank[b]], signal=(m == 3))
                    xs = self.xT[:, c, tb * TB:(tb + 1) * TB]
                    self.tt(xs, xs, self.bank(b), ALU.add, reads=[self.t_x[c][tb], self.t_bank[b]], writes=[self.t_x[c][tb]])

        ffn1(0)
        for g in range(NG):
            if g + 1 < NG:
                ffn1(g + 1)
            ffn2(g)

    def emit_output(self):
        P = self.P
        P.scope = "output"
        self.regD_phase()
        osem = self.dsem("out")
        st = [self.regD_alloc(8192, F32, f"ostage{i}") for i in range(2)]
        for tt in range(8):
            o_ap, o_t = st[tt % 2]
            for c4 in range(4):
                b = self.next_mm_bank()
                for k in range(4):
                    c = c4 * 4 + k
                    def fn(e, o=self.psum[:, b, k * 128:(k + 1) * 128], i=self.xT[:, c, tt * 128:(tt + 1) * 128]):
                        return e.transpose(o, i, self.id32)
                    P.emit(P.pe, fn, reads=[self.t_x[c][tt // 4], self.t_const], writes=[self.t_bank[b]], signal=(k == 3))
                self.copy(self.evac_engine(), o_ap[:, c4 * 512:(c4 + 1) * 512], self.bank(b),
                          reads=[self.t_bank[b]], writes=[o_t])
            self.dma(P.sp, self.out[tt * 128:(tt + 1) * 128, :], o_ap, reads=[o_t], writes=[], sem=osem)
        fin = [(osem, osem.n)]
        if DEBUG and "dbg" in self.misc_sems:
            fin.append((self.misc_sems["dbg"], self.misc_sems["dbg"].n))
        return fin


def _bf16(a):
    return np.asarray(a, dtype=np.float32).astype(ml_dtypes.bfloat16)


def _negmask(rank):
    types = [["diag", "zero", "full", "diag"], ["full", "diag", "diag", "zero"]][rank]
    out = np.zeros((128, 4, MASKW), np.float32)
    k = np.arange(128)[:, None]
    for u, ty in enumerate(types):
        if ty == "zero":
            out[:, u, :] = NEG
        elif ty == "diag":
            out[:, u, 0] = NEG
            s = np.arange(MASKW - 1)[None, :]
            m = np.where(s < 384, NEG, np.where(s < 512, np.where(k > (s - 384), NEG, 0.0), 0.0))
            out[:, u, 1:] = m
    return _bf16(out.reshape(128, 4 * MASKW))


def _consts_bf():
    j = np.arange(128)[:, None]
    s = np.arange(128)[None, :]
    ident = (j == s).astype(np.float32)
    ones = np.ones((128, 128), np.float32)
    negutri = np.where(j >= s, -1.0, 0.0).astype(np.float32)
    negones = -ones
    return _bf16(np.concatenate([ident, ones, negutri, negones], axis=1))


def _col_layout(v):
    v = np.asarray(v, np.float32)
    return np.ascontiguousarray(v.reshape(-1, 128).T)


_BUILD_CACHE = {}


def _get_nc(layers):
    key = tuple(layers)
    if key not in _BUILD_CACHE:
        _BUILD_CACHE[key] = Builder(list(layers))
    return _BUILD_CACHE[key].nc


def _own_rows(rank):
    blocks = [0, 3] if rank == 0 else [1, 2]
    return np.concatenate([np.arange(b * TB, (b + 1) * TB) for b in blocks])


def _selw(rank):
    mine = [0, 3] if rank == 0 else [1, 2]
    out = np.zeros((128, 8, 128), np.float32)
    for jb in range(4):
        for l in range(2):
            if AG_GLOBAL[jb] < mine[l]:
                out[:, jb * 2 + l, :] = 1.0
    return out.reshape(128, 8 * 128)


def _tile_w(W, ncols=512):
    W = np.asarray(W, np.float32)
    K, N = W.shape
    return np.ascontiguousarray(W.reshape(K // 128, 128, N // ncols, ncols).transpose(2, 1, 0, 3).reshape(N // ncols, 128, -1))


def run_layers(layers, x, inputs):
    nc = _get_nc(layers)
    gains = np.zeros((128, 128), np.float32)
    for n in range(4):
        gains[:, n * 16:(n + 1) * 16] = _col_layout(inputs["mix_norm"][n])
        gains[:, (4 + n) * 16:(5 + n) * 16] = _col_layout(inputs["mlp_norm"][n])
    cols = np.zeros((128, 64), np.float32)
    cols[:, 0] = inputs["fox_q_gain"][0]
    cols[:, 1] = inputs["fox_k_gain"][0]
    cols[:, 2] = inputs["mla_q_gain"][0][:128]
    cols[:64, 3] = inputs["mla_q_gain"][0][128:]
    cols[:, 4] = inputs["mla_k_gain"][0][:128]
    cols[:64, 5] = inputs["mla_k_gain"][0][128:]
    cols[:, 8:14] = _col_layout(inputs["mla_q_norm"][0])
    cols[:, 14:18] = _col_layout(inputs["mla_kv_norm"][0])
    half = 32
    invf = (10000.0 ** (-np.arange(0, half, dtype=np.float32) * 2.0 / 64.0)).astype(np.float32)
    cols[:64, 18] = np.concatenate([invf, invf])
    cols[:64, 19] = np.concatenate([-np.ones(half, np.float32), np.ones(half, np.float32)])
    cols[:, 20] = -math.pi
    bfb = np.ascontiguousarray(np.broadcast_to(np.tile(np.asarray(inputs["fox_b_f"][0], np.float32), 8)[None, :], (128, 128)))
    j = np.arange(128)[:, None]
    f = np.arange(128)[None, :]
    c32 = np.concatenate([np.eye(128, dtype=np.float32), np.ones((128, 128), np.float32),
                          (j <= f).astype(np.float32)], axis=1)
    cbf = _consts_bf()
    shared = {}
    for li in layers:
        kind, jj = li % 3, li // 3
        if kind == 0:
            shared[f"w{li}_in"] = _tile_w(inputs["sb_w_in"][jj])
            shared[f"w{li}_out"] = _tile_w(inputs["sb_w_out"][jj])
        elif kind == 1:
            wf = np.asarray(inputs["fox_w_in"][jj], np.float32)
            shared[f"w{li}_in"] = _tile_w(wf[:, :3 * D])
            shared[f"w{li}_gate"] = np.ascontiguousarray(wf[:, 3 * D:].reshape(16, 128, NH).transpose(1, 0, 2).reshape(128, 16 * NH))
            shared[f"w{li}_out"] = _tile_w(inputs["fox_w_out"][jj])
        else:
            wm = np.asarray(inputs["mla_w_in"][jj], np.float32)
            t01 = _tile_w(wm[:, :1024])
            t2 = _tile_w(np.concatenate([wm[:, 1024:1344], wm[:, 1312:1344], wm[:, 1280:1312]], axis=1), ncols=384)[0]
            t2 = np.concatenate([t2, np.zeros((128, 8192 - t2.shape[1]), np.float32)], axis=1)
            shared[f"w{li}_in"] = np.ascontiguousarray(np.concatenate([t01, t2[None]], axis=0))
            wq = np.asarray(inputs["mla_w_uq"][jj], np.float32).reshape(768, NH, 192)
            wq = np.concatenate([wq, wq[:, :, 160:192], wq[:, :, 128:160]], axis=2)
            shared[f"w{li}_uq"] = _tile_w(wq.reshape(768, NH * 256), ncols=1024)
            shared[f"w{li}_ukv"] = np.ascontiguousarray(inputs["mla_w_ukv"][jj])
            shared[f"w{li}_out"] = _tile_w(inputs["mla_w_out"][jj])
        if not SKIP_MLP:
            shared[f"w{li}_w1"] = _tile_w(inputs["mlp_w1"][li])
            shared[f"w{li}_w2"] = np.ascontiguousarray(inputs["mlp_w2"][li])
    pos = np.asarray(inputs["positions"])
    in_maps = []
    for c in range(8):
        b, r = c // 2, c % 2
        m = {
            "x_in": np.ascontiguousarray(x[b][_own_rows(r)]),
            "negm": _negmask(r),
            "cbf": cbf,
            "c32": c32,
            "bfb": bfb,
            "selw": _selw(r),
            "gains": gains,
            "cols": cols,
        }
        if any(l % 3 == 2 for l in layers):
            m["pos"] = np.ascontiguousarray(pos[b][_own_rows(r)][None, :].astype(np.int32))
        m.update(shared)
        in_maps.append(m)
    res = run_bass_kernel_spmd(nc, in_maps, core_ids=list(range(8)))
    out = np.empty((4, 2048, D), np.float32)
    for c in range(8):
        b, r = c // 2, c % 2
        out[b][_own_rows(r)] = np.asarray(res.results[c]["out"])
    if DEBUG:
        global LAST_DBG
        LAST_DBG = [np.asarray(res.results[c]["dbg"]) for c in range(8)]
    return out


def kernel(**inputs):
    inputs = {k: np.asarray(v) for k, v in inputs.items()}
    x = np.asarray(inputs["x"], np.float32)
    return run_layers([0, 1, 2, 3], x, inputs)
```
